# Optimizing a Trainium2 kernel written in Bass

```python
import jax, jax.numpy as jnp
from jax import lax
import numpy as np

D_MODEL = 1024
BATCH = 8
SEQ = 4096
DEPTH = 2
DEC_BATCH = 16
DEC_SEQ = 32
PAST_LEN = 1024

CHUNK = 64
EXPAND = 2
MIX_WIDTH = EXPAND * D_MODEL
N_A_LAYERS = DEPTH // 2
N_B_LAYERS = DEPTH - N_A_LAYERS
RWKV_HEAD = 64
RWKV_HEADS = MIX_WIDTH // RWKV_HEAD
DECAY_RANK = 64
ICLR_RANK = 64
SB_HEAD = 128
SB_HEADS = MIX_WIDTH // SB_HEAD
BLOCK_Q = 128
RMS_EPS = 1e-6
GN_EPS = 64e-5
L2_EPS = 1e-12

kernel_name = "yoco_rwkv7_stickbreaking_stream"


def rms_norm(x, g):
    xf = x.astype(jnp.float32)
    y = xf * lax.rsqrt(jnp.mean(xf * xf, axis=-1, keepdims=True) + RMS_EPS)
    return y.astype(x.dtype) * g


def ada_modulate(x, c, g, w_ada, b_ada):
    shift, scale, gate = jnp.split(c @ w_ada + b_ada, 3, axis=-1)
    h = rms_norm(x, g) * (1 + scale[:, None]) + shift[:, None]
    return h, gate[:, None]


def l2_normalize(x):
    xf = x.astype(jnp.float32)
    return (xf * lax.rsqrt(jnp.sum(xf * xf, axis=-1, keepdims=True) + L2_EPS)).astype(x.dtype)


def rwkv7_recurrence(r, decay, k, v, kk, a, s0):
    def step(s, inp):
        r_t, w_t, k_t, v_t, kk_t, a_t = inp
        s_kk = jnp.einsum('bhvk,bhk->bhv', s, kk_t)
        s = (s * w_t[:, :, None, :]
             - s_kk[..., None] * (kk_t * a_t)[:, :, None, :]
             + v_t[..., None] * k_t[:, :, None, :])
        return s, jnp.einsum('bhvk,bhk->bhv', s, r_t)
    xs = tuple(jnp.swapaxes(t.astype(jnp.float32), 0, 1) for t in (r, decay, k, v, kk, a))
    s_fin, ys = lax.scan(step, s0.astype(jnp.float32), xs)
    return jnp.swapaxes(ys, 0, 1).astype(v.dtype), s_fin.astype(s0.dtype)


def rwkv7_layer(x, c, shift_prev, s0, norm_g, ada_w, ada_b, w_in, mu_in, mu_w, mu_a,
                w0, w1, w2, a0, a1, a2, k_k, k_a, r_k, ln_g, ln_b, w_out):
    B, T, _ = x.shape
    h, gate = ada_modulate(x, c, norm_g, ada_w, ada_b)
    h_prev = jnp.concatenate([shift_prev[:, None].astype(h.dtype), h[:, :-1]], axis=1)
    dx = h_prev - h
    w_mu = (w_in.reshape(D_MODEL, 4, MIX_WIDTH) * mu_in.T[:, :, None]).reshape(D_MODEL, 4 * MIX_WIDTH)
    proj = jnp.concatenate([h, dx], axis=-1) @ jnp.concatenate([w_in, w_mu], axis=0)
    r, k, v, z = jnp.split(proj, 4, axis=-1)
    xw = h + dx * mu_w
    xa = h + dx * mu_a
    w_log = -jax.nn.softplus(-(w0 + jnp.tanh(xw @ w1) @ w2)) - 0.5
    decay = jnp.exp(-jnp.exp(w_log.astype(jnp.float32)))
    a = jax.nn.sigmoid(a0 + (xa @ a1) @ a2)
    heads = lambda t: t.reshape(B, T, RWKV_HEADS, RWKV_HEAD)
    r, k, v, a, decay = heads(r), heads(k), heads(v), heads(a), heads(decay)
    kk = l2_normalize(k * k_k.reshape(RWKV_HEADS, RWKV_HEAD))
    k = k * (1 + (a - 1) * k_a.reshape(RWKV_HEADS, RWKV_HEAD))
    y, s_new = rwkv7_recurrence(r, decay, k, v, kk, a, s0)
    yf = y.astype(jnp.float32)
    mean = jnp.mean(yf, axis=-1, keepdims=True)
    var = jnp.mean(jnp.square(yf - mean), axis=-1, keepdims=True)
    y = ((yf - mean) * lax.rsqrt(var + GN_EPS)).astype(y.dtype)
    y = y * ln_g.reshape(RWKV_HEADS, RWKV_HEAD) + ln_b.reshape(RWKV_HEADS, RWKV_HEAD)
    y = y + jnp.sum(r * k * r_k, axis=-1, keepdims=True) * v
    y = y.reshape(B, T, MIX_WIDTH) * jax.nn.silu(z)
    return x + gate * (y @ w_out), h[:, -1], s_new


def head_rms(x, g):
    xf = x.astype(jnp.float32)
    y = xf * lax.rsqrt(jnp.mean(xf * xf, axis=-1, keepdims=True) + RMS_EPS)
    return y.astype(x.dtype) * g


def shared_kv(x_mid, kv_norm_g, kv_w, k_gain):
    B, T, _ = x_mid.shape
    k, v = jnp.split(rms_norm(x_mid, kv_norm_g) @ kv_w, 2, axis=-1)
    k = head_rms(k.reshape(B, T, SB_HEADS, SB_HEAD), k_gain)
    return k, v.reshape(B, T, SB_HEADS, SB_HEAD)


def sb_block(q, k, v, q_start):
    Tq, Tk = q.shape[1], k.shape[1]
    z = jnp.einsum('bqhd,bkhd->bhqk', q, k).astype(jnp.float32) * (SB_HEAD ** -0.5)
    q_pos = q_start + jnp.arange(Tq)
    k_pos = jnp.arange(Tk)
    mask = k_pos[None, :] < q_pos[:, None]
    log_stay = jnp.where(mask, jax.nn.log_sigmoid(-z), 0.0)
    tail = lax.cumsum(log_stay, axis=3, reverse=True) - log_stay
    weights = jnp.where(mask, jnp.exp(jax.nn.log_sigmoid(z) + tail), 0.0)
    return jnp.einsum('bhqk,bkhd->bqhd', weights.astype(v.dtype), v)


def sb_sweep(q, k, v, offset):
    T = q.shape[1]
    outs = []
    for s in range(0, T, BLOCK_Q):
        e = min(s + BLOCK_Q, T)
        outs.append(sb_block(q[:, s:e], k[:, :offset + e], v[:, :offset + e], offset + s))
    return jnp.concatenate(outs, axis=1)


def sb_layer(x, c, k_all, v_all, offset, norm_g, ada_w, ada_b, w_in, q_gain, w_out):
    B, T, _ = x.shape
    h, gate = ada_modulate(x, c, norm_g, ada_w, ada_b)
    q, z = jnp.split(h @ w_in, 2, axis=-1)
    q = head_rms(q.reshape(B, T, SB_HEADS, SB_HEAD), q_gain)
    o = sb_sweep(q, k_all, v_all, offset).reshape(B, T, MIX_WIDTH)
    return x + gate * ((o * jax.nn.silu(z)) @ w_out)


def trunk(x, c, past_k, past_v, shift0, wkv0, p):
    new_shift, new_wkv = [], []
    k_new = v_new = k_all = v_all = None
    offset = 0 if past_k is None else past_k.shape[1]
    for layer in range(DEPTH):
        if layer < N_A_LAYERS:
            i = layer
            x, sh, s = rwkv7_layer(
                x, c, shift0[i], wkv0[i], p['a_norm_g'][i], p['a_ada_w'][i], p['a_ada_b'][i],
                p['a_w_in'][i], p['a_mu_in'][i], p['a_mu_w'][i], p['a_mu_a'][i],
                p['a_w0'][i], p['a_w1'][i], p['a_w2'][i], p['a_a0'][i], p['a_a1'][i], p['a_a2'][i],
                p['a_k_k'][i], p['a_k_a'][i], p['a_r_k'][i], p['a_ln_g'][i], p['a_ln_b'][i],
                p['a_w_out'][i])
            new_shift.append(sh)
            new_wkv.append(s)
            if layer == N_A_LAYERS - 1:
                k_new, v_new = shared_kv(x, p['kv_norm_g'], p['kv_w'], p['k_gain'])
                if past_k is None:
                    k_all, v_all = k_new, v_new
                else:
                    k_all = jnp.concatenate([past_k.astype(k_new.dtype), k_new], axis=1)
                    v_all = jnp.concatenate([past_v.astype(v_new.dtype), v_new], axis=1)
        else:
            j = layer - N_A_LAYERS
            x = sb_layer(x, c, k_all, v_all, offset, p['b_norm_g'][j], p['b_ada_w'][j],
                         p['b_ada_b'][j], p['b_w_in'][j], p['b_q_gain'][j], p['b_w_out'][j])
    return x, k_new, v_new, jnp.stack(new_wkv), jnp.stack(new_shift)


def setup_inputs(seed: int = 0) -> dict:
    key = jax.random.key(seed)
    ks = iter(jax.random.split(key, 48))
    f32 = jnp.float32
    nrm = lambda shape, s: jax.random.normal(next(ks), shape, f32) * s
    uni = lambda shape: jax.random.uniform(next(ks), shape, f32)
    D, E, NA, NB = D_MODEL, MIX_WIDTH, N_A_LAYERS, N_B_LAYERS
    return {
        'x_prompt': nrm((BATCH, SEQ, D), 1.0),
        'x_sample': nrm((DEC_BATCH, DEC_SEQ, D), 1.0),
        'cache_k': nrm((DEC_BATCH, PAST_LEN, SB_HEADS, SB_HEAD), 1.0),
        'cache_v': nrm((DEC_BATCH, PAST_LEN, SB_HEADS, SB_HEAD), 1.0),
        'state_wkv': nrm((NA, DEC_BATCH, RWKV_HEADS, RWKV_HEAD, RWKV_HEAD), RWKV_HEAD ** -0.5),
        'state_shift': nrm((NA, DEC_BATCH, D), 1.0),
        'c_prompt': nrm((BATCH, D), 1.0),
        'c_sample': nrm((DEC_BATCH, D), 1.0),
        'a_norm_g': 1.0 + nrm((NA, D), 0.05),
        'a_ada_w': nrm((NA, D, 3 * D), 0.5 * D ** -0.5),
        'a_ada_b': nrm((NA, 3 * D), 0.01),
        'a_w_in': nrm((NA, D, 4 * E), D ** -0.5),
        'a_mu_in': uni((NA, 4, D)),
        'a_mu_w': uni((NA, D)),
        'a_mu_a': uni((NA, D)),
        'a_w0': nrm((NA, E), 0.5),
        'a_w1': nrm((NA, D, DECAY_RANK), D ** -0.5),
        'a_w2': nrm((NA, DECAY_RANK, E), 0.5 * DECAY_RANK ** -0.5),
        'a_a0': nrm((NA, E), 0.5),
        'a_a1': nrm((NA, D, ICLR_RANK), D ** -0.5),
        'a_a2': nrm((NA, ICLR_RANK, E), 0.5 * ICLR_RANK ** -0.5),
        'a_k_k': 0.85 + nrm((NA, E), 0.05),
        'a_k_a': 1.0 + nrm((NA, E), 0.05),
        'a_r_k': nrm((NA, RWKV_HEADS, RWKV_HEAD), 0.1),
        'a_ln_g': 1.0 + nrm((NA, E), 0.05),
        'a_ln_b': nrm((NA, E), 0.01),
        'a_w_out': nrm((NA, E, D), E ** -0.5),
        'kv_norm_g': 1.0 + nrm((D,), 0.05),
        'kv_w': nrm((D, 2 * E), D ** -0.5),
        'k_gain': 1.0 + nrm((SB_HEAD,), 0.05),
        'b_norm_g': 1.0 + nrm((NB, D), 0.05),
        'b_ada_w': nrm((NB, D, 3 * D), 0.5 * D ** -0.5),
        'b_ada_b': nrm((NB, 3 * D), 0.01),
        'b_w_in': nrm((NB, D, 2 * E), D ** -0.5),
        'b_q_gain': 1.0 + nrm((NB, SB_HEAD), 0.05),
        'b_w_out': nrm((NB, E, D), E ** -0.5),
    }


def reference(x_prompt, x_sample, cache_k, cache_v, state_wkv, state_shift, c_prompt, c_sample,
              a_norm_g, a_ada_w, a_ada_b, a_w_in, a_mu_in, a_mu_w, a_mu_a, a_w0, a_w1, a_w2,
              a_a0, a_a1, a_a2, a_k_k, a_k_a, a_r_k, a_ln_g, a_ln_b, a_w_out,
              kv_norm_g, kv_w, k_gain,
              b_norm_g, b_ada_w, b_ada_b, b_w_in, b_q_gain, b_w_out):
    p = dict(a_norm_g=a_norm_g, a_ada_w=a_ada_w, a_ada_b=a_ada_b, a_w_in=a_w_in, a_mu_in=a_mu_in,
             a_mu_w=a_mu_w, a_mu_a=a_mu_a, a_w0=a_w0, a_w1=a_w1, a_w2=a_w2, a_a0=a_a0,
             a_a1=a_a1, a_a2=a_a2, a_k_k=a_k_k, a_k_a=a_k_a, a_r_k=a_r_k, a_ln_g=a_ln_g,
             a_ln_b=a_ln_b, a_w_out=a_w_out, kv_norm_g=kv_norm_g, kv_w=kv_w, k_gain=k_gain,
             b_norm_g=b_norm_g, b_ada_w=b_ada_w, b_ada_b=b_ada_b, b_w_in=b_w_in,
             b_q_gain=b_q_gain, b_w_out=b_w_out)
    bp = x_prompt.shape[0]
    shift0_p = jnp.zeros((N_A_LAYERS, bp, D_MODEL), x_prompt.dtype)
    wkv0_p = jnp.zeros((N_A_LAYERS, bp, RWKV_HEADS, RWKV_HEAD, RWKV_HEAD), x_prompt.dtype)
    y_prompt, k_prompt, v_prompt, wkv_prompt, shift_prompt = trunk(
        x_prompt, c_prompt, None, None, shift0_p, wkv0_p, p)
    y_sample, k_sample, v_sample, wkv_sample, shift_sample = trunk(
        x_sample, c_sample, cache_k, cache_v, state_shift, state_wkv, p)
    return (y_prompt, y_sample, k_prompt, v_prompt, wkv_prompt, shift_prompt,
            k_sample, v_sample, wkv_sample, shift_sample)
```

```python
import numpy as np
from contextlib import ExitStack
import concourse.bass as bass
import concourse.mybir as mybir
from concourse.bass_utils import run_bass_kernel_spmd

F32 = mybir.dt.float32
BF16 = mybir.dt.bfloat16
ALU = mybir.AluOpType
AF = mybir.ActivationFunctionType
AX = mybir.AxisListType

PE, ACT, DVE, POOL, SP = 0, 1, 2, 3, 4
ENG_NAMES = ["tensor", "scalar", "vector", "gpsimd", "sync"]
EPOCH = 20000
NDMA = 24

D = 1024
E = 2048
SEQ = 4096
DSEQ = 32
PAST = 1024
NPAIR = 16
NHEAD = 16
KC = 8
DECAY_C = 0.6065306597126334


class Buf:
    __slots__ = ("name", "w", "r", "excl")

    def __init__(self, name="", excl=False):
        self.name = name
        self.w = {}
        self.r = {}
        self.excl = excl


class V:
    __slots__ = ("ap", "bufs")

    def __init__(self, ap, bufs):
        self.ap = ap
        self.bufs = bufs

    def __getitem__(self, k):
        return V(self.ap[k], self.bufs)


class Prog:
    def __init__(self):
        self.ops = [[] for _ in range(5)]
        self.cnt = [0] * 5
        self.seen = [dict() for _ in range(5)]
        self.dma_i = 0
        self.dma_cnt = [0] * NDMA
        self.total = 0
        self.limit = None
        self.log = []

    def _need(self, eng, key, ticket, waits):
        if key == PE and eng == PE:
            return
        s = self.seen[eng]
        if s.get(key, 0) >= ticket:
            return
        s[key] = ticket
        waits.append((key, ticket))

    def _deps(self, eng, reads, writes, waits):
        for b in reads:
            for k, t in b.w.items():
                self._need(eng, k, t, waits)
            if b.excl:
                for k, t in b.r.items():
                    if k != eng:
                        self._need(eng, k, t, waits)
        for b in writes:
            for k, t in b.w.items():
                self._need(eng, k, t, waits)
            for k, t in b.r.items():
                self._need(eng, k, t, waits)

    def op(self, eng, fn, reads=(), writes=()):
        self.total += 1
        if self.limit is not None and self.total > self.limit:
            return
        waits = []
        self._deps(eng, reads, writes, waits)
        self.cnt[eng] += 1
        t = self.cnt[eng]
        self.ops[eng].append([waits, fn, t, False])
        for b in reads:
            b.r[eng] = t
        for b in writes:
            b.w = {eng: t}
            b.r = {}

    def dma(self, fn, reads=(), writes=(), eng=SP):
        self.total += 1
        if self.limit is not None and self.total > self.limit:
            return
        slot = self.dma_i % NDMA
        self.dma_i += 1
        key = ("d", slot)
        waits = []
        if self.dma_cnt[slot] > 0:
            self._need(eng, key, self.dma_cnt[slot], waits)
        self._deps(eng, reads, writes, waits)
        self.dma_cnt[slot] += 1
        t = self.dma_cnt[slot]
        self.ops[eng].append([waits, fn, ("d", slot, t), True])
        for b in reads:
            b.r[key] = t
        for b in writes:
            b.w = {key: t}
            b.r = {}

    def final_all(self, eng=SP):
        waits = []
        for slot in range(NDMA):
            if self.dma_cnt[slot] > 0:
                waits.append((("d", slot), self.dma_cnt[slot]))
        for k in range(4):
            if self.cnt[k] > 0:
                waits.append((k, self.cnt[k]))
        self.ops[eng].append([waits, None, None, False])

    def emit(self, block, esems, dsems):
        prog = self

        def semval(key, t):
            if isinstance(key, tuple):
                return dsems[key[1]], 16 * t
            ep = (t - 1) // EPOCH
            return esems[key][ep], t - ep * EPOCH

        def run(eng_idx):
            def body(e):
                for waits, fn, sig, isdma in prog.ops[eng_idx]:
                    for key, t in waits:
                        s, v = semval(key, t)
                        e.wait_ge(s, v)
                    if fn is None:
                        continue
                    ins = fn(e)
                    if isdma:
                        ins.then_inc(dsems[sig[1]], 16)
                    else:
                        ep = (sig - 1) // EPOCH
                        ins.then_inc(esems[eng_idx][ep], 1)
            return body

        block.tensor(run(PE))
        block.scalar(run(ACT))
        block.vector(run(DVE))
        block.gpsimd(run(POOL))
        block.sync(run(SP))


PV_E = ["w0", "a0", "k_k", "k_a", "r_k", "ln_g", "ln_b"]
PV_D = ["a_norm_g", "mu_r", "mu_k", "mu_v", "mu_z", "mu_w", "mu_a", "kv_norm_g", "b_norm_g"]
PVO = {}
_o = 0
for _n in PV_E:
    PVO[_n] = _o
    _o += 16
for _n in PV_D:
    PVO[_n] = _o
    _o += 8
PVO["ada_b_a"] = _o; _o += 24
PVO["ada_b_b"] = _o; _o += 24
PVO["k_gain"] = _o; _o += 1
PVO["q_gain"] = _o; _o += 1
PVO["eps_rms"] = _o; _o += 1
PVO["eps_gn"] = _o; _o += 1
PVO["eps_l2"] = _o; _o += 1
PVO["one"] = _o; _o += 1
PVO["hm0"] = _o; _o += 1
PVO["hm1"] = _o; _o += 1
PVO["nw0"] = _o; _o += 16
PVO["na0"] = _o; _o += 16
PVO["omka"] = _o; _o += 16
NPV = _o

CO = {"ident": 0, "ML": 128, "MUcat": 256, "MKcat": 512, "TriNeg": 768, "BO64": 896, "BO1": 1024, "ones": 1152}
NCST = 1280


def _fm(v, ncol):
    return np.ascontiguousarray(np.asarray(v, np.float32).reshape(ncol, 128).T)


def make_consts():
    c = np.zeros((128, NCST), np.float32)
    i = np.arange(128)
    c[:, 0:128] = np.eye(128, dtype=np.float32)
    c[:, 128:256] = -(i[None, :] < i[:, None]).astype(np.float32)
    up_strict = (i[:, None] < i[None, :]).astype(np.float32)
    up_incl = (i[:, None] <= i[None, :]).astype(np.float32)
    c[:, 256:384] = -up_strict
    c[:, 384:512] = -up_incl
    c[:, 512:640] = up_strict
    c[:, 640:768] = up_incl
    c[:, 768:896] = -(i[:, None] >= i[None, :]).astype(np.float32)
    blk = (i[:, None] // 64 == i[None, :] // 64).astype(np.float32)
    c[:, 896:1024] = blk / 64.0
    c[:, 1024:1152] = blk
    c[:, 1152:1280] = 1.0
    return c


def make_pvec(inp):
    pv = np.zeros((128, NPV), np.float32)
    src_e = {"w0": inp["a_w0"][0], "a0": inp["a_a0"][0], "k_k": inp["a_k_k"][0], "k_a": inp["a_k_a"][0],
             "r_k": inp["a_r_k"][0].reshape(-1), "ln_g": inp["a_ln_g"][0], "ln_b": inp["a_ln_b"][0]}
    for n in PV_E:
        pv[:, PVO[n]:PVO[n] + 16] = _fm(src_e[n], 16)
    mu = inp["a_mu_in"][0]
    src_d = {"a_norm_g": inp["a_norm_g"][0], "mu_r": mu[0], "mu_k": mu[1], "mu_v": mu[2], "mu_z": mu[3],
             "mu_w": inp["a_mu_w"][0], "mu_a": inp["a_mu_a"][0], "kv_norm_g": inp["kv_norm_g"],
             "b_norm_g": inp["b_norm_g"][0]}
    for n in PV_D:
        pv[:, PVO[n]:PVO[n] + 8] = _fm(src_d[n], 8)
    pv[:, PVO["ada_b_a"]:PVO["ada_b_a"] + 24] = _fm(inp["a_ada_b"][0], 24)
    pv[:, PVO["ada_b_b"]:PVO["ada_b_b"] + 24] = _fm(inp["b_ada_b"][0], 24)
    pv[:, PVO["k_gain"]] = np.asarray(inp["k_gain"], np.float32)
    pv[:, PVO["q_gain"]] = np.asarray(inp["b_q_gain"][0], np.float32)
    pv[:, PVO["eps_rms"]] = 1e-6
    pv[:, PVO["eps_gn"]] = 64e-5
    pv[:, PVO["eps_l2"]] = 1e-12
    pv[:, PVO["one"]] = 1.0
    pv[:64, PVO["hm0"]] = 1.0
    pv[64:, PVO["hm1"]] = 1.0
    return pv


def build_program(n_prompt_tiles=32, do_samples=True, limit=None):
    nc = bass.Bass("TRN2", target_bir_lowering=False)
    P = Prog()
    P.limit = limit

    def din(name, shape, dt=F32):
        return nc.dram_tensor(name, list(shape), dt, kind="ExternalInput").ap()

    def dout(name, shape, dt=F32):
        return nc.dram_tensor(name, list(shape), dt, kind="ExternalOutput").ap()

    def dscr(name, shape, dt=BF16):
        return nc.dram_tensor(name, list(shape), dt, kind="Internal").ap()

    xp = din("xp", [SEQ, D]); xs = din("xs", [2 * DSEQ, D])
    ck = din("ck", [2 * PAST, E]); cv = din("cv", [2 * PAST, E])
    swkv = din("swkv", [2, 32, 64, 64])
    hprev = din("hprev", [128, KC, 3]); c3T = din("c3T", [128, KC, 3])
    pvec = din("pvec", [128, NPV]); cst = din("cst", [128, NCST])
    rowv = din("rowv", [3, D])
    ada_a = din("ada_a", [D, 3 * D]); ada_b = din("ada_b", [D, 3 * D])
    w_in = din("w_in", [D, 4 * E]); w1 = din("w1", [D, 64]); a1 = din("a1", [D, 64])
    w2 = din("w2", [64, E]); a2 = din("a2", [64, E])
    wout_a = din("wout_a", [E, D]); kvw = din("kvw", [D, 2 * E]); bwin = din("bwin", [D, 2 * E])
    wout_b = din("wout_b", [E, D])

    yp = dout("yp", [SEQ, D]); ys = dout("ys", [2 * DSEQ, D])
    kp = dout("kp", [SEQ, E]); vp = dout("vp", [SEQ, E])
    wkvp = dout("wkvp", [32, 64, 64]); shp = dout("shp", [128, KC])
    kso = dout("kso", [2 * DSEQ, E]); vso = dout("vso", [2 * DSEQ, E])
    wkvs = dout("wkvs", [2, 32, 64, 64]); shs = dout("shs", [2, 128, KC])

    WINS = dscr("WINS", [NPAIR, 128, 4, KC, 128])
    KVWS = dscr("KVWS", [8, 128, KC, 512])
    BWS = dscr("BWS", [NHEAD, 128, 2, KC, 128])
    WOS = [dscr("WOSa", [2, 128, 16, 512]), dscr("WOSb", [2, 128, 16, 512])]
    GSC = dscr("GSC", [2, 3, 128, D], F32)
    KTSP = dscr("KTSP", [NHEAD, 128, SEQ]); VSP = dscr("VSP", [SEQ, E])
    KTSS = dscr("KTSS", [2, NHEAD, 128, PAST + DSEQ]); VSS = dscr("VSS", [2, PAST + DSEQ, E])
    bWINS, bKVWS, bBWS, bWOS, bGSC = Buf(), Buf(), Buf(), [Buf(), Buf()], Buf()
    bKT = [Buf(), Buf(), Buf()]
    bVS = [Buf(), Buf(), Buf()]
    bOUT = Buf("outs")

    es = ExitStack()
    with es:
        def T(name, shape, dt=F32):
            t = es.enter_context(nc.sbuf_tensor(name, list(shape), dt))
            return V(t[tuple(slice(None) for _ in shape)], [Buf(name)])

        PB = []
        PBb = []
        for i in range(8):
            t = es.enter_context(nc.psum_tensor(f"pb{i}", [128, 512], F32))
            PB.append(t)
            PBb.append([Buf(f"pb{i}", excl=True)] * 4)

        def ps(i, s0, ns=1):
            w = 128
            return V(PB[i][:, s0 * w:(s0 + ns) * w], PBb[i][s0:s0 + ns])

        def ps4(i, inner):
            return V(PB[i][:, :].rearrange("p (a b) -> p a b", b=inner), PBb[i][:])

        def rb(*vs):
            out = []
            for v in vs:
                if isinstance(v, V):
                    out.extend(v.bufs)
            return out

        def A(x):
            return x.ap if isinstance(x, V) else x

        def mm(out, lhsT, rhs, start=True, stop=True, nochk=False):
            P.op(PE, lambda e: e.matmul(out.ap, lhsT=lhsT.ap, rhs=rhs.ap, start=start, stop=stop, skip_group_check=nochk),
                 reads=rb(lhsT, rhs), writes=out.bufs)

        def tr(out, in_, ident):
            P.op(PE, lambda e: e.matmul(out.ap, lhsT=in_.ap, rhs=ident.ap, start=True, stop=True),
                 reads=rb(in_, ident), writes=out.bufs)

        def act(out, in_, func, bias=0.0, scale=1.0, eng=ACT):
            P.op(ACT, lambda e: e.activation(out=out.ap, in_=in_.ap, func=func, bias=A(bias), scale=A(scale)),
                 reads=rb(in_, bias, scale), writes=out.bufs)

        EH = {DVE: "vector", POOL: "gpsimd"}

        def cp(eng, out, in_):
            if eng == ACT:
                P.op(ACT, lambda e: e.activation(out=out.ap, in_=in_.ap, func=AF.Identity), reads=rb(in_), writes=out.bufs)
            else:
                P.op(eng, lambda e: e.tensor_copy(out=out.ap, in_=in_.ap), reads=rb(in_), writes=out.bufs)

        def tt(eng, out, in0, in1, op):
            P.op(eng, lambda e: e.tensor_tensor(out=out.ap, in0=in0.ap, in1=in1.ap, op=op),
                 reads=rb(in0, in1), writes=out.bufs)

        def ts(eng, out, in0, s1, op0, s2=None, op1=None):
            if op1 is None:
                P.op(eng, lambda e: e.tensor_scalar(out=out.ap, in0=in0.ap, scalar1=A(s1), scalar2=None, op0=op0),
                     reads=rb(in0, s1), writes=out.bufs)
            else:
                P.op(eng, lambda e: e.tensor_scalar(out=out.ap, in0=in0.ap, scalar1=A(s1), scalar2=A(s2), op0=op0, op1=op1),
                     reads=rb(in0, s1, s2), writes=out.bufs)

        def stt(eng, out, in0, scalar, in1, op0, op1):
            P.op(eng, lambda e: e.scalar_tensor_tensor(out=out.ap, in0=in0.ap, scalar=A(scalar), in1=in1.ap, op0=op0, op1=op1),
                 reads=rb(in0, scalar, in1), writes=out.bufs)

        def recip(out, in_):
            P.op(DVE, lambda e: e.reciprocal(out=out.ap, in_=in_.ap), reads=rb(in_), writes=out.bufs)

        def memset(eng, out, val):
            P.op(eng, lambda e: e.memset(out.ap, val), writes=out.bufs)

        def dma(out_ap, in_ap, reads=(), writes=()):
            P.dma(lambda e: e.dma_start(out=out_ap, in_=in_ap, allow_slow_non_contiguous=True), reads=list(reads), writes=list(writes))

        def load(dst, src_ap, rbufs=()):
            dma(dst.ap, src_ap, reads=rbufs, writes=dst.bufs)

        def store(dst_ap, src, wbufs=()):
            dma(dst_ap, src.ap, reads=src.bufs, writes=list(wbufs))

        CST = T("CST", [128, NCST]); PVt = T("PV", [128, NPV]); C3 = T("C3", [128, KC, 3]); HP = T("HP", [128, KC, 3])
        IDB = T("IDB", [128, 128], BF16); TRIB = T("TRIB", [128, 128], BF16); BO1B = T("BO1B", [128, 128], BF16)
        ONEB = T("ONEB", [128, 128], BF16); MSKB = T("MSKB", [128, 128], BF16)
        MOD = T("MOD", [128, 2, 3, 2, KC])
        GBC = [T("GBC0", [128, D]), T("GBC1", [128, D])]
        KGBC = T("KGBC", [128, 128])
        W1A1 = T("W1A1", [128, KC, 2, 64], BF16); W2A2 = T("W2A2", [64, 2, E], BF16)
        STG = T("STG", [128, 2048]); STB = T("STB", [128, 2048], BF16)
        XT = T("XT", [128, D]); XN = T("XN", [128, D])
        HT = T("HT", [128, KC, 129]); DXT = T("DXT", [128, KC, 128])
        MIX = [T(f"MIX{j}", [128, KC, 128], BF16) for j in range(6)]
        HWA = T("HWA", [64, 2, 128], BF16)
        WIN = [T(f"WIN{i}", [128, 4, KC, 128], BF16) for i in range(2)]
        TT = [T(f"T{i}", [128, 128]) for i in range(12)]
        KR = T("KR", [128, 2, 128], BF16); BTT = T("BTT", [128, 128], BF16); KTT = T("KTT", [128, 128], BF16)
        T4B = T("T4B", [128, 128], BF16); T8B = T("T8B", [128, 128], BF16)
        VTOK = T("VTOK", [128, 128], BF16); KTOK = T("KTOK", [128, 128], BF16); NBTOK = T("NBTOK", [128, 128], BF16)
        MALL = T("MALL", [128, 7, 2, 2, 128], BF16)
        ARBT = T("ARBT", [128, 2, 128], BF16); AAKT = T("AAKT", [128, 2, 128], BF16); ARKT = T("ARKT", [128, 2, 128], BF16)
        XB = [T("XB0", [128, 128], BF16), T("XB1", [128, 128], BF16)]
        YG = T("YG", [128, 16, 128], BF16)
        WOUT = T("WOUT", [128, 16, 512], BF16)
        KVW = [T(f"KVW{i}", [128, KC, 512], BF16) for i in range(2)]
        XKT = T("XKT", [128, KC, 128], BF16); H1T = T("H1T", [128, KC, 128], BF16)
        SQ = T("SQ", [128, 512]); KOUT = [T(f"KOUT{i}", [128, 512]) for i in range(2)]
        VOUT = [T(f"VOUT{i}", [128, 512]) for i in range(2)]
        VBt = [T(f"VB{i}", [128, 512], BF16) for i in range(2)]
        KTST = [T(f"KTST{i}", [128, 4, 128], BF16) for i in range(2)]
        SS = T("SS", [128, 8]); SSQ = T("SSQ", [128, 2])
        SF = T("SF", [128, NPAIR, 64]); SBD = T("SBD", [128, NPAIR, 128], BF16)
        KRM = [T(f"KRM{i}", [128, 128], BF16) for i in range(2)]
        BTM = [T(f"BTM{i}", [128, 128], BF16) for i in range(2)]
        KTM = [T(f"KTM{i}", [128, 128], BF16) for i in range(2)]
        BW = [T(f"BW{i}", [128, 2, KC, 128], BF16) for i in range(2)]
        QT = T("QT", [128, 128], BF16)
        NKMAX = SEQ // 128
        KTH = T("KTH", [128, SEQ], BF16); VH = T("VH", [128, NKMAX, 128], BF16)
        E1 = T("E1", [128, 4, 128]); SPB = [T("SPB0", [128, 4, 128], BF16), T("SPB1", [128, 4, 128], BF16)]
        ESt = T("ES", [128, 4, 128])
        AT = [T("AT0", [128, 4, 128], BF16), T("AT1", [128, 4, 128], BF16)]
        NONEB = T("NONEB", [128, 128], BF16)
        CB = T("CB", [128, 128])
        WST = T("WST", [64, 128])

        def pv(name, k=None, n=1):
            o = PVO[name] + (0 if k is None else k)
            return PVt[:, o:o + n]

        def cc(name, w=128):
            return CST[:, CO[name]:CO[name] + w]

        IDF = cc("ident")

        load(CST, cst); load(PVt, pvec); load(C3, c3T); load(HP, hprev)
        load(KGBC, rowv[2:3, 0:128].partition_broadcast(128))
        cp(DVE, IDB, IDF); cp(DVE, TRIB, cc("TriNeg")); cp(DVE, BO1B, cc("BO1")); cp(DVE, ONEB, cc("ones")); ts(DVE, NONEB, cc("ones"), -1.0, ALU.mult)
        cp(DVE, MSKB, CST[:, CO["MKcat"]:CO["MKcat"] + 128])
        ts(DVE, pv("nw0", 0, 16), pv("w0", 0, 16), -1.0, ALU.mult)
        ts(DVE, pv("na0", 0, 16), pv("a0", 0, 16), -1.0, ALU.mult)
        ts(DVE, pv("omka", 0, 16), pv("k_a", 0, 16), -1.0, ALU.mult, 1.0, ALU.add)
        memset(DVE, SF, 0.0); memset(POOL, SBD, 0.0)

        CBC = XN
        for layer, (adaw, normg, bname) in enumerate([(ada_a, "a_norm_g", "ada_b_a"), (ada_b, "b_norm_g", "ada_b_b")]):
            adav = adaw.rearrange("(k p) c -> p k c", p=128)
            stg3 = V(STG.ap.rearrange("p (k c) -> p k c", k=KC), STG.bufs)
            for cch in range(8):
                load(stg3, adav[:, :, cch * 256:(cch + 1) * 256])
                for bi in range(2):
                    blk = cch * 2 + bi
                    o = ps(7, 0)[:, blk * 3:(blk + 1) * 3]
                    for kc in range(KC):
                        mm(o, stg3[:, kc, bi * 128:(bi + 1) * 128], C3[:, kc, :], start=(kc == 0), stop=(kc == KC - 1))
            adp = V(ps(7, 0).ap[:, 0:48].rearrange("p (b s) -> p b s", s=3), ps(7, 0).bufs)
            for s in range(3):
                tt(DVE, MOD[:, layer, s, 1, :], adp[:, 0:8, s], pv(bname, 0, 8), ALU.add)
                tt(DVE, MOD[:, layer, s, 0, :], adp[:, 8:16, s], pv(bname, 8, 8), ALU.add)
                ts(DVE, MOD[:, layer, s, 0, :], MOD[:, layer, s, 0, :], 1.0, ALU.add)
                tt(DVE, MOD[:, layer, s, 0, :], MOD[:, layer, s, 0, :], pv(normg, 0, 8), ALU.mult)
            load(GBC[1], rowv[layer:layer + 1, :].partition_broadcast(128))
            cbc3 = V(CBC.ap.rearrange("p (k c) -> p k c", k=KC), CBC.bufs)
            for s in range(3):
                for kc in range(KC):
                    ts(DVE, cbc3[:, kc, :], cc("ones"), C3[:, kc, s:s + 1], ALU.mult)
                for cch in range(4):
                    load(stg3, adav[:, :, 2048 + cch * 256:2048 + (cch + 1) * 256])
                    o = ps(cch % 2, 0, 2)
                    for kc in range(KC):
                        mm(o, cbc3[:, kc, :], stg3[:, kc, :], start=(kc == 0), stop=(kc == KC - 1))
                    tt(DVE, GBC[0][:, cch * 256:(cch + 1) * 256], o, GBC[1][:, cch * 256:(cch + 1) * 256], ALU.add)
                store(GSC[layer, s], GBC[0], [bGSC])

        cvt_i = [0]

        def convert(src_ap, dst_ap, wb, ncol=2048, shape3=None):
            load(STG[:, 0:ncol], src_ap)
            eng = [DVE, ACT, POOL][cvt_i[0] % 3]
            cvt_i[0] += 1
            cp(eng, STB[:, 0:ncol], STG[:, 0:ncol])
            src = STB[:, 0:ncol]
            if shape3 is not None:
                src = V(src.ap.rearrange("p (a c) -> p a c", a=shape3), src.bufs)
            store(dst_ap, src, [wb])

        for kc in range(KC):
            rows = slice(kc * 128, (kc + 1) * 128)
            for j in range(4):
                convert(w_in[rows, j * E:(j + 1) * E], WINS.rearrange("a p j k c -> p a j k c")[:, :, j, kc, :], bWINS, shape3=16)
            for half in range(2):
                convert(kvw[rows, half * E:(half + 1) * E],
                        KVWS.rearrange("g p k c -> p g k c")[:, half * 4:(half + 1) * 4, kc, :], bKVWS, shape3=4)
            for qz in range(2):
                convert(bwin[rows, qz * E:(qz + 1) * E], BWS.rearrange("h p q k c -> p h q k c")[:, :, qz, kc, :], bBWS, shape3=16)
        for li, wo in enumerate([wout_a, wout_b]):
            for kc in range(16):
                convert(wo[kc * 128:(kc + 1) * 128, :], WOS[li].rearrange("h p k c -> p h k c")[:, :, kc, :], bWOS[li],
                        ncol=1024, shape3=2)
        for i, wsrc in enumerate([w1, a1]):
            load(V(STG.ap[:, 0:512].rearrange("p (k c) -> p k c", k=KC), STG.bufs), wsrc.rearrange("(k p) c -> p k c", p=128))
            cp(DVE, W1A1[:, :, i, :], V(STG.ap[:, 0:512].rearrange("p (k c) -> p k c", k=KC), STG.bufs))
        for i, wsrc in enumerate([w2, a2]):
            load(STG[0:64, :], wsrc)
            cp(DVE, W2A2[:, i, :], STG[0:64, :])

        if do_samples:
            for b in range(2):
                for blk in range(PAST // 128):
                    r0 = b * PAST + blk * 128
                    load(STG, cv[r0:r0 + 128, :])
                    cp([DVE, POOL][blk % 2], STB, STG)
                    store(VSS[b, blk * 128:(blk + 1) * 128, :], STB, [bVS[1 + b]])
                for h in range(NHEAD):
                    stg3 = V(STG.ap[:, 0:1024].rearrange("p (n c) -> p n c", n=8), STG.bufs)
                    load(stg3, ck[b * PAST:(b + 1) * PAST, h * 128:(h + 1) * 128].rearrange("(n p) c -> p n c", p=128))
                    for half in range(2):
                        o = ps(half, 0, 4)
                        for q in range(4):
                            tr(o[:, q * 128:(q + 1) * 128], stg3[:, half * 4 + q, :], IDF)
                        cp([ACT, DVE][half], STB[:, half * 512:(half + 1) * 512], o)
                    store(KTSS[b, h, :, 0:PAST], STB[:, 0:1024], [bKT[1 + b]])

        def gelem(i):
            return [ACT, DVE][i % 2]

        def token_tile(seq, t0, n, x_src, y_dst, k_dst, v_dst, KTS_s, VS_s, koff, first, last, wkv_dst, sh_dst, swkv_src):
            bKTs, bVSs = bKT[seq], bVS[seq]
            L = int(np.log2(n))
            load(XT[:n, :], x_src)
            if first:
                load(GBC[0], GSC[0, seq], [bGSC]); load(GBC[1], GSC[1, seq], [bGSC])
                cp(DVE, HT[:, :, 0], HP[:, :, seq])
                if swkv_src is None:
                    memset(DVE, SF, 0.0); memset(POOL, SBD, 0.0)
                else:
                    for g4 in range(4):
                        sg = V(STG.ap[0:64, :].rearrange("p (h k) -> p h k", h=32), STG.bufs)
                        if g4 == 0:
                            load(sg, swkv_src.rearrange("h v k -> v h k"))
                        o = ps(7, 0, 2)
                        for q in range(4):
                            pr = g4 * 4 + q
                            tr(o[:, q * 64:(q + 1) * 64], V(STG.ap[0:64, pr * 128:(pr + 1) * 128], STG.bufs), IDF[0:64, 0:64])
                        o3 = V(o.ap.rearrange("p (a v) -> p a v", v=64), o.bufs)
                        cp(DVE, SF[:, g4 * 4:(g4 + 1) * 4, :], o3)
                        if g4 == 0:
                            memset(POOL, SBD, 0.0)
                        cp(DVE, SBD[0:64, g4 * 4:(g4 + 1) * 4, 0:64], o3[0:64])
                        cp(DVE, SBD[64:128, g4 * 4:(g4 + 1) * 4, 64:128], o3[64:128])

            def rms_to_T(dsts):
                act(XN[:n, :], XT[:n, :], AF.Square)
                P.op(DVE, lambda e: e.reduce_sum(out=SSQ.ap[:n, 0:1], in_=XN.ap[:n, :], axis=AX.X), reads=XN.bufs, writes=SSQ.bufs)
                act(SSQ[:n, 1:2], SSQ[:n, 0:1], AF.Ln, bias=pv("eps_rms")[:n], scale=1.0 / D)
                act(SSQ[:n, 1:2], SSQ[:n, 1:2], AF.Exp, scale=-0.5)
                ts(DVE, XN[:n, :], XT[:n, :], SSQ[:n, 1:2], ALU.mult)
                for half in range(2):
                    bank = [7, 5][half]
                    for q in range(4):
                        kc = half * 4 + q
                        tr(ps(bank, q)[:, :n], XN[:n, kc * 128:(kc + 1) * 128], IDF[:n, :n])
                    for q in range(4):
                        kc = half * 4 + q
                        for (dst, sc, bi) in dsts:
                            if bi is None:
                                act(dst(kc), ps(bank, q)[:, :n], AF.Identity, scale=sc(kc))
                            else:
                                act(dst(kc), ps(bank, q)[:, :n], AF.Identity, bias=bi(kc), scale=sc(kc))

            rms_to_T([(lambda kc: HT[:, kc, 1:1 + n], lambda kc: MOD[:, 0, seq, 0, kc:kc + 1], lambda kc: MOD[:, 0, seq, 1, kc:kc + 1])])
            tt(DVE, DXT[:, :, :n], HT[:, :, 0:n], HT[:, :, 1:1 + n], ALU.subtract)
            for j, mun in enumerate(["mu_r", "mu_k", "mu_v", "mu_z", "mu_w", "mu_a"]):
                for kc in range(KC):
                    stt(DVE, MIX[j][:, kc, :n], DXT[:, kc, :n], pv(mun, kc), HT[:, kc, 1:1 + n], ALU.mult, ALU.add)
            if last and sh_dst is not None:
                store(sh_dst, HT[:, :, n], [bOUT])
            cp(DVE, HT[:, :, 0], HT[:, :, n])
            for i in range(2):
                o = ps(1, i)[0:64, :n]
                for kc in range(KC):
                    mm(o, W1A1[:, kc, i, :], MIX[4 + i][:, kc, :n], start=(kc == 0), stop=(kc == KC - 1))
            act(WST[:, :n], ps(1, 0)[0:64, :n], AF.Exp, scale=2.0)
            ts(DVE, WST[:, :n], WST[:, :n], 1.0, ALU.add)
            recip(WST[:, :n], WST[:, :n])
            ts(DVE, HWA[:, 0, :n], WST[:, :n], -2.0, ALU.mult, 1.0, ALU.add)
            cp(ACT, HWA[:, 1, :n], ps(1, 1)[0:64, :n])

            PR = ps4(0, 128)
            PW = ps4(1, 128)
            for pr in range(NPAIR):
                Wt = WIN[pr % 2]
                load(Wt, WINS[pr], [bWINS])
                pc = slice(pr * 128, (pr + 1) * 128)
                for j in range(4):
                    for kc in range(KC):
                        mm(ps(0, j)[:, :n], Wt[:, j, kc, :], MIX[j][:, kc, :n], start=(kc == 0), stop=(kc == KC - 1))
                for i in range(2):
                    mm(ps(1, i)[:, :n], W2A2[:, i, pc], HWA[:, i, :n])
                PRr, PRk, PRv, PRz = (ps(0, j)[:, :n] for j in range(4))
                T1, T2, T3, T5, T6, T7, T9, G, EG, EGI, EGM, TZ = (t[:, :n] for t in TT)
                pe = lambda nm: pv(nm, pr)
                act(T1, ps(1, 0)[:, :n], AF.Exp, bias=pe("nw0"), scale=-1.0)
                act(T2, ps(1, 1)[:, :n], AF.Exp, bias=pe("na0"), scale=-1.0)
                ts(DVE, T1, T1, 1.0, ALU.add); recip(T1, T1)
                ts(POOL, T2, T2, 1.0, ALU.add); recip(T2, T2)
                P.op(DVE, lambda e, G=G, T1=T1: e.tensor_tensor_scan(out=G.ap, data0=cc("ones")[:, :n].ap, data1=T1.ap, initial=0.0,
                                                                      op0=ALU.mult, op1=ALU.add), reads=rb(T1, CST), writes=G.bufs)
                act(EG, G, AF.Exp, scale=-DECAY_C)
                act(EGI, G, AF.Exp, scale=DECAY_C)
                tt(POOL, EGM, G, T1, ALU.subtract)
                act(EGM, EGM, AF.Exp, scale=-DECAY_C)
                ts(DVE, T3, PRk, pe("k_k"), ALU.mult)
                act(T4B[:, :n], T3, AF.Square)
                mm(ps(1, 2)[:, :n], BO1B, T4B[:, :n])
                act(T5, ps(1, 2)[:, :n], AF.Ln, bias=pv("eps_l2"))
                act(T5, T5, AF.Exp, scale=-0.5)
                tt(DVE, T3, T3, T5, ALU.mult)
                ts(POOL, T6, T2, pe("k_a"), ALU.mult, pe("omka"), ALU.add)
                tt(DVE, T6, PRk, T6, ALU.mult)
                tt(POOL, T7, T3, T2, ALU.mult)
                tt(DVE, KR[:, 0, :n], T3, EGM, ALU.mult)
                tt(DVE, KR[:, 1, :n], PRr, EG, ALU.mult)
                tt(POOL, BTT[:, :n], T7, EGI, ALU.mult)
                tt(POOL, KTT[:, :n], T6, EGI, ALU.mult)
                for hh in range(2):
                    hm = pv("hm%d" % hh)
                    ts(POOL, KRM[hh][:, :n], KR[:, 0, :n], hm, ALU.mult)
                    ts(POOL, BTM[hh][:, :n], BTT[:, :n], hm, ALU.mult)
                    ts(POOL, KTM[hh][:, :n], KTT[:, :n], hm, ALU.mult)
                stt(DVE, T8B[:, :n], PRr, pe("r_k"), T6, ALU.mult, ALU.mult)
                mm(ps(1, 3)[:, :n], BO1B, T8B[:, :n])
                cp(ACT, T9, PRv)
                tr(ps(7, 0)[:n, :], T9, IDF)
                cp(ACT, VTOK[:n, :], ps(7, 0)[:n, :])
                tr(ps(6, 0)[:n, 0:128], KTT[:, :n], IDB)
                tr(ps(6, 1)[:n, 0:128], BTT[:, :n], IDB)
                cp(DVE, KTOK[:n, :], ps(6, 0)[:n, 0:128])
                act(NBTOK[:n, :], ps(6, 1)[:n, 0:128], AF.Identity, scale=-1.0)
                psN = V(ps(4, 0, 2).ap.rearrange("p (h c) -> p h c", h=2), ps(4, 0, 2).bufs)
                psB = V(PB[2][:, :].rearrange("p (h s c) -> p h s c", h=2, s=2), PBb[2][:])
                psK = V(PB[3][:, :].rearrange("p (h s c) -> p h s c", h=2, s=2), PBb[3][:])
                for hh in range(2):
                    hs = slice(hh * 64, hh * 64 + 64)
                    mm(psN[:n, hh, :n], KRM[hh][:, :n], BTT[:, :n])
                    for s2 in range(2):
                        mm(psB[:n, hh, s2, :n], BTM[hh][:, :n], KR[:, s2, :n])
                        mm(psK[:n, hh, s2, :n], KTM[hh][:, :n], KR[:, s2, :n])
                bc = lambda name, off: V(CST.ap[:n, CO[name] + off:CO[name] + off + n].unsqueeze(1).to_broadcast([n, 2, n]), CST.bufs)
                tt(DVE, MALL[:n, 0, 0, :, :n], psN[:n, :, :n], bc("ML", 0), ALU.mult)
                tt(DVE, MALL[:n, 0, 1, :, :n], psB[:n, :, 0, :n], bc("MUcat", 0), ALU.mult)
                tt(DVE, ARBT[:n, :, :n], psB[:n, :, 1, :n], bc("MUcat", 128), ALU.mult)
                tt(DVE, AAKT[:n, :, :n], psK[:n, :, 0, :n], bc("MKcat", 0), ALU.mult)
                tt(DVE, ARKT[:n, :, :n], psK[:n, :, 1, :n], bc("MKcat", 128), ALU.mult)
                psM = V(PB[5][:, :].rearrange("p (s h c) -> p s h c", s=2, h=2), PBb[5][:])
                for k in range(L - 1):
                    lastk = (k == L - 2)
                    for hh in range(2):
                        if not lastk:
                            mm(psM[:n, 0, hh, :n], MALL[:n, k, 1, hh, :n], MALL[:n, k, 0, hh, :n])
                        mm(psM[:n, 1, hh, :n], MALL[:n, k, 0, hh, :n], MALL[:n, k, 1, hh, :n])
                    if not lastk:
                        cp(gelem(k), MALL[:n, k + 1, :, :, :n], psM[:n, :, :, :n])
                    else:
                        cp(gelem(k), MALL[:n, k + 1, 1, :, :n], psM[:n, 1, :, :n])
                psX = ps(4, 2)
                for hh in range(2):
                    hs = slice(hh * 64, hh * 64 + 64)
                    mm(psX[:n, hs], KR[:, 0, :n], SBD[:, pr, hs], start=(hh == 0), stop=False, nochk=True)
                    mm(psX[:n, hs], AAKT[:n, hh, :n], VTOK[:n, hs], start=False, stop=False, nochk=True)
                cp(ACT, XB[0][:n, :], psX[:n, :])
                for k in range(L):
                    for hh in range(2):
                        hs = slice(hh * 64, hh * 64 + 64)
                        mm(psX[:n, hs], MALL[:n, k, 1, hh, :n], XB[k % 2][:n, hs], start=False, stop=(k == L - 1 and hh == 1), nochk=True)
                    cp(gelem(k + 1), XB[(k + 1) % 2][:n, :], psX[:n, :])
                U = XB[L % 2]
                psY = ps(4, 3)
                mm(psY[:, :n], SBD[:, pr, :], KR[:, 1, :n], start=True, stop=False, nochk=True)
                for hh in range(2):
                    hs = slice(hh * 64, hh * 64 + 64)
                    mm(psY[hs, :n], VTOK[:n, hs], ARKT[:n, hh, :n], start=False, stop=False, nochk=True)
                    mm(psY[hs, :n], U[:n, hs], ARBT[:n, hh, :n], start=False, stop=True, nochk=True)
                psS = ps(7, 1)
                for hh in range(2):
                    hs = slice(hh * 64, hh * 64 + 64)
                    mm(psS[hs, 0:64], KTOK[:n, hs], VTOK[:n, hs], start=True, stop=False)
                    mm(psS[hs, 0:64], NBTOK[:n, hs], U[:n, hs], start=False, stop=True)
                tt(DVE, SF[:, pr, :], psS[:, 0:64], SF[:, pr, :], ALU.add)
                ts(DVE, SF[:, pr, :], SF[:, pr, :], TT[8][:, n - 1:n], ALU.mult)
                cp(ACT, SBD[0:64, pr, 0:64], SF[0:64, pr, :])
                cp(ACT, SBD[64:128, pr, 64:128], SF[64:128, pr, :])
                Y = T1
                cp(ACT, Y, psY[:, :n])
                mm(ps(7, 2)[:, :n], cc("BO64"), Y)
                tt(DVE, Y, Y, ps(7, 2)[:, :n], ALU.subtract)
                act(T2, Y, AF.Square)
                mm(ps(7, 3)[:, :n], cc("BO64"), T2)
                act(T5, ps(7, 3)[:, :n], AF.Ln, bias=pv("eps_gn"))
                act(T5, T5, AF.Exp, scale=-0.5)
                tt(DVE, Y, Y, T5, ALU.mult)
                ts(DVE, Y, Y, pe("ln_g"), ALU.mult, pe("ln_b"), ALU.add)
                tt(DVE, T2, ps(1, 3)[:, :n], T9, ALU.mult)
                tt(POOL, Y, Y, T2, ALU.add)
                act(TZ, PRz, AF.Exp, scale=-1.0)
                ts(DVE, TZ, TZ, 1.0, ALU.add); recip(TZ, TZ)
                tt(DVE, TZ, PRz, TZ, ALU.mult)
                tt(DVE, YG[:, pr, :n], Y, TZ, ALU.mult)

            if last and wkv_dst is not None:
                for g4 in range(4):
                    o = ps(7, 0, 4)
                    for q in range(4):
                        pr = g4 * 4 + q
                        tr(o[0:64, q * 128:(q + 1) * 128], SF[:, pr, :], IDF)
                    cp(ACT, STG[0:64, g4 * 512:(g4 + 1) * 512], o[0:64, :])
                store(wkv_dst.rearrange("h v k -> v h k"), V(STG.ap[0:64, :].rearrange("p (h k) -> p h k", h=32), STG.bufs), [bOUT])

            def out_proj(li, src):
                for half in range(2):
                    load(WOUT, WOS[li][half], [bWOS[li]])
                    o = ps(half, 0, 4)
                    for pr in range(16):
                        mm(o[:n, :], src[:, pr, :n], WOUT[:, pr, :], start=(pr == 0), stop=(pr == 15))
                    hc = slice(half * 512, (half + 1) * 512)
                    tt(DVE, XN[:n, hc], o[:n, :], GBC[li][:n, hc], ALU.mult)
                    tt(POOL, XT[:n, hc], XT[:n, hc], XN[:n, hc], ALU.add)

            out_proj(0, YG)

            rms_to_T([(lambda kc: XKT[:, kc, :n], lambda kc: pv("kv_norm_g", kc), None),
                      (lambda kc: H1T[:, kc, :n], lambda kc: MOD[:, 1, seq, 0, kc:kc + 1], lambda kc: MOD[:, 1, seq, 1, kc:kc + 1])])
            for cg in range(8):
                Wk = KVW[cg % 2]
                load(Wk, KVWS[cg], [bKVWS])
                o = ps(2 + cg % 2, 0, 4)
                for kc in range(KC):
                    mm(o[:n, :], XKT[:, kc, :n], Wk[:, kc, :], start=(kc == 0), stop=(kc == KC - 1))
                cs = slice((cg % 4) * 512, (cg % 4 + 1) * 512)
                if cg < 4:
                    ko = KOUT[cg % 2]
                    act(SQ[:n, :], o[:n, :], AF.Square)
                    P.op(DVE, lambda e, cg=cg: e.reduce_sum(out=SS.ap[:n, 0:4], in_=SQ.ap[:n, :].rearrange("p (h c) -> p h c", h=4), axis=AX.X),
                         reads=SQ.bufs, writes=SS.bufs)
                    act(SS[:n, 4:8], SS[:n, 0:4], AF.Ln, bias=pv("eps_rms")[:n], scale=1.0 / 128)
                    act(SS[:n, 4:8], SS[:n, 4:8], AF.Exp, scale=-0.5)
                    o3 = V(o.ap[:n, :].rearrange("p (h c) -> p h c", h=4), o.bufs)
                    ko3 = V(ko.ap[:n, :].rearrange("p (h c) -> p h c", h=4), ko.bufs)
                    tt(DVE, ko3, o3, V(SS.ap[:n, 4:8].unsqueeze(2).to_broadcast([n, 4, 128]), SS.bufs), ALU.mult)
                    tt(POOL, ko3, ko3, V(KGBC.ap[:n, :].unsqueeze(1).to_broadcast([n, 4, 128]), KGBC.bufs), ALU.mult)
                    store(k_dst[:, cs], ko[:n, :], [bOUT])
                    kst = KTST[cg % 2]
                    ob = ps(7, 0, 4)
                    for q in range(4):
                        tr(ob[:, q * 128:q * 128 + n], ko[:n, q * 128:(q + 1) * 128], IDF[:n, :n])
                    cp(ACT, kst[:, :, :n], V(ob.ap.rearrange("p (h c) -> p h c", h=4)[:, :, :n], ob.bufs))
                    store(KTS_s[cg * 4:(cg + 1) * 4, :, koff + t0:koff + t0 + n].rearrange("h p c -> p h c"), kst[:, :, :n], [bKTs])
                else:
                    vo = VOUT[cg % 2]
                    cp(ACT, vo[:n, :], o[:n, :])
                    cp(DVE, VBt[cg % 2][:n, :], o[:n, :])
                    store(v_dst[:, cs], vo[:n, :], [bOUT])
                    store(VS_s[koff + t0:koff + t0 + n, cs], VBt[cg % 2][:n, :], [bVSs])

            nk_total = koff + t0 + n
            nkb = (nk_total + 127) // 128
            OG = YG
            for h in range(NHEAD):
                Bw = BW[h % 2]
                load(Bw, BWS[h], [bBWS])
                load(KTH[:, 0:nk_total], KTS_s[h, :, 0:nk_total], [bKTs])
                nfull = nk_total // 128
                if nfull > 0:
                    load(VH[:, 0:nfull, :], VS_s[0:nfull * 128, h * 128:(h + 1) * 128].rearrange("(b p) c -> p b c", p=128), [bVSs])
                rem = nk_total - nfull * 128
                if rem > 0:
                    load(VH[0:rem, nfull, :], VS_s[nfull * 128:nk_total, h * 128:(h + 1) * 128], [bVSs])
                psQ, psZg, psSS, psO = ps(0, 0), ps(1, 0), ps(2, 0), ps(4, 0)
                for kc in range(KC):
                    mm(psQ[:, :n], Bw[:, 0, kc, :], H1T[:, kc, :n], start=(kc == 0), stop=(kc == KC - 1))
                for kc in range(KC):
                    mm(psZg[:, :n], Bw[:, 1, kc, :], H1T[:, kc, :n], start=(kc == 0), stop=(kc == KC - 1))
                act(E1[:, 0, :n], psQ[:, :n], AF.Square)
                mm(psSS[:, :n], cc("ones"), E1[:, 0, :n])
                act(ESt[:, 0, :n], psSS[:, :n], AF.Ln, bias=pv("eps_rms"), scale=1.0 / 128)
                act(ESt[:, 0, :n], ESt[:, 0, :n], AF.Exp, scale=-0.5)
                ts(DVE, ESt[:, 0, :n], ESt[:, 0, :n], pv("q_gain"), ALU.mult, 128 ** -0.5, ALU.mult)
                tt(DVE, QT[:, :n], psQ[:, :n], ESt[:, 0, :n], ALU.mult)
                memset(POOL, CB[:, :n], 0.0)
                blocks = list(range(nkb - 1, -1, -1))
                groups = [[blocks[0]]] + [blocks[i:i + 4] for i in range(1, len(blocks), 4)]
                ng = len(groups)
                PZB = [5, 6, 7]

                def pzv(g):
                    return V(PB[PZB[g % 3]][:, :].rearrange("p (a c) -> p a c", a=4), PBb[PZB[g % 3]][:])

                def S1(g):
                    pz = pzv(g)
                    for i, kb in enumerate(groups[g]):
                        ks = min(128, nk_total - kb * 128)
                        mm(pz[:ks, i, :n], KTH[:, kb * 128:kb * 128 + ks], QT[:, :n], start=(i == 0), stop=False, nochk=True)

                def S2(g):
                    pz = pzv(g)
                    G = len(groups[g])
                    kb0 = groups[g][0]
                    ks = min(128, nk_total - kb0 * 128)
                    sp = SPB[g % 2]
                    act(E1[:ks, 0:G, :n], pz[:ks, 0:G, :n], AF.Exp)
                    act(sp[:ks, 0:G, :n], E1[:ks, 0:G, :n], AF.Ln, bias=pv("one")[:ks])
                    if g == 0:
                        tt(POOL, sp[:ks, 0, :n], sp[:ks, 0, :n], MSKB[:ks, :n], ALU.mult)

                def S3(g):
                    pz = pzv(g)
                    G = len(groups[g])
                    kb0 = groups[g][0]
                    ks = min(128, nk_total - kb0 * 128)
                    sp = SPB[g % 2]
                    at = AT[g % 2]
                    for i in range(G):
                        mm(pz[:ks, i, :n], TRIB[:ks, :ks], sp[:ks, i, :n], start=False, stop=False, nochk=True)
                        for i2 in range(i):
                            mm(pz[:ks, i, :n], NONEB[:ks, :ks], sp[:ks, i2, :n], start=False, stop=False, nochk=True)
                    lastg = (g == ng - 1)
                    if not lastg:
                        pcb = ps(2 + g % 2, 0)
                        for i in range(G):
                            mm(pcb[:, :n], ONEB[:ks, :], sp[:ks, i, :n], start=(i == 0), stop=(i == G - 1))
                    cbb = V(CB.ap[:ks, :n].unsqueeze(1).to_broadcast([ks, G, n]), CB.bufs)
                    tt(DVE, ESt[:ks, 0:G, :n], pz[:ks, 0:G, :n], cbb, ALU.subtract)
                    act(at[:ks, 0:G, :n], ESt[:ks, 0:G, :n], AF.Exp)
                    if g == 0:
                        tt(POOL, at[:ks, 0, :n], at[:ks, 0, :n], MSKB[:ks, :n], ALU.mult)
                    if not lastg:
                        tt(DVE, CB[:, :n], CB[:, :n], pcb[:, :n], ALU.add)
                    for i, kb in enumerate(groups[g]):
                        mm(psO[:, :n], VH[:ks, kb, :], at[:ks, i, :n], start=(g == 0 and i == 0), stop=(lastg and i == G - 1), nochk=True)

                S1(0)
                if ng > 1:
                    S1(1)
                S2(0)
                for g in range(ng):
                    if g + 2 < ng:
                        S1(g + 2)
                    if g + 1 < ng:
                        S2(g + 1)
                    S3(g)
                act(E1[:, 0, :n], psZg[:, :n], AF.Exp, scale=-1.0)
                ts(DVE, E1[:, 0, :n], E1[:, 0, :n], 1.0, ALU.add); recip(E1[:, 0, :n], E1[:, 0, :n])
                tt(DVE, E1[:, 0, :n], psZg[:, :n], E1[:, 0, :n], ALU.mult)
                tt(DVE, OG[:, h, :n], psO[:, :n], E1[:, 0, :n], ALU.mult)
            out_proj(1, OG)
            store(y_dst, XT[:n, :], [bOUT])

        if do_samples:
            for b in range(2):
                r = slice(b * DSEQ, (b + 1) * DSEQ)
                token_tile(1 + b, 0, DSEQ, xs[r, :], ys[r, :], kso[r, :], vso[r, :], KTSS[b], VSS[b], PAST, True, True,
                           wkvs[b], shs[b], swkv[b])
        for ti in range(n_prompt_tiles):
            r = slice(ti * 128, (ti + 1) * 128)
            token_tile(0, ti * 128, 128, xp[r, :], yp[r, :], kp[r, :], vp[r, :], KTSP, VSP, 0, ti == 0, ti == n_prompt_tiles - 1,
                       wkvp, shp, None)

        P.final_all()
        nse = [max(1, (c + EPOCH - 1) // EPOCH) for c in P.cnt]
        esems = [[es.enter_context(nc.semaphore(f"s{ENG_NAMES[i]}{k}")) for k in range(nse[i])] for i in range(5)]
        dsems = [es.enter_context(nc.semaphore(f"d{k}")) for k in range(NDMA)]
        block = es.enter_context(nc.Block())
        P.emit(block, esems, dsems)
    return nc, P


_CACHE = {}


def make_in_maps(inp, n_cores=8):
    f = lambda a: np.ascontiguousarray(np.asarray(a, np.float32))
    cst = make_consts()
    pvec = make_pvec(inp)
    rowv = np.zeros((3, D), np.float32)
    rowv[0] = np.asarray(inp["a_ada_b"][0][2 * D:3 * D], np.float32)
    rowv[1] = np.asarray(inp["b_ada_b"][0][2 * D:3 * D], np.float32)
    rowv[2, :128] = np.asarray(inp["k_gain"], np.float32)
    shared = {
        "pvec": pvec, "cst": cst, "rowv": rowv,
        "ada_a": f(inp["a_ada_w"][0]), "ada_b": f(inp["b_ada_w"][0]),
        "w_in": f(inp["a_w_in"][0]), "w1": f(inp["a_w1"][0]), "a1": f(inp["a_a1"][0]),
        "w2": f(inp["a_w2"][0]), "a2": f(inp["a_a2"][0]), "wout_a": f(inp["a_w_out"][0]),
        "kvw": f(inp["kv_w"]), "bwin": f(inp["b_w_in"][0]), "wout_b": f(inp["b_w_out"][0]),
    }
    maps = []
    for i in range(n_cores):
        c3 = np.stack([np.asarray(inp["c_prompt"][i], np.float32),
                       np.asarray(inp["c_sample"][2 * i], np.float32),
                       np.asarray(inp["c_sample"][2 * i + 1], np.float32)], axis=0)
        c3T = np.ascontiguousarray(c3.reshape(3, KC, 128).transpose(2, 1, 0))
        hp = np.zeros((3, D), np.float32)
        hp[1] = inp["state_shift"][0, 2 * i]
        hp[2] = inp["state_shift"][0, 2 * i + 1]
        hpT = np.ascontiguousarray(hp.reshape(3, KC, 128).transpose(2, 1, 0))
        m = dict(shared)
        m.update({
            "xp": f(inp["x_prompt"][i]),
            "xs": f(np.asarray(inp["x_sample"][2 * i:2 * i + 2]).reshape(2 * DSEQ, D)),
            "ck": f(np.asarray(inp["cache_k"][2 * i:2 * i + 2]).reshape(2 * PAST, E)),
            "cv": f(np.asarray(inp["cache_v"][2 * i:2 * i + 2]).reshape(2 * PAST, E)),
            "swkv": f(inp["state_wkv"][0, 2 * i:2 * i + 2]),
            "hprev": hpT, "c3T": c3T,
        })
        maps.append(m)
    return maps


def kernel(**inputs):
    n = 8
    if "nc" not in _CACHE:
        _CACHE["nc"] = build_program()[0]
    nc = _CACHE["nc"]
    maps = make_in_maps(inputs, n)
    res = run_bass_kernel_spmd(nc, maps, core_ids=list(range(n)))
    R = res.results
    B, BD = 8, 16
    y_prompt = np.stack([R[i]["yp"] for i in range(n)], 0).astype(np.float32)
    y_sample = np.concatenate([R[i]["ys"].reshape(2, DSEQ, D) for i in range(n)], 0).astype(np.float32)
    k_prompt = np.stack([R[i]["kp"].reshape(SEQ, 16, 128) for i in range(n)], 0).astype(np.float32)
    v_prompt = np.stack([R[i]["vp"].reshape(SEQ, 16, 128) for i in range(n)], 0).astype(np.float32)
    wkv_prompt = np.stack([R[i]["wkvp"] for i in range(n)], 0)[None].astype(np.float32)
    shift_prompt = np.stack([R[i]["shp"].T.reshape(D) for i in range(n)], 0)[None].astype(np.float32)
    k_sample = np.concatenate([R[i]["kso"].reshape(2, DSEQ, 16, 128) for i in range(n)], 0).astype(np.float32)
    v_sample = np.concatenate([R[i]["vso"].reshape(2, DSEQ, 16, 128) for i in range(n)], 0).astype(np.float32)
    wkv_sample = np.concatenate([R[i]["wkvs"] for i in range(n)], 0)[None].astype(np.float32)
    shift_sample = np.concatenate([np.stack([R[i]["shs"][b].T.reshape(D) for b in range(2)], 0) for i in range(n)], 0)[None].astype(np.float32)
    return (y_prompt, y_sample, k_prompt, v_prompt, wkv_prompt, shift_prompt,
            k_sample, v_sample, wkv_sample, shift_sample)
```

```python
import os
import numpy as np
from contextlib import ExitStack
import concourse.bass as bass
import concourse.mybir as mybir
from concourse.bass_utils import run_bass_kernel_spmd

F32 = mybir.dt.float32
BF16 = mybir.dt.bfloat16
ALU = mybir.AluOpType
AF = mybir.ActivationFunctionType
AX = mybir.AxisListType

PE, ACT, DVE, POOL, SP = 0, 1, 2, 3, 4
ENG_NAMES = ["tensor", "scalar", "vector", "gpsimd", "sync"]
EPOCH = 20000
NDMA = 24

D = 1024
E = 2048
SEQ = 4096
DSEQ = 32
PAST = 1024
NPAIR = 16
NHEAD = 16
KC = 8
DECAY_C = 0.6065306597126334


class Buf:
    __slots__ = ("name", "w", "r", "excl")

    def __init__(self, name="", excl=False):
        self.name = name
        self.w = {}
        self.r = {}
        self.excl = excl


class V:
    __slots__ = ("ap", "bufs")

    def __init__(self, ap, bufs):
        self.ap = ap
        self.bufs = bufs

    def __getitem__(self, k):
        return V(self.ap[k], self.bufs)


class Prog:
    def __init__(self):
        self.ops = [[] for _ in range(5)]
        self.cnt = [0] * 5
        self.seen = [dict() for _ in range(5)]
        self.dma_i = 0
        self.dma_cnt = [0] * NDMA
        self.total = 0
        self.limit = None
        self.log = []

    def _need(self, eng, key, ticket, waits):
        if key == PE and eng == PE:
            return
        s = self.seen[eng]
        if s.get(key, 0) >= ticket:
            return
        s[key] = ticket
        waits.append((key, ticket))

    def _deps(self, eng, reads, writes, waits):
        for b in reads:
            for k, t in b.w.items():
                self._need(eng, k, t, waits)
            if b.excl:
                for k, t in b.r.items():
                    if k != eng:
                        self._need(eng, k, t, waits)
        for b in writes:
            for k, t in b.w.items():
                self._need(eng, k, t, waits)
            for k, t in b.r.items():
                self._need(eng, k, t, waits)

    def op(self, eng, fn, reads=(), writes=()):
        self.total += 1
        if self.limit is not None and self.total > self.limit:
            return
        waits = []
        self._deps(eng, reads, writes, waits)
        self.cnt[eng] += 1
        t = self.cnt[eng]
        self.ops[eng].append([waits, fn, t, False])
        for b in reads:
            b.r[eng] = t
        for b in writes:
            b.w = {eng: t}
            b.r = {}

    def dma(self, fn, reads=(), writes=(), eng=SP):
        self.total += 1
        if self.limit is not None and self.total > self.limit:
            return
        slot = self.dma_i % NDMA
        self.dma_i += 1
        key = ("d", slot)
        waits = []
        if self.dma_cnt[slot] > 0:
            self._need(eng, key, self.dma_cnt[slot], waits)
        self._deps(eng, reads, writes, waits)
        self.dma_cnt[slot] += 1
        t = self.dma_cnt[slot]
        self.ops[eng].append([waits, fn, ("d", slot, t), True])
        for b in reads:
            b.r[key] = t
        for b in writes:
            b.w = {key: t}
            b.r = {}

    def final_all(self, eng=SP):
        waits = []
        for slot in range(NDMA):
            if self.dma_cnt[slot] > 0:
                waits.append((("d", slot), self.dma_cnt[slot]))
        for k in range(4):
            if self.cnt[k] > 0:
                waits.append((k, self.cnt[k]))
        self.ops[eng].append([waits, None, None, False])

    def emit(self, block, esems, dsems):
        prog = self

        def semval(key, t):
            if isinstance(key, tuple):
                return dsems[key[1]], 16 * t
            ep = (t - 1) // EPOCH
            return esems[key][ep], t - ep * EPOCH

        def run(eng_idx):
            def body(e):
                for waits, fn, sig, isdma in prog.ops[eng_idx]:
                    for key, t in waits:
                        s, v = semval(key, t)
                        e.wait_ge(s, v)
                    if fn is None:
                        continue
                    ins = fn(e)
                    if isdma:
                        ins.then_inc(dsems[sig[1]], 16)
                    else:
                        ep = (sig - 1) // EPOCH
                        ins.then_inc(esems[eng_idx][ep], 1)
            return body

        block.tensor(run(PE))
        block.scalar(run(ACT))
        block.vector(run(DVE))
        block.gpsimd(run(POOL))
        block.sync(run(SP))


PV_E = ["w0", "a0", "k_k", "k_a", "r_k", "ln_g", "ln_b"]
PV_D = ["a_norm_g", "mu_r", "mu_k", "mu_v", "mu_z", "mu_w", "mu_a", "kv_norm_g", "b_norm_g"]
PVO = {}
_o = 0
for _n in PV_E:
    PVO[_n] = _o
    _o += 16
for _n in PV_D:
    PVO[_n] = _o
    _o += 8
PVO["ada_b_a"] = _o; _o += 24
PVO["ada_b_b"] = _o; _o += 24
PVO["k_gain"] = _o; _o += 1
PVO["q_gain"] = _o; _o += 1
PVO["eps_rms"] = _o; _o += 1
PVO["eps_gn"] = _o; _o += 1
PVO["eps_l2"] = _o; _o += 1
PVO["one"] = _o; _o += 1
PVO["hm0"] = _o; _o += 1
PVO["hm1"] = _o; _o += 1
PVO["nw0"] = _o; _o += 16
PVO["na0"] = _o; _o += 16
PVO["omka"] = _o; _o += 16
NPV = _o

CO = {"ident": 0, "ML": 128, "MUcat": 256, "MKcat": 512, "TriNeg": 768, "BO64": 896, "BO1": 1024, "ones": 1152}
NCST = 1280


def _fm(v, ncol):
    return np.ascontiguousarray(np.asarray(v, np.float32).reshape(ncol, 128).T)


def make_consts():
    c = np.zeros((128, NCST), np.float32)
    i = np.arange(128)
    c[:, 0:128] = np.eye(128, dtype=np.float32)
    c[:, 128:256] = -(i[None, :] < i[:, None]).astype(np.float32)
    up_strict = (i[:, None] < i[None, :]).astype(np.float32)
    up_incl = (i[:, None] <= i[None, :]).astype(np.float32)
    c[:, 256:384] = -up_strict
    c[:, 384:512] = -up_incl
    c[:, 512:640] = up_strict
    c[:, 640:768] = up_incl
    c[:, 768:896] = -(i[:, None] >= i[None, :]).astype(np.float32)
    blk = (i[:, None] // 64 == i[None, :] // 64).astype(np.float32)
    c[:, 896:1024] = blk / 64.0
    c[:, 1024:1152] = blk
    c[:, 1152:1280] = 1.0
    return c


def make_pvec(inp):
    pv = np.zeros((128, NPV), np.float32)
    src_e = {"w0": inp["a_w0"][0], "a0": inp["a_a0"][0], "k_k": inp["a_k_k"][0], "k_a": inp["a_k_a"][0],
             "r_k": inp["a_r_k"][0].reshape(-1), "ln_g": inp["a_ln_g"][0], "ln_b": inp["a_ln_b"][0]}
    for n in PV_E:
        pv[:, PVO[n]:PVO[n] + 16] = _fm(src_e[n], 16)
    mu = inp["a_mu_in"][0]
    src_d = {"a_norm_g": inp["a_norm_g"][0], "mu_r": mu[0], "mu_k": mu[1], "mu_v": mu[2], "mu_z": mu[3],
             "mu_w": inp["a_mu_w"][0], "mu_a": inp["a_mu_a"][0], "kv_norm_g": inp["kv_norm_g"],
             "b_norm_g": inp["b_norm_g"][0]}
    for n in PV_D:
        pv[:, PVO[n]:PVO[n] + 8] = _fm(src_d[n], 8)
    pv[:, PVO["ada_b_a"]:PVO["ada_b_a"] + 24] = _fm(inp["a_ada_b"][0], 24)
    pv[:, PVO["ada_b_b"]:PVO["ada_b_b"] + 24] = _fm(inp["b_ada_b"][0], 24)
    pv[:, PVO["k_gain"]] = np.asarray(inp["k_gain"], np.float32)
    pv[:, PVO["q_gain"]] = np.asarray(inp["b_q_gain"][0], np.float32)
    pv[:, PVO["eps_rms"]] = 1e-6
    pv[:, PVO["eps_gn"]] = 64e-5
    pv[:, PVO["eps_l2"]] = 1e-12
    pv[:, PVO["one"]] = 1.0
    pv[:64, PVO["hm0"]] = 1.0
    pv[64:, PVO["hm1"]] = 1.0
    return pv


def build_program(n_prompt_tiles=32, do_samples=True, limit=None):
    nc = bass.Bass("TRN2", target_bir_lowering=False)
    P = Prog()
    P.limit = limit

    def din(name, shape, dt=F32):
        return nc.dram_tensor(name, list(shape), dt, kind="ExternalInput").ap()

    def dout(name, shape, dt=F32):
        return nc.dram_tensor(name, list(shape), dt, kind="ExternalOutput").ap()

    def dscr(name, shape, dt=BF16):
        return nc.dram_tensor(name, list(shape), dt, kind="Internal").ap()

    xp = din("xp", [SEQ, D]); xs = din("xs", [2 * DSEQ, D])
    ck = din("ck", [2 * PAST, E]); cv = din("cv", [2 * PAST, E])
    swkv = din("swkv", [2, 32, 64, 64])
    hprev = din("hprev", [128, KC, 3]); c3T = din("c3T", [128, KC, 3])
    pvec = din("pvec", [128, NPV]); cst = din("cst", [128, NCST])
    rowv = din("rowv", [3, D])
    ada_a = din("ada_a", [D, 3 * D]); ada_b = din("ada_b", [D, 3 * D])
    w_in = din("w_in", [D, 4 * E]); w1 = din("w1", [D, 64]); a1 = din("a1", [D, 64])
    w2 = din("w2", [64, E]); a2 = din("a2", [64, E])
    wout_a = din("wout_a", [E, D]); kvw = din("kvw", [D, 2 * E]); bwin = din("bwin", [D, 2 * E])
    wout_b = din("wout_b", [E, D])

    yp = dout("yp", [SEQ, D]); ys = dout("ys", [2 * DSEQ, D])
    kp = dout("kp", [SEQ, E]); vp = dout("vp", [SEQ, E])
    wkvp = dout("wkvp", [32, 64, 64]); shp = dout("shp", [128, KC])
    kso = dout("kso", [2 * DSEQ, E]); vso = dout("vso", [2 * DSEQ, E])
    wkvs = dout("wkvs", [2, 32, 64, 64]); shs = dout("shs", [2, 128, KC])

    WINS = dscr("WINS", [NPAIR, 128, 4, KC, 128])
    KVWS = dscr("KVWS", [8, 128, KC, 512])
    BWS = dscr("BWS", [NHEAD, 128, 2, KC, 128])
    WOS = [dscr("WOSa", [2, 128, 16, 512]), dscr("WOSb", [2, 128, 16, 512])]
    GSC = dscr("GSC", [2, 3, 128, D], F32)
    KTSP = dscr("KTSP", [NHEAD, 128, SEQ]); VSP = dscr("VSP", [SEQ, E])
    KTSS = dscr("KTSS", [2, NHEAD, 128, PAST + DSEQ]); VSS = dscr("VSS", [2, PAST + DSEQ, E])
    bWINS, bKVWS, bBWS, bWOS, bGSC = Buf(), Buf(), Buf(), [Buf(), Buf()], Buf()
    bKT = [Buf(), Buf(), Buf()]
    bVS = [Buf(), Buf(), Buf()]
    bOUT = Buf("outs")

    es = ExitStack()
    with es:
        def T(name, shape, dt=F32):
            t = es.enter_context(nc.sbuf_tensor(name, list(shape), dt))
            return V(t[tuple(slice(None) for _ in shape)], [Buf(name)])

        PB = []
        PBb = []
        for i in range(8):
            t = es.enter_context(nc.psum_tensor(f"pb{i}", [128, 512], F32))
            PB.append(t)
            PBb.append([Buf(f"pb{i}", excl=True)] * 4)

        def ps(i, s0, ns=1):
            w = 128
            return V(PB[i][:, s0 * w:(s0 + ns) * w], PBb[i][s0:s0 + ns])

        def ps4(i, inner):
            return V(PB[i][:, :].rearrange("p (a b) -> p a b", b=inner), PBb[i][:])

        def rb(*vs):
            out = []
            for v in vs:
                if isinstance(v, V):
                    out.extend(v.bufs)
            return out

        def A(x):
            return x.ap if isinstance(x, V) else x

        def mm(out, lhsT, rhs, start=True, stop=True, nochk=False):
            P.op(PE, lambda e: e.matmul(out.ap, lhsT=lhsT.ap, rhs=rhs.ap, start=start, stop=stop, skip_group_check=nochk),
                 reads=rb(lhsT, rhs), writes=out.bufs)

        def tr(out, in_, ident):
            P.op(PE, lambda e: e.matmul(out.ap, lhsT=in_.ap, rhs=ident.ap, start=True, stop=True),
                 reads=rb(in_, ident), writes=out.bufs)

        def act(out, in_, func, bias=0.0, scale=1.0, eng=ACT):
            P.op(ACT, lambda e: e.activation(out=out.ap, in_=in_.ap, func=func, bias=A(bias), scale=A(scale)),
                 reads=rb(in_, bias, scale), writes=out.bufs)

        EH = {DVE: "vector", POOL: "gpsimd"}

        def cp(eng, out, in_):
            if eng == ACT:
                P.op(ACT, lambda e: e.activation(out=out.ap, in_=in_.ap, func=AF.Identity), reads=rb(in_), writes=out.bufs)
            else:
                P.op(eng, lambda e: e.tensor_copy(out=out.ap, in_=in_.ap), reads=rb(in_), writes=out.bufs)

        def tt(eng, out, in0, in1, op):
            P.op(eng, lambda e: e.tensor_tensor(out=out.ap, in0=in0.ap, in1=in1.ap, op=op),
                 reads=rb(in0, in1), writes=out.bufs)

        def ts(eng, out, in0, s1, op0, s2=None, op1=None):
            if op1 is None:
                P.op(eng, lambda e: e.tensor_scalar(out=out.ap, in0=in0.ap, scalar1=A(s1), scalar2=None, op0=op0),
                     reads=rb(in0, s1), writes=out.bufs)
            else:
                P.op(eng, lambda e: e.tensor_scalar(out=out.ap, in0=in0.ap, scalar1=A(s1), scalar2=A(s2), op0=op0, op1=op1),
                     reads=rb(in0, s1, s2), writes=out.bufs)

        def stt(eng, out, in0, scalar, in1, op0, op1):
            P.op(eng, lambda e: e.scalar_tensor_tensor(out=out.ap, in0=in0.ap, scalar=A(scalar), in1=in1.ap, op0=op0, op1=op1),
                 reads=rb(in0, scalar, in1), writes=out.bufs)

        def recip(out, in_):
            P.op(DVE, lambda e: e.reciprocal(out=out.ap, in_=in_.ap), reads=rb(in_), writes=out.bufs)

        def memset(eng, out, val):
            P.op(eng, lambda e: e.memset(out.ap, val), writes=out.bufs)

        def dma(out_ap, in_ap, reads=(), writes=()):
            P.dma(lambda e: e.dma_start(out=out_ap, in_=in_ap, allow_slow_non_contiguous=True), reads=list(reads), writes=list(writes))

        def load(dst, src_ap, rbufs=()):
            dma(dst.ap, src_ap, reads=rbufs, writes=dst.bufs)

        def store(dst_ap, src, wbufs=()):
            dma(dst_ap, src.ap, reads=src.bufs, writes=list(wbufs))

        CST = T("CST", [128, NCST]); PVt = T("PV", [128, NPV]); C3 = T("C3", [128, KC, 3]); HP = T("HP", [128, KC, 3])
        IDB = T("IDB", [128, 128], BF16); TRIB = T("TRIB", [128, 128], BF16); BO1B = T("BO1B", [128, 128], BF16)
        ONEB = T("ONEB", [128, 128], BF16); MSKB = T("MSKB", [128, 128], BF16)
        MOD = T("MOD", [128, 2, 3, 2, KC])
        GBC = [T("GBC0", [128, D]), T("GBC1", [128, D])]
        KGBC = T("KGBC", [128, 128])
        W1A1 = T("W1A1", [128, KC, 2, 64], BF16); W2A2 = T("W2A2", [64, 2, E], BF16)
        STG = T("STG", [128, 2048]); STB = T("STB", [128, 2048], BF16)
        XT = T("XT", [128, D]); XN = T("XN", [128, D])
        HT = T("HT", [128, KC, 129]); DXT = T("DXT", [128, KC, 128])
        MIX = [T(f"MIX{j}", [128, KC, 128], BF16) for j in range(6)]
        HWA = T("HWA", [64, 2, 128], BF16)
        WIN = [T(f"WIN{i}", [128, 4, KC, 128], BF16) for i in range(2)]
        TT = [T(f"T{i}", [128, 128]) for i in range(10)]
        KR = [T(f"KR{i}", [128, 2, 128], BF16) for i in range(2)]
        BTT = [T(f"BTT{i}", [128, 128], BF16) for i in range(2)]; KTT = [T(f"KTT{i}", [128, 128], BF16) for i in range(2)]
        T4B = T("T4B", [128, 128], BF16); T8B = T("T8B", [128, 128], BF16)
        VTOK = [T(f"VTOK{i}", [128, 128], BF16) for i in range(2)]; KTOK = [T(f"KTOK{i}", [128, 128], BF16) for i in range(2)]
        NBTOK = [T(f"NBTOK{i}", [128, 128], BF16) for i in range(2)]
        EGP = [T(f"EGP{i}", [128, 128]) for i in range(2)]; SZP = [T(f"SZP{i}", [128, 128]) for i in range(2)]
        BONV = [T(f"BONV{i}", [128, 128]) for i in range(2)]
        CY = T("CY", [128, 128]); CT2 = T("CT2", [128, 128]); CT5 = T("CT5", [128, 128])
        MALL = T("MALL", [128, 7, 2, 2, 128], BF16)
        ARBT = T("ARBT", [128, 2, 128], BF16); AAKT = T("AAKT", [128, 2, 128], BF16); ARKT = T("ARKT", [128, 2, 128], BF16)
        XB = [T("XB0", [128, 128], BF16), T("XB1", [128, 128], BF16)]
        YG = T("YG", [128, 16, 128], BF16)
        WOUT = T("WOUT", [128, 16, 512], BF16)
        KVW = [T(f"KVW{i}", [128, KC, 512], BF16) for i in range(2)]
        XKT = T("XKT", [128, KC, 128], BF16); H1T = T("H1T", [128, KC, 128], BF16)
        SQ = T("SQ", [128, 512]); KOUT = [T(f"KOUT{i}", [128, 512]) for i in range(2)]
        VOUT = [T(f"VOUT{i}", [128, 512]) for i in range(2)]
        VBt = [T(f"VB{i}", [128, 512], BF16) for i in range(2)]
        KTST = [T(f"KTST{i}", [128, 4, 128], BF16) for i in range(2)]
        SS = T("SS", [128, 8]); SSQ = T("SSQ", [128, 2])
        SF = T("SF", [128, NPAIR, 64]); SBD = T("SBD", [128, NPAIR, 128], BF16)
        KRM = [[T(f"KRM{p}{i}", [128, 128], BF16) for i in range(2)] for p in range(2)]
        BTM = [[T(f"BTM{p}{i}", [128, 128], BF16) for i in range(2)] for p in range(2)]
        KTM = [[T(f"KTM{p}{i}", [128, 128], BF16) for i in range(2)] for p in range(2)]
        BW = [T(f"BW{i}", [128, 2, KC, 128], BF16) for i in range(2)]
        QT = T("QT", [128, 128], BF16)
        NKMAX = SEQ // 128
        KTH = T("KTH", [128, SEQ], BF16); VH = T("VH", [128, NKMAX, 128], BF16)
        E1 = T("E1", [128, 4, 128]); SPB = [T("SPB0", [128, 4, 128], BF16), T("SPB1", [128, 4, 128], BF16)]
        ESt = T("ES", [128, 4, 128])
        AT = [T("AT0", [128, 4, 128], BF16), T("AT1", [128, 4, 128], BF16)]
        NONEB = T("NONEB", [128, 128], BF16)
        CB = T("CB", [128, 128])
        WST = T("WST", [64, 128])

        def pv(name, k=None, n=1):
            o = PVO[name] + (0 if k is None else k)
            return PVt[:, o:o + n]

        def cc(name, w=128):
            return CST[:, CO[name]:CO[name] + w]

        IDF = cc("ident")

        load(CST, cst); load(PVt, pvec); load(C3, c3T); load(HP, hprev)
        load(KGBC, rowv[2:3, 0:128].partition_broadcast(128))
        cp(DVE, IDB, IDF); cp(DVE, TRIB, cc("TriNeg")); cp(DVE, BO1B, cc("BO1")); cp(DVE, ONEB, cc("ones")); ts(DVE, NONEB, cc("ones"), -1.0, ALU.mult)
        cp(DVE, MSKB, CST[:, CO["MKcat"]:CO["MKcat"] + 128])
        ts(DVE, pv("nw0", 0, 16), pv("w0", 0, 16), -1.0, ALU.mult)
        ts(DVE, pv("na0", 0, 16), pv("a0", 0, 16), -1.0, ALU.mult)
        ts(DVE, pv("omka", 0, 16), pv("k_a", 0, 16), -1.0, ALU.mult, 1.0, ALU.add)
        memset(DVE, SF, 0.0); memset(POOL, SBD, 0.0)

        CBC = XN
        for layer, (adaw, normg, bname) in enumerate([(ada_a, "a_norm_g", "ada_b_a"), (ada_b, "b_norm_g", "ada_b_b")]):
            adav = adaw.rearrange("(k p) c -> p k c", p=128)
            stg3 = V(STG.ap.rearrange("p (k c) -> p k c", k=KC), STG.bufs)
            for cch in range(8):
                load(stg3, adav[:, :, cch * 256:(cch + 1) * 256])
                for bi in range(2):
                    blk = cch * 2 + bi
                    o = ps(7, 0)[:, blk * 3:(blk + 1) * 3]
                    for kc in range(KC):
                        mm(o, stg3[:, kc, bi * 128:(bi + 1) * 128], C3[:, kc, :], start=(kc == 0), stop=(kc == KC - 1))
            adp = V(ps(7, 0).ap[:, 0:48].rearrange("p (b s) -> p b s", s=3), ps(7, 0).bufs)
            for s in range(3):
                tt(DVE, MOD[:, layer, s, 1, :], adp[:, 0:8, s], pv(bname, 0, 8), ALU.add)
                tt(DVE, MOD[:, layer, s, 0, :], adp[:, 8:16, s], pv(bname, 8, 8), ALU.add)
                ts(DVE, MOD[:, layer, s, 0, :], MOD[:, layer, s, 0, :], 1.0, ALU.add)
                tt(DVE, MOD[:, layer, s, 0, :], MOD[:, layer, s, 0, :], pv(normg, 0, 8), ALU.mult)
            load(GBC[1], rowv[layer:layer + 1, :].partition_broadcast(128))
            cbc3 = V(CBC.ap.rearrange("p (k c) -> p k c", k=KC), CBC.bufs)
            for s in range(3):
                for kc in range(KC):
                    ts(DVE, cbc3[:, kc, :], cc("ones"), C3[:, kc, s:s + 1], ALU.mult)
                for cch in range(4):
                    load(stg3, adav[:, :, 2048 + cch * 256:2048 + (cch + 1) * 256])
                    o = ps(cch % 2, 0, 2)
                    for kc in range(KC):
                        mm(o, cbc3[:, kc, :], stg3[:, kc, :], start=(kc == 0), stop=(kc == KC - 1))
                    tt(DVE, GBC[0][:, cch * 256:(cch + 1) * 256], o, GBC[1][:, cch * 256:(cch + 1) * 256], ALU.add)
                store(GSC[layer, s], GBC[0], [bGSC])

        cvt_i = [0]

        def convert(src_ap, dst_ap, wb, ncol=2048, shape3=None):
            load(STG[:, 0:ncol], src_ap)
            eng = [DVE, ACT, POOL][cvt_i[0] % 3]
            cvt_i[0] += 1
            cp(eng, STB[:, 0:ncol], STG[:, 0:ncol])
            src = STB[:, 0:ncol]
            if shape3 is not None:
                src = V(src.ap.rearrange("p (a c) -> p a c", a=shape3), src.bufs)
            store(dst_ap, src, [wb])

        for kc in range(KC):
            rows = slice(kc * 128, (kc + 1) * 128)
            for j in range(4):
                convert(w_in[rows, j * E:(j + 1) * E], WINS.rearrange("a p j k c -> p a j k c")[:, :, j, kc, :], bWINS, shape3=16)
            for half in range(2):
                convert(kvw[rows, half * E:(half + 1) * E],
                        KVWS.rearrange("g p k c -> p g k c")[:, half * 4:(half + 1) * 4, kc, :], bKVWS, shape3=4)
            for qz in range(2):
                convert(bwin[rows, qz * E:(qz + 1) * E], BWS.rearrange("h p q k c -> p h q k c")[:, :, qz, kc, :], bBWS, shape3=16)
        for li, wo in enumerate([wout_a, wout_b]):
            for kc in range(16):
                convert(wo[kc * 128:(kc + 1) * 128, :], WOS[li].rearrange("h p k c -> p h k c")[:, :, kc, :], bWOS[li],
                        ncol=1024, shape3=2)
        for i, wsrc in enumerate([w1, a1]):
            load(V(STG.ap[:, 0:512].rearrange("p (k c) -> p k c", k=KC), STG.bufs), wsrc.rearrange("(k p) c -> p k c", p=128))
            cp(DVE, W1A1[:, :, i, :], V(STG.ap[:, 0:512].rearrange("p (k c) -> p k c", k=KC), STG.bufs))
        for i, wsrc in enumerate([w2, a2]):
            load(STG[0:64, :], wsrc)
            cp(DVE, W2A2[:, i, :], STG[0:64, :])

        if do_samples:
            for b in range(2):
                for blk in range(PAST // 128):
                    r0 = b * PAST + blk * 128
                    load(STG, cv[r0:r0 + 128, :])
                    cp([DVE, POOL][blk % 2], STB, STG)
                    store(VSS[b, blk * 128:(blk + 1) * 128, :], STB, [bVS[1 + b]])
                for h in range(NHEAD):
                    stg3 = V(STG.ap[:, 0:1024].rearrange("p (n c) -> p n c", n=8), STG.bufs)
                    load(stg3, ck[b * PAST:(b + 1) * PAST, h * 128:(h + 1) * 128].rearrange("(n p) c -> p n c", p=128))
                    for half in range(2):
                        o = ps(half, 0, 4)
                        for q in range(4):
                            tr(o[:, q * 128:(q + 1) * 128], stg3[:, half * 4 + q, :], IDF)
                        cp([ACT, DVE][half], STB[:, half * 512:(half + 1) * 512], o)
                    store(KTSS[b, h, :, 0:PAST], STB[:, 0:1024], [bKT[1 + b]])

        def gelem(i):
            return [ACT, DVE][i % 2]

        def token_tile(seq, t0, n, x_src, y_dst, k_dst, v_dst, KTS_s, VS_s, koff, first, last, wkv_dst, sh_dst, swkv_src):
            bKTs, bVSs = bKT[seq], bVS[seq]
            L = int(np.log2(n))
            load(XT[:n, :], x_src)
            if first:
                load(GBC[0], GSC[0, seq], [bGSC]); load(GBC[1], GSC[1, seq], [bGSC])
                cp(DVE, HT[:, :, 0], HP[:, :, seq])
                if swkv_src is None:
                    memset(DVE, SF, 0.0); memset(POOL, SBD, 0.0)
                else:
                    for g4 in range(4):
                        sg = V(STG.ap[0:64, :].rearrange("p (h k) -> p h k", h=32), STG.bufs)
                        if g4 == 0:
                            load(sg, swkv_src.rearrange("h v k -> v h k"))
                        o = ps(7, 0, 2)
                        for q in range(4):
                            pr = g4 * 4 + q
                            tr(o[:, q * 64:(q + 1) * 64], V(STG.ap[0:64, pr * 128:(pr + 1) * 128], STG.bufs), IDF[0:64, 0:64])
                        o3 = V(o.ap.rearrange("p (a v) -> p a v", v=64), o.bufs)
                        cp(DVE, SF[:, g4 * 4:(g4 + 1) * 4, :], o3)
                        if g4 == 0:
                            memset(POOL, SBD, 0.0)
                        cp(DVE, SBD[0:64, g4 * 4:(g4 + 1) * 4, 0:64], o3[0:64])
                        cp(DVE, SBD[64:128, g4 * 4:(g4 + 1) * 4, 64:128], o3[64:128])

            def rms_to_T(dsts):
                act(XN[:n, :], XT[:n, :], AF.Square)
                P.op(DVE, lambda e: e.reduce_sum(out=SSQ.ap[:n, 0:1], in_=XN.ap[:n, :], axis=AX.X), reads=XN.bufs, writes=SSQ.bufs)
                act(SSQ[:n, 1:2], SSQ[:n, 0:1], AF.Ln, bias=pv("eps_rms")[:n], scale=1.0 / D)
                act(SSQ[:n, 1:2], SSQ[:n, 1:2], AF.Exp, scale=-0.5)
                ts(DVE, XN[:n, :], XT[:n, :], SSQ[:n, 1:2], ALU.mult)
                for half in range(2):
                    bank = [7, 5][half]
                    for q in range(4):
                        kc = half * 4 + q
                        tr(ps(bank, q)[:, :n], XN[:n, kc * 128:(kc + 1) * 128], IDF[:n, :n])
                    for q in range(4):
                        kc = half * 4 + q
                        for (dst, sc, bi) in dsts:
                            if bi is None:
                                act(dst(kc), ps(bank, q)[:, :n], AF.Identity, scale=sc(kc))
                            else:
                                act(dst(kc), ps(bank, q)[:, :n], AF.Identity, bias=bi(kc), scale=sc(kc))

            rms_to_T([(lambda kc: HT[:, kc, 1:1 + n], lambda kc: MOD[:, 0, seq, 0, kc:kc + 1], lambda kc: MOD[:, 0, seq, 1, kc:kc + 1])])
            tt(DVE, DXT[:, :, :n], HT[:, :, 0:n], HT[:, :, 1:1 + n], ALU.subtract)
            for j, mun in enumerate(["mu_r", "mu_k", "mu_v", "mu_z", "mu_w", "mu_a"]):
                for kc in range(KC):
                    stt(DVE, MIX[j][:, kc, :n], DXT[:, kc, :n], pv(mun, kc), HT[:, kc, 1:1 + n], ALU.mult, ALU.add)
            if last and sh_dst is not None:
                store(sh_dst, HT[:, :, n], [bOUT])
            cp(DVE, HT[:, :, 0], HT[:, :, n])
            for i in range(2):
                o = ps(1, i)[0:64, :n]
                for kc in range(KC):
                    mm(o, W1A1[:, kc, i, :], MIX[4 + i][:, kc, :n], start=(kc == 0), stop=(kc == KC - 1))
            act(WST[:, :n], ps(1, 0)[0:64, :n], AF.Exp, scale=2.0)
            ts(DVE, WST[:, :n], WST[:, :n], 1.0, ALU.add)
            recip(WST[:, :n], WST[:, :n])
            ts(DVE, HWA[:, 0, :n], WST[:, :n], -2.0, ALU.mult, 1.0, ALU.add)
            cp(ACT, HWA[:, 1, :n], ps(1, 1)[0:64, :n])

            def genA(pr, par):
                Wt = WIN[pr % 2]
                if pr == 0:
                    load(Wt, WINS[pr], [bWINS])
                if pr + 1 < NPAIR:
                    load(WIN[(pr + 1) % 2], WINS[pr + 1], [bWINS])
                pc = slice(pr * 128, (pr + 1) * 128)
                for j in range(4):
                    for kc in range(KC):
                        mm(ps(0, j)[:, :n], Wt[:, j, kc, :], MIX[j][:, kc, :n], start=(kc == 0), stop=(kc == KC - 1))
                    yield
                for i in range(2):
                    mm(ps(1, i)[:, :n], W2A2[:, i, pc], HWA[:, i, :n])
                PRr, PRk, PRv, PRz = (ps(0, j)[:, :n] for j in range(4))
                T1, T2, T3, T5, T6, T7, T9, G, EGI, EGM = (t[:, :n] for t in TT[:10])
                EG = EGP[par][:, :n]
                TZ = SZP[par][:, :n]
                KRp, BTTp, KTTp = KR[par], BTT[par], KTT[par]
                pe = lambda nm: pv(nm, pr)
                act(T1, ps(1, 0)[:, :n], AF.Exp, bias=pe("nw0"), scale=-1.0)
                act(T2, ps(1, 1)[:, :n], AF.Exp, bias=pe("na0"), scale=-1.0)
                yield
                ts(DVE, T1, T1, 1.0, ALU.add); recip(T1, T1)
                ts(POOL, T2, T2, 1.0, ALU.add); recip(T2, T2)
                yield
                P.op(DVE, lambda e, G=G, T1=T1: e.tensor_tensor_scan(out=G.ap, data0=cc("ones")[:, :n].ap, data1=T1.ap, initial=0.0,
                                                                      op0=ALU.mult, op1=ALU.add), reads=rb(T1, CST), writes=G.bufs)
                act(EG, G, AF.Exp, scale=-DECAY_C)
                act(EGI, G, AF.Exp, scale=DECAY_C)
                tt(POOL, EGM, G, T1, ALU.subtract)
                act(EGM, EGM, AF.Exp, scale=-DECAY_C)
                yield
                ts(DVE, T3, PRk, pe("k_k"), ALU.mult)
                act(T4B[:, :n], T3, AF.Square)
                mm(ps(1, 2)[:, :n], BO1B, T4B[:, :n])
                yield
                act(T5, ps(1, 2)[:, :n], AF.Ln, bias=pv("eps_l2"))
                act(T5, T5, AF.Exp, scale=-0.5)
                tt(DVE, T3, T3, T5, ALU.mult)
                yield
                ts(POOL, T6, T2, pe("k_a"), ALU.mult, pe("omka"), ALU.add)
                tt(DVE, T6, PRk, T6, ALU.mult)
                tt(POOL, T7, T3, T2, ALU.mult)
                yield
                tt(DVE, KRp[:, 0, :n], T3, EGM, ALU.mult)
                tt(DVE, KRp[:, 1, :n], PRr, EG, ALU.mult)
                tt(POOL, BTTp[:, :n], T7, EGI, ALU.mult)
                tt(POOL, KTTp[:, :n], T6, EGI, ALU.mult)
                yield
                for hh in range(2):
                    hm = pv("hm%d" % hh)
                    ts(POOL, KRM[par][hh][:, :n], KRp[:, 0, :n], hm, ALU.mult)
                    ts(POOL, BTM[par][hh][:, :n], BTTp[:, :n], hm, ALU.mult)
                    ts(POOL, KTM[par][hh][:, :n], KTTp[:, :n], hm, ALU.mult)
                    yield
                stt(DVE, T8B[:, :n], PRr, pe("r_k"), T6, ALU.mult, ALU.mult)
                mm(ps(1, 3)[:, :n], BO1B, T8B[:, :n])
                cp(ACT, T9, PRv)
                yield
                tr(ps(6, 2)[:n, :], T9, IDF)
                tr(ps(6, 0)[:n, 0:128], KTTp[:, :n], IDB)
                tr(ps(6, 1)[:n, 0:128], BTTp[:, :n], IDB)
                tt(DVE, BONV[par][:, :n], ps(1, 3)[:, :n], T9, ALU.mult)
                yield
                cp(ACT, VTOK[par][:n, :], ps(6, 2)[:n, :])
                cp(DVE, KTOK[par][:n, :], ps(6, 0)[:n, 0:128])
                act(NBTOK[par][:n, :], ps(6, 1)[:n, 0:128], AF.Identity, scale=-1.0)
                yield
                act(TZ, PRz, AF.Exp, scale=-1.0)
                ts(DVE, TZ, TZ, 1.0, ALU.add); recip(TZ, TZ)
                tt(DVE, TZ, PRz, TZ, ALU.mult)
                yield

            def genB(pr, par):
                KRp, BTTp, KTTp = KR[par], BTT[par], KTT[par]
                VT, KT, NBT = VTOK[par], KTOK[par], NBTOK[par]
                pe = lambda nm: pv(nm, pr)
                psN = V(ps(4, 0, 2).ap.rearrange("p (h c) -> p h c", h=2), ps(4, 0, 2).bufs)
                psB = V(PB[2][:, :].rearrange("p (h s c) -> p h s c", h=2, s=2), PBb[2][:])
                psK = V(PB[3][:, :].rearrange("p (h s c) -> p h s c", h=2, s=2), PBb[3][:])
                for hh in range(2):
                    mm(psN[:n, hh, :n], KRM[par][hh][:, :n], BTTp[:, :n])
                    for s2 in range(2):
                        mm(psB[:n, hh, s2, :n], BTM[par][hh][:, :n], KRp[:, s2, :n])
                        mm(psK[:n, hh, s2, :n], KTM[par][hh][:, :n], KRp[:, s2, :n])
                bc = lambda name, off: V(CST.ap[:n, CO[name] + off:CO[name] + off + n].unsqueeze(1).to_broadcast([n, 2, n]), CST.bufs)
                tt(DVE, MALL[:n, 0, 0, :, :n], psN[:n, :, :n], bc("ML", 0), ALU.mult)
                tt(DVE, MALL[:n, 0, 1, :, :n], psB[:n, :, 0, :n], bc("MUcat", 0), ALU.mult)
                yield
                tt(DVE, ARBT[:n, :, :n], psB[:n, :, 1, :n], bc("MUcat", 128), ALU.mult)
                tt(DVE, AAKT[:n, :, :n], psK[:n, :, 0, :n], bc("MKcat", 0), ALU.mult)
                tt(DVE, ARKT[:n, :, :n], psK[:n, :, 1, :n], bc("MKcat", 128), ALU.mult)
                psX = ps(4, 2)
                for hh in range(2):
                    hs = slice(hh * 64, hh * 64 + 64)
                    mm(psX[:n, hs], KRp[:, 0, :n], SBD[:, pr, hs], start=(hh == 0), stop=False, nochk=True)
                    mm(psX[:n, hs], AAKT[:n, hh, :n], VT[:n, hs], start=False, stop=False, nochk=True)
                cp(ACT, XB[0][:n, :], psX[:n, :])
                yield
                psM = V(PB[5][:, :].rearrange("p (s h c) -> p s h c", s=2, h=2), PBb[5][:])
                for k in range(L):
                    for hh in range(2):
                        hs = slice(hh * 64, hh * 64 + 64)
                        mm(psX[:n, hs], MALL[:n, k, 1, hh, :n], XB[k % 2][:n, hs], start=False, stop=(k == L - 1 and hh == 1), nochk=True)
                    if k < L - 1:
                        lastk = (k == L - 2)
                        for hh in range(2):
                            if not lastk:
                                mm(psM[:n, 0, hh, :n], MALL[:n, k, 1, hh, :n], MALL[:n, k, 0, hh, :n])
                            mm(psM[:n, 1, hh, :n], MALL[:n, k, 0, hh, :n], MALL[:n, k, 1, hh, :n])
                        if not lastk:
                            cp(DVE, MALL[:n, k + 1, :, :, :n], psM[:n, :, :, :n])
                        else:
                            cp(DVE, MALL[:n, k + 1, 1, :, :n], psM[:n, 1, :, :n])
                    cp(ACT, XB[(k + 1) % 2][:n, :], psX[:n, :])
                    yield
                U = XB[L % 2]
                psY = ps(4, 3)
                mm(psY[:, :n], SBD[:, pr, :], KRp[:, 1, :n], start=True, stop=False, nochk=True)
                for hh in range(2):
                    hs = slice(hh * 64, hh * 64 + 64)
                    mm(psY[hs, :n], VT[:n, hs], ARKT[:n, hh, :n], start=False, stop=False, nochk=True)
                    mm(psY[hs, :n], U[:n, hs], ARBT[:n, hh, :n], start=False, stop=True, nochk=True)
                psS = ps(7, 1)
                for hh in range(2):
                    hs = slice(hh * 64, hh * 64 + 64)
                    mm(psS[hs, 0:64], KT[:n, hs], VT[:n, hs], start=True, stop=False)
                    mm(psS[hs, 0:64], NBT[:n, hs], U[:n, hs], start=False, stop=True)
                Y = CY[:, :n]
                C2 = CT2[:, :n]
                C5 = CT5[:, :n]
                cp(ACT, Y, psY[:, :n])
                yield
                tt(DVE, SF[:, pr, :], psS[:, 0:64], SF[:, pr, :], ALU.add)
                ts(DVE, SF[:, pr, :], SF[:, pr, :], EGP[par][:, n - 1:n], ALU.mult)
                cp(ACT, SBD[0:64, pr, 0:64], SF[0:64, pr, :])
                cp(ACT, SBD[64:128, pr, 64:128], SF[64:128, pr, :])
                mm(ps(7, 2)[:, :n], cc("BO64"), Y)
                yield
                tt(DVE, Y, Y, ps(7, 2)[:, :n], ALU.subtract)
                act(C2, Y, AF.Square)
                mm(ps(7, 3)[:, :n], cc("BO64"), C2)
                yield
                act(C5, ps(7, 3)[:, :n], AF.Ln, bias=pv("eps_gn"))
                act(C5, C5, AF.Exp, scale=-0.5)
                tt(DVE, Y, Y, C5, ALU.mult)
                yield
                ts(DVE, Y, Y, pe("ln_g"), ALU.mult, pe("ln_b"), ALU.add)
                tt(POOL, Y, Y, BONV[par][:, :n], ALU.add)
                tt(DVE, YG[:, pr, :n], Y, SZP[par][:, :n], ALU.mult)
                yield

            def drain(g):
                for _ in g:
                    pass

            def interleave(ga, gb):
                a_done = ga is None
                b_done = False
                while not (a_done and b_done):
                    if not b_done:
                        try:
                            next(gb)
                        except StopIteration:
                            b_done = True
                    if not a_done:
                        try:
                            next(ga)
                        except StopIteration:
                            a_done = True

            npairs = 0 if 'pair' in os.environ.get('KSKIP', '') else NPAIR
            if npairs:
                drain(genA(0, 0))
            for pr in range(npairs):
                interleave(genA(pr + 1, (pr + 1) % 2) if pr + 1 < NPAIR else None, genB(pr, pr % 2))

            if last and wkv_dst is not None:
                for g4 in range(4):
                    o = ps(7, 0, 4)
                    for q in range(4):
                        pr = g4 * 4 + q
                        tr(o[0:64, q * 128:(q + 1) * 128], SF[:, pr, :], IDF)
                    cp(ACT, STG[0:64, g4 * 512:(g4 + 1) * 512], o[0:64, :])
                store(wkv_dst.rearrange("h v k -> v h k"), V(STG.ap[0:64, :].rearrange("p (h k) -> p h k", h=32), STG.bufs), [bOUT])

            def out_proj(li, src):
                for half in range(2):
                    load(WOUT, WOS[li][half], [bWOS[li]])
                    o = ps(half, 0, 4)
                    for pr in range(16):
                        mm(o[:n, :], src[:, pr, :n], WOUT[:, pr, :], start=(pr == 0), stop=(pr == 15))
                    hc = slice(half * 512, (half + 1) * 512)
                    tt(DVE, XN[:n, hc], o[:n, :], GBC[li][:n, hc], ALU.mult)
                    tt(POOL, XT[:n, hc], XT[:n, hc], XN[:n, hc], ALU.add)

            out_proj(0, YG)

            rms_to_T([(lambda kc: XKT[:, kc, :n], lambda kc: pv("kv_norm_g", kc), None),
                      (lambda kc: H1T[:, kc, :n], lambda kc: MOD[:, 1, seq, 0, kc:kc + 1], lambda kc: MOD[:, 1, seq, 1, kc:kc + 1])])
            for cg in range(8):
                Wk = KVW[cg % 2]
                load(Wk, KVWS[cg], [bKVWS])
                o = ps(2 + cg % 2, 0, 4)
                for kc in range(KC):
                    mm(o[:n, :], XKT[:, kc, :n], Wk[:, kc, :], start=(kc == 0), stop=(kc == KC - 1))
                cs = slice((cg % 4) * 512, (cg % 4 + 1) * 512)
                if cg < 4:
                    ko = KOUT[cg % 2]
                    act(SQ[:n, :], o[:n, :], AF.Square)
                    P.op(DVE, lambda e, cg=cg: e.reduce_sum(out=SS.ap[:n, 0:4], in_=SQ.ap[:n, :].rearrange("p (h c) -> p h c", h=4), axis=AX.X),
                         reads=SQ.bufs, writes=SS.bufs)
                    act(SS[:n, 4:8], SS[:n, 0:4], AF.Ln, bias=pv("eps_rms")[:n], scale=1.0 / 128)
                    act(SS[:n, 4:8], SS[:n, 4:8], AF.Exp, scale=-0.5)
                    o3 = V(o.ap[:n, :].rearrange("p (h c) -> p h c", h=4), o.bufs)
                    ko3 = V(ko.ap[:n, :].rearrange("p (h c) -> p h c", h=4), ko.bufs)
                    tt(DVE, ko3, o3, V(SS.ap[:n, 4:8].unsqueeze(2).to_broadcast([n, 4, 128]), SS.bufs), ALU.mult)
                    tt(POOL, ko3, ko3, V(KGBC.ap[:n, :].unsqueeze(1).to_broadcast([n, 4, 128]), KGBC.bufs), ALU.mult)
                    store(k_dst[:, cs], ko[:n, :], [bOUT])
                    kst = KTST[cg % 2]
                    ob = ps(7, 0, 4)
                    for q in range(4):
                        tr(ob[:, q * 128:q * 128 + n], ko[:n, q * 128:(q + 1) * 128], IDF[:n, :n])
                    cp(ACT, kst[:, :, :n], V(ob.ap.rearrange("p (h c) -> p h c", h=4)[:, :, :n], ob.bufs))
                    store(KTS_s[cg * 4:(cg + 1) * 4, :, koff + t0:koff + t0 + n].rearrange("h p c -> p h c"), kst[:, :, :n], [bKTs])
                else:
                    vo = VOUT[cg % 2]
                    cp(ACT, vo[:n, :], o[:n, :])
                    cp(DVE, VBt[cg % 2][:n, :], o[:n, :])
                    store(v_dst[:, cs], vo[:n, :], [bOUT])
                    store(VS_s[koff + t0:koff + t0 + n, cs], VBt[cg % 2][:n, :], [bVSs])

            nk_total = koff + t0 + n
            nkb = (nk_total + 127) // 128
            OG = YG
            for h in range(0 if 'att' in os.environ.get('KSKIP', '') else NHEAD):
                Bw = BW[h % 2]
                load(Bw, BWS[h], [bBWS])
                load(KTH[:, 0:nk_total], KTS_s[h, :, 0:nk_total], [bKTs])
                nfull = nk_total // 128
                if nfull > 0:
                    load(VH[:, 0:nfull, :], VS_s[0:nfull * 128, h * 128:(h + 1) * 128].rearrange("(b p) c -> p b c", p=128), [bVSs])
                rem = nk_total - nfull * 128
                if rem > 0:
                    load(VH[0:rem, nfull, :], VS_s[nfull * 128:nk_total, h * 128:(h + 1) * 128], [bVSs])
                psQ, psZg, psSS, psO = ps(0, 0), ps(1, 0), ps(2, 0), ps(4, 0)
                for kc in range(KC):
                    mm(psQ[:, :n], Bw[:, 0, kc, :], H1T[:, kc, :n], start=(kc == 0), stop=(kc == KC - 1))
                for kc in range(KC):
                    mm(psZg[:, :n], Bw[:, 1, kc, :], H1T[:, kc, :n], start=(kc == 0), stop=(kc == KC - 1))
                act(E1[:, 0, :n], psQ[:, :n], AF.Square)
                mm(psSS[:, :n], cc("ones"), E1[:, 0, :n])
                act(ESt[:, 0, :n], psSS[:, :n], AF.Ln, bias=pv("eps_rms"), scale=1.0 / 128)
                act(ESt[:, 0, :n], ESt[:, 0, :n], AF.Exp, scale=-0.5)
                ts(DVE, ESt[:, 0, :n], ESt[:, 0, :n], pv("q_gain"), ALU.mult, 128 ** -0.5, ALU.mult)
                tt(DVE, QT[:, :n], psQ[:, :n], ESt[:, 0, :n], ALU.mult)
                memset(POOL, CB[:, :n], 0.0)
                blocks = list(range(nkb - 1, -1, -1))
                groups = [[blocks[0]]] + [blocks[i:i + 4] for i in range(1, len(blocks), 4)]
                ng = len(groups)
                PZB = [5, 6, 7]

                def pzv(g):
                    return V(PB[PZB[g % 3]][:, :].rearrange("p (a c) -> p a c", a=4), PBb[PZB[g % 3]][:])

                def S1(g):
                    pz = pzv(g)
                    for i, kb in enumerate(groups[g]):
                        ks = min(128, nk_total - kb * 128)
                        mm(pz[:ks, i, :n], KTH[:, kb * 128:kb * 128 + ks], QT[:, :n], start=(i == 0), stop=False, nochk=True)

                def S2(g):
                    pz = pzv(g)
                    G = len(groups[g])
                    kb0 = groups[g][0]
                    ks = min(128, nk_total - kb0 * 128)
                    sp = SPB[g % 2]
                    act(E1[:ks, 0:G, :n], pz[:ks, 0:G, :n], AF.Exp)
                    act(sp[:ks, 0:G, :n], E1[:ks, 0:G, :n], AF.Ln, bias=pv("one")[:ks])
                    if g == 0:
                        tt(POOL, sp[:ks, 0, :n], sp[:ks, 0, :n], MSKB[:ks, :n], ALU.mult)

                def S3(g):
                    pz = pzv(g)
                    G = len(groups[g])
                    kb0 = groups[g][0]
                    ks = min(128, nk_total - kb0 * 128)
                    sp = SPB[g % 2]
                    at = AT[g % 2]
                    for i in range(G):
                        mm(pz[:ks, i, :n], TRIB[:ks, :ks], sp[:ks, i, :n], start=False, stop=False, nochk=True)
                        for i2 in range(i):
                            mm(pz[:ks, i, :n], NONEB[:ks, :ks], sp[:ks, i2, :n], start=False, stop=False, nochk=True)
                    lastg = (g == ng - 1)
                    if not lastg:
                        pcb = ps(2 + g % 2, 0)
                        for i in range(G):
                            mm(pcb[:, :n], ONEB[:ks, :], sp[:ks, i, :n], start=(i == 0), stop=(i == G - 1))
                    cbb = V(CB.ap[:ks, :n].unsqueeze(1).to_broadcast([ks, G, n]), CB.bufs)
                    tt(DVE, ESt[:ks, 0:G, :n], pz[:ks, 0:G, :n], cbb, ALU.subtract)
                    act(at[:ks, 0:G, :n], ESt[:ks, 0:G, :n], AF.Exp)
                    if g == 0:
                        tt(POOL, at[:ks, 0, :n], at[:ks, 0, :n], MSKB[:ks, :n], ALU.mult)
                    if not lastg:
                        tt(DVE, CB[:, :n], CB[:, :n], pcb[:, :n], ALU.add)
                    for i, kb in enumerate(groups[g]):
                        mm(psO[:, :n], VH[:ks, kb, :], at[:ks, i, :n], start=(g == 0 and i == 0), stop=(lastg and i == G - 1), nochk=True)

                S1(0)
                if ng > 1:
                    S1(1)
                S2(0)
                for g in range(ng):
                    if g + 2 < ng:
                        S1(g + 2)
                    if g + 1 < ng:
                        S2(g + 1)
                    S3(g)
                act(E1[:, 0, :n], psZg[:, :n], AF.Exp, scale=-1.0)
                ts(DVE, E1[:, 0, :n], E1[:, 0, :n], 1.0, ALU.add); recip(E1[:, 0, :n], E1[:, 0, :n])
                tt(DVE, E1[:, 0, :n], psZg[:, :n], E1[:, 0, :n], ALU.mult)
                tt(DVE, OG[:, h, :n], psO[:, :n], E1[:, 0, :n], ALU.mult)
            out_proj(1, OG)
            store(y_dst, XT[:n, :], [bOUT])

        if do_samples:
            for b in range(2):
                r = slice(b * DSEQ, (b + 1) * DSEQ)
                token_tile(1 + b, 0, DSEQ, xs[r, :], ys[r, :], kso[r, :], vso[r, :], KTSS[b], VSS[b], PAST, True, True,
                           wkvs[b], shs[b], swkv[b])
        for ti in range(n_prompt_tiles):
            r = slice(ti * 128, (ti + 1) * 128)
            token_tile(0, ti * 128, 128, xp[r, :], yp[r, :], kp[r, :], vp[r, :], KTSP, VSP, 0, ti == 0, ti == n_prompt_tiles - 1,
                       wkvp, shp, None)

        P.final_all()
        nse = [max(1, (c + EPOCH - 1) // EPOCH) for c in P.cnt]
        esems = [[es.enter_context(nc.semaphore(f"s{ENG_NAMES[i]}{k}")) for k in range(nse[i])] for i in range(5)]
        dsems = [es.enter_context(nc.semaphore(f"d{k}")) for k in range(NDMA)]
        block = es.enter_context(nc.Block())
        P.emit(block, esems, dsems)
    return nc, P


_CACHE = {}


def make_in_maps(inp, n_cores=8):
    f = lambda a: np.ascontiguousarray(np.asarray(a, np.float32))
    cst = make_consts()
    pvec = make_pvec(inp)
    rowv = np.zeros((3, D), np.float32)
    rowv[0] = np.asarray(inp["a_ada_b"][0][2 * D:3 * D], np.float32)
    rowv[1] = np.asarray(inp["b_ada_b"][0][2 * D:3 * D], np.float32)
    rowv[2, :128] = np.asarray(inp["k_gain"], np.float32)
    shared = {
        "pvec": pvec, "cst": cst, "rowv": rowv,
        "ada_a": f(inp["a_ada_w"][0]), "ada_b": f(inp["b_ada_w"][0]),
        "w_in": f(inp["a_w_in"][0]), "w1": f(inp["a_w1"][0]), "a1": f(inp["a_a1"][0]),
        "w2": f(inp["a_w2"][0]), "a2": f(inp["a_a2"][0]), "wout_a": f(inp["a_w_out"][0]),
        "kvw": f(inp["kv_w"]), "bwin": f(inp["b_w_in"][0]), "wout_b": f(inp["b_w_out"][0]),
    }
    maps = []
    for i in range(n_cores):
        c3 = np.stack([np.asarray(inp["c_prompt"][i], np.float32),
                       np.asarray(inp["c_sample"][2 * i], np.float32),
                       np.asarray(inp["c_sample"][2 * i + 1], np.float32)], axis=0)
        c3T = np.ascontiguousarray(c3.reshape(3, KC, 128).transpose(2, 1, 0))
        hp = np.zeros((3, D), np.float32)
        hp[1] = inp["state_shift"][0, 2 * i]
        hp[2] = inp["state_shift"][0, 2 * i + 1]
        hpT = np.ascontiguousarray(hp.reshape(3, KC, 128).transpose(2, 1, 0))
        m = dict(shared)
        m.update({
            "xp": f(inp["x_prompt"][i]),
            "xs": f(np.asarray(inp["x_sample"][2 * i:2 * i + 2]).reshape(2 * DSEQ, D)),
            "ck": f(np.asarray(inp["cache_k"][2 * i:2 * i + 2]).reshape(2 * PAST, E)),
            "cv": f(np.asarray(inp["cache_v"][2 * i:2 * i + 2]).reshape(2 * PAST, E)),
            "swkv": f(inp["state_wkv"][0, 2 * i:2 * i + 2]),
            "hprev": hpT, "c3T": c3T,
        })
        maps.append(m)
    return maps


def kernel(**inputs):
    n = 8
    if "nc" not in _CACHE:
        _CACHE["nc"] = build_program()[0]
    nc = _CACHE["nc"]
    maps = make_in_maps(inputs, n)
    res = run_bass_kernel_spmd(nc, maps, core_ids=list(range(n)))
    R = res.results
    B, BD = 8, 16
    y_prompt = np.stack([R[i]["yp"] for i in range(n)], 0).astype(np.float32)
    y_sample = np.concatenate([R[i]["ys"].reshape(2, DSEQ, D) for i in range(n)], 0).astype(np.float32)
    k_prompt = np.stack([R[i]["kp"].reshape(SEQ, 16, 128) for i in range(n)], 0).astype(np.float32)
    v_prompt = np.stack([R[i]["vp"].reshape(SEQ, 16, 128) for i in range(n)], 0).astype(np.float32)
    wkv_prompt = np.stack([R[i]["wkvp"] for i in range(n)], 0)[None].astype(np.float32)
    shift_prompt = np.stack([R[i]["shp"].T.reshape(D) for i in range(n)], 0)[None].astype(np.float32)
    k_sample = np.concatenate([R[i]["kso"].reshape(2, DSEQ, 16, 128) for i in range(n)], 0).astype(np.float32)
    v_sample = np.concatenate([R[i]["vso"].reshape(2, DSEQ, 16, 128) for i in range(n)], 0).astype(np.float32)
    wkv_sample = np.concatenate([R[i]["wkvs"] for i in range(n)], 0)[None].astype(np.float32)
    shift_sample = np.concatenate([np.stack([R[i]["shs"][b].T.reshape(D) for b in range(2)], 0) for i in range(n)], 0)[None].astype(np.float32)
    return (y_prompt, y_sample, k_prompt, v_prompt, wkv_prompt, shift_prompt,
            k_sample, v_sample, wkv_sample, shift_sample)
```

```python
import os
import numpy as np
from contextlib import ExitStack
import concourse.bass as bass
import concourse.mybir as mybir
from concourse.bass_utils import run_bass_kernel_spmd

F32 = mybir.dt.float32
BF16 = mybir.dt.bfloat16
ALU = mybir.AluOpType
AF = mybir.ActivationFunctionType
AX = mybir.AxisListType

PE, ACT, DVE, POOL, SP = 0, 1, 2, 3, 4
ENG_NAMES = ["tensor", "scalar", "vector", "gpsimd", "sync"]
EPOCH = 20000
NDMA = 24

D = 1024
E = 2048
SEQ = 4096
DSEQ = 32
PAST = 1024
NPAIR = 16
NHEAD = 16
KC = 8
DECAY_C = 0.6065306597126334


class Buf:
    __slots__ = ("name", "w", "r", "excl")

    def __init__(self, name="", excl=False):
        self.name = name
        self.w = {}
        self.r = {}
        self.excl = excl


class V:
    __slots__ = ("ap", "bufs")

    def __init__(self, ap, bufs):
        self.ap = ap
        self.bufs = bufs

    def __getitem__(self, k):
        return V(self.ap[k], self.bufs)


class Prog:
    def __init__(self):
        self.ops = [[] for _ in range(5)]
        self.cnt = [0] * 5
        self.seen = [dict() for _ in range(5)]
        self.dma_i = 0
        self.dma_cnt = [0] * NDMA
        self.total = 0
        self.limit = None
        self.log = []

    def _need(self, eng, key, ticket, waits):
        if key == PE and eng == PE:
            return
        s = self.seen[eng]
        if s.get(key, 0) >= ticket:
            return
        s[key] = ticket
        waits.append((key, ticket))

    def _deps(self, eng, reads, writes, waits):
        for b in reads:
            for k, t in b.w.items():
                self._need(eng, k, t, waits)
            if b.excl:
                for k, t in b.r.items():
                    if k != eng:
                        self._need(eng, k, t, waits)
        for b in writes:
            for k, t in b.w.items():
                self._need(eng, k, t, waits)
            for k, t in b.r.items():
                self._need(eng, k, t, waits)

    def op(self, eng, fn, reads=(), writes=()):
        self.total += 1
        if self.limit is not None and self.total > self.limit:
            return
        waits = []
        self._deps(eng, reads, writes, waits)
        self.cnt[eng] += 1
        t = self.cnt[eng]
        self.ops[eng].append([waits, fn, t, False])
        for b in reads:
            b.r[eng] = t
        for b in writes:
            b.w = {eng: t}
            b.r = {}

    def dma(self, fn, reads=(), writes=(), eng=SP):
        self.total += 1
        if self.limit is not None and self.total > self.limit:
            return
        slot = self.dma_i % NDMA
        self.dma_i += 1
        key = ("d", slot)
        waits = []
        if self.dma_cnt[slot] > 0:
            self._need(eng, key, self.dma_cnt[slot], waits)
        self._deps(eng, reads, writes, waits)
        self.dma_cnt[slot] += 1
        t = self.dma_cnt[slot]
        self.ops[eng].append([waits, fn, ("d", slot, t), True])
        for b in reads:
            b.r[key] = t
        for b in writes:
            b.w = {key: t}
            b.r = {}

    def final_all(self, eng=SP):
        waits = []
        for slot in range(NDMA):
            if self.dma_cnt[slot] > 0:
                waits.append((("d", slot), self.dma_cnt[slot]))
        for k in range(4):
            if self.cnt[k] > 0:
                waits.append((k, self.cnt[k]))
        self.ops[eng].append([waits, None, None, False])

    def emit(self, block, esems, dsems):
        prog = self

        def semval(key, t):
            if isinstance(key, tuple):
                return dsems[key[1]], 16 * t
            ep = (t - 1) // EPOCH
            return esems[key][ep], t - ep * EPOCH

        def run(eng_idx):
            def body(e):
                for waits, fn, sig, isdma in prog.ops[eng_idx]:
                    for key, t in waits:
                        s, v = semval(key, t)
                        e.wait_ge(s, v)
                    if fn is None:
                        continue
                    ins = fn(e)
                    if isdma:
                        ins.then_inc(dsems[sig[1]], 16)
                    else:
                        ep = (sig - 1) // EPOCH
                        ins.then_inc(esems[eng_idx][ep], 1)
            return body

        block.tensor(run(PE))
        block.scalar(run(ACT))
        block.vector(run(DVE))
        block.gpsimd(run(POOL))
        block.sync(run(SP))


PV_E = ["w0", "a0", "k_k", "k_a", "r_k", "ln_g", "ln_b"]
PV_D = ["a_norm_g", "mu_r", "mu_k", "mu_v", "mu_z", "mu_w", "mu_a", "kv_norm_g", "b_norm_g"]
PVO = {}
_o = 0
for _n in PV_E:
    PVO[_n] = _o
    _o += 16
for _n in PV_D:
    PVO[_n] = _o
    _o += 8
PVO["ada_b_a"] = _o; _o += 24
PVO["ada_b_b"] = _o; _o += 24
PVO["k_gain"] = _o; _o += 1
PVO["q_gain"] = _o; _o += 1
PVO["eps_rms"] = _o; _o += 1
PVO["eps_gn"] = _o; _o += 1
PVO["eps_l2"] = _o; _o += 1
PVO["one"] = _o; _o += 1
PVO["hm0"] = _o; _o += 1
PVO["hm1"] = _o; _o += 1
PVO["nw0"] = _o; _o += 16
PVO["na0"] = _o; _o += 16
PVO["omka"] = _o; _o += 16
NPV = _o

CO = {"ident": 0, "ML": 128, "MUcat": 256, "MKcat": 512, "TriNeg": 768, "BO64": 896, "BO1": 1024, "ones": 1152}
NCST = 1280


def _fm(v, ncol):
    return np.ascontiguousarray(np.asarray(v, np.float32).reshape(ncol, 128).T)


def make_consts():
    c = np.zeros((128, NCST), np.float32)
    i = np.arange(128)
    c[:, 0:128] = np.eye(128, dtype=np.float32)
    c[:, 128:256] = -(i[None, :] < i[:, None]).astype(np.float32)
    up_strict = (i[:, None] < i[None, :]).astype(np.float32)
    up_incl = (i[:, None] <= i[None, :]).astype(np.float32)
    c[:, 256:384] = -up_strict
    c[:, 384:512] = -up_incl
    c[:, 512:640] = up_strict
    c[:, 640:768] = up_incl
    c[:, 768:896] = -(i[:, None] >= i[None, :]).astype(np.float32)
    blk = (i[:, None] // 64 == i[None, :] // 64).astype(np.float32)
    c[:, 896:1024] = blk / 64.0
    c[:, 1024:1152] = blk
    c[:, 1152:1280] = 1.0
    return c


def make_pvec(inp):
    pv = np.zeros((128, NPV), np.float32)
    src_e = {"w0": inp["a_w0"][0], "a0": inp["a_a0"][0], "k_k": inp["a_k_k"][0], "k_a": inp["a_k_a"][0],
             "r_k": inp["a_r_k"][0].reshape(-1), "ln_g": inp["a_ln_g"][0], "ln_b": inp["a_ln_b"][0]}
    for n in PV_E:
        pv[:, PVO[n]:PVO[n] + 16] = _fm(src_e[n], 16)
    mu = inp["a_mu_in"][0]
    src_d = {"a_norm_g": inp["a_norm_g"][0], "mu_r": mu[0], "mu_k": mu[1], "mu_v": mu[2], "mu_z": mu[3],
             "mu_w": inp["a_mu_w"][0], "mu_a": inp["a_mu_a"][0], "kv_norm_g": inp["kv_norm_g"],
             "b_norm_g": inp["b_norm_g"][0]}
    for n in PV_D:
        pv[:, PVO[n]:PVO[n] + 8] = _fm(src_d[n], 8)
    pv[:, PVO["ada_b_a"]:PVO["ada_b_a"] + 24] = _fm(inp["a_ada_b"][0], 24)
    pv[:, PVO["ada_b_b"]:PVO["ada_b_b"] + 24] = _fm(inp["b_ada_b"][0], 24)
    pv[:, PVO["k_gain"]] = np.asarray(inp["k_gain"], np.float32)
    pv[:, PVO["q_gain"]] = np.asarray(inp["b_q_gain"][0], np.float32)
    pv[:, PVO["eps_rms"]] = 1e-6
    pv[:, PVO["eps_gn"]] = 64e-5
    pv[:, PVO["eps_l2"]] = 1e-12
    pv[:, PVO["one"]] = 1.0
    pv[:64, PVO["hm0"]] = 1.0
    pv[64:, PVO["hm1"]] = 1.0
    return pv


def build_program(n_prompt_tiles=32, do_samples=True, limit=None):
    nc = bass.Bass("TRN2", target_bir_lowering=False)
    P = Prog()
    P.limit = limit

    def din(name, shape, dt=F32):
        return nc.dram_tensor(name, list(shape), dt, kind="ExternalInput").ap()

    def dout(name, shape, dt=F32):
        return nc.dram_tensor(name, list(shape), dt, kind="ExternalOutput").ap()

    def dscr(name, shape, dt=BF16):
        return nc.dram_tensor(name, list(shape), dt, kind="Internal").ap()

    xp = din("xp", [SEQ, D]); xs = din("xs", [2 * DSEQ, D])
    ck = din("ck", [2 * PAST, E]); cv = din("cv", [2 * PAST, E])
    swkv = din("swkv", [2, 32, 64, 64])
    hprev = din("hprev", [128, KC, 3]); c3T = din("c3T", [128, KC, 3])
    pvec = din("pvec", [128, NPV]); cst = din("cst", [128, NCST])
    rowv = din("rowv", [3, D])
    ada_a = din("ada_a", [D, 3 * D]); ada_b = din("ada_b", [D, 3 * D])
    w_in = din("w_in", [D, 4 * E]); w1 = din("w1", [D, 64]); a1 = din("a1", [D, 64])
    w2 = din("w2", [64, E]); a2 = din("a2", [64, E])
    wout_a = din("wout_a", [E, D]); kvw = din("kvw", [D, 2 * E]); bwin = din("bwin", [D, 2 * E])
    wout_b = din("wout_b", [E, D])

    yp = dout("yp", [SEQ, D]); ys = dout("ys", [2 * DSEQ, D])
    kp = dout("kp", [SEQ, E]); vp = dout("vp", [SEQ, E])
    wkvp = dout("wkvp", [32, 64, 64]); shp = dout("shp", [128, KC])
    kso = dout("kso", [2 * DSEQ, E]); vso = dout("vso", [2 * DSEQ, E])
    wkvs = dout("wkvs", [2, 32, 64, 64]); shs = dout("shs", [2, 128, KC])

    WINS = dscr("WINS", [NPAIR, 128, 4, KC, 128])
    KVWS = dscr("KVWS", [8, 128, KC, 512])
    BWS = dscr("BWS", [NHEAD, 128, 2, KC, 128])
    WOS = [dscr("WOSa", [2, 128, 16, 512]), dscr("WOSb", [2, 128, 16, 512])]
    GSC = dscr("GSC", [2, 3, 128, D], F32)
    KTSP = dscr("KTSP", [NHEAD, 128, SEQ]); VSP = dscr("VSP", [SEQ, E])
    KTSS = dscr("KTSS", [2, NHEAD, 128, PAST + DSEQ]); VSS = dscr("VSS", [2, PAST + DSEQ, E])
    bWINS, bKVWS, bBWS, bWOS, bGSC = Buf(), Buf(), Buf(), [Buf(), Buf()], Buf()
    bKT = [Buf(), Buf(), Buf()]
    bVS = [Buf(), Buf(), Buf()]
    bOUT = Buf("outs")

    es = ExitStack()
    with es:
        def T(name, shape, dt=F32):
            t = es.enter_context(nc.sbuf_tensor(name, list(shape), dt))
            return V(t[tuple(slice(None) for _ in shape)], [Buf(name)])

        PB = []
        PBb = []
        for i in range(8):
            t = es.enter_context(nc.psum_tensor(f"pb{i}", [128, 512], F32))
            PB.append(t)
            PBb.append([Buf(f"pb{i}", excl=True)] * 4)

        def ps(i, s0, ns=1):
            w = 128
            return V(PB[i][:, s0 * w:(s0 + ns) * w], PBb[i][s0:s0 + ns])

        def ps4(i, inner):
            return V(PB[i][:, :].rearrange("p (a b) -> p a b", b=inner), PBb[i][:])

        def rb(*vs):
            out = []
            for v in vs:
                if isinstance(v, V):
                    out.extend(v.bufs)
            return out

        def A(x):
            return x.ap if isinstance(x, V) else x

        def mm(out, lhsT, rhs, start=True, stop=True, nochk=False):
            P.op(PE, lambda e: e.matmul(out.ap, lhsT=lhsT.ap, rhs=rhs.ap, start=start, stop=stop, skip_group_check=nochk),
                 reads=rb(lhsT, rhs), writes=out.bufs)

        def tr(out, in_, ident):
            P.op(PE, lambda e: e.matmul(out.ap, lhsT=in_.ap, rhs=ident.ap, start=True, stop=True),
                 reads=rb(in_, ident), writes=out.bufs)

        def act(out, in_, func, bias=0.0, scale=1.0, eng=ACT):
            P.op(ACT, lambda e: e.activation(out=out.ap, in_=in_.ap, func=func, bias=A(bias), scale=A(scale)),
                 reads=rb(in_, bias, scale), writes=out.bufs)

        EH = {DVE: "vector", POOL: "gpsimd"}

        def cp(eng, out, in_):
            if eng == ACT:
                P.op(ACT, lambda e: e.activation(out=out.ap, in_=in_.ap, func=AF.Identity), reads=rb(in_), writes=out.bufs)
            else:
                P.op(eng, lambda e: e.tensor_copy(out=out.ap, in_=in_.ap), reads=rb(in_), writes=out.bufs)

        def tt(eng, out, in0, in1, op):
            P.op(eng, lambda e: e.tensor_tensor(out=out.ap, in0=in0.ap, in1=in1.ap, op=op),
                 reads=rb(in0, in1), writes=out.bufs)

        def ts(eng, out, in0, s1, op0, s2=None, op1=None):
            if op1 is None:
                P.op(eng, lambda e: e.tensor_scalar(out=out.ap, in0=in0.ap, scalar1=A(s1), scalar2=None, op0=op0),
                     reads=rb(in0, s1), writes=out.bufs)
            else:
                P.op(eng, lambda e: e.tensor_scalar(out=out.ap, in0=in0.ap, scalar1=A(s1), scalar2=A(s2), op0=op0, op1=op1),
                     reads=rb(in0, s1, s2), writes=out.bufs)

        def stt(eng, out, in0, scalar, in1, op0, op1):
            P.op(eng, lambda e: e.scalar_tensor_tensor(out=out.ap, in0=in0.ap, scalar=A(scalar), in1=in1.ap, op0=op0, op1=op1),
                 reads=rb(in0, scalar, in1), writes=out.bufs)

        def recip(out, in_):
            P.op(DVE, lambda e: e.reciprocal(out=out.ap, in_=in_.ap), reads=rb(in_), writes=out.bufs)

        def memset(eng, out, val):
            P.op(eng, lambda e: e.memset(out.ap, val), writes=out.bufs)

        def dma(out_ap, in_ap, reads=(), writes=()):
            P.dma(lambda e: e.dma_start(out=out_ap, in_=in_ap, allow_slow_non_contiguous=True), reads=list(reads), writes=list(writes))

        def load(dst, src_ap, rbufs=()):
            dma(dst.ap, src_ap, reads=rbufs, writes=dst.bufs)

        def store(dst_ap, src, wbufs=()):
            dma(dst_ap, src.ap, reads=src.bufs, writes=list(wbufs))

        CST = T("CST", [128, NCST]); PVt = T("PV", [128, NPV]); C3 = T("C3", [128, KC, 3]); HP = T("HP", [128, KC, 3])
        IDB = T("IDB", [128, 128], BF16); TRIB = T("TRIB", [128, 128], BF16); BO1B = T("BO1B", [128, 128], BF16)
        ONEB = T("ONEB", [128, 128], BF16); MSKB = T("MSKB", [128, 128], BF16)
        MOD = T("MOD", [128, 2, 3, 2, KC])
        GBC = [T("GBC0", [128, D]), T("GBC1", [128, D])]
        KGBC = T("KGBC", [128, 128])
        W1A1 = T("W1A1", [128, KC, 2, 64], BF16); W2A2 = T("W2A2", [64, 2, E], BF16)
        STG = T("STG", [128, 2048]); STB = T("STB", [128, 2048], BF16)
        XT = T("XT", [128, D]); XN = T("XN", [128, D])
        HT = T("HT", [128, KC, 129]); DXT = T("DXT", [128, KC, 128])
        MIX = [T(f"MIX{j}", [128, KC, 128], BF16) for j in range(6)]
        HWA = T("HWA", [64, 2, 128], BF16)
        WIN = [T(f"WIN{i}", [128, 4, KC, 128], BF16) for i in range(2)]
        TT = [T(f"T{i}", [128, 128]) for i in range(10)]
        KR = [T(f"KR{i}", [128, 2, 128], BF16) for i in range(2)]
        BTT = [T(f"BTT{i}", [128, 128], BF16) for i in range(2)]; KTT = [T(f"KTT{i}", [128, 128], BF16) for i in range(2)]
        T4B = T("T4B", [128, 128], BF16); T8B = T("T8B", [128, 128], BF16)
        VTOK = [T(f"VTOK{i}", [128, 128], BF16) for i in range(2)]; KTOK = [T(f"KTOK{i}", [128, 128], BF16) for i in range(2)]
        NBTOK = [T(f"NBTOK{i}", [128, 128], BF16) for i in range(2)]
        EGP = [T(f"EGP{i}", [128, 128]) for i in range(2)]; SZP = [T(f"SZP{i}", [128, 128]) for i in range(2)]
        BONV = [T(f"BONV{i}", [128, 128]) for i in range(2)]
        CY = T("CY", [128, 128]); CT2 = T("CT2", [128, 128]); CT5 = T("CT5", [128, 128])
        MALL = T("MALL", [128, 7, 2, 2, 128], BF16)
        ARBT = T("ARBT", [128, 2, 128], BF16); AAKT = T("AAKT", [128, 2, 128], BF16); ARKT = T("ARKT", [128, 2, 128], BF16)
        XB = [T("XB0", [128, 128], BF16), T("XB1", [128, 128], BF16)]
        YG = T("YG", [128, 16, 128], BF16)
        WOUTQ = [T(f"WOUTQ{i}", [128, 16, 256], BF16) for i in range(2)]
        KVW = [T(f"KVW{i}", [128, KC, 512], BF16) for i in range(2)]
        XKT = T("XKT", [128, KC, 128], BF16); H1T = T("H1T", [128, KC, 128], BF16)
        SQ = T("SQ", [128, 512]); KOUT = [T(f"KOUT{i}", [128, 512]) for i in range(2)]
        VOUT = [T(f"VOUT{i}", [128, 512]) for i in range(2)]
        VBt = [T(f"VB{i}", [128, 512], BF16) for i in range(2)]
        KTST = [T(f"KTST{i}", [128, 4, 128], BF16) for i in range(2)]
        SS = T("SS", [128, 8]); SSQ = T("SSQ", [128, 2])
        SF = T("SF", [128, NPAIR, 64]); SBD = T("SBD", [128, NPAIR, 128], BF16)
        KRM = [[T(f"KRM{p}{i}", [128, 128], BF16) for i in range(2)] for p in range(2)]
        BTM = [[T(f"BTM{p}{i}", [128, 128], BF16) for i in range(2)] for p in range(2)]
        KTM = [[T(f"KTM{p}{i}", [128, 128], BF16) for i in range(2)] for p in range(2)]
        BW = [T(f"BW{i}", [128, 2, KC, 128], BF16) for i in range(2)]
        QTP = [T(f"QTP{i}", [128, 128], BF16) for i in range(2)]
        SZQ = [T(f"SZQ{i}", [128, 128]) for i in range(2)]
        QE1 = T("QE1", [128, 128]); QE2 = T("QE2", [128, 128])
        NKMAX = SEQ // 128
        KTH = T("KTH", [128, SEQ], BF16); VH = T("VH", [128, NKMAX, 128], BF16)
        E1 = T("E1", [128, 4, 128]); SPB = [T("SPB0", [128, 4, 128], BF16), T("SPB1", [128, 4, 128], BF16)]
        ESt = T("ES", [128, 4, 128])
        AT = [T("AT0", [128, 4, 128], BF16), T("AT1", [128, 4, 128], BF16)]
        NONEB = T("NONEB", [128, 128], BF16)
        CB = T("CB", [128, 128])
        WST = T("WST", [64, 128])

        def pv(name, k=None, n=1):
            o = PVO[name] + (0 if k is None else k)
            return PVt[:, o:o + n]

        def cc(name, w=128):
            return CST[:, CO[name]:CO[name] + w]

        IDF = cc("ident")

        load(CST, cst); load(PVt, pvec); load(C3, c3T); load(HP, hprev)
        load(KGBC, rowv[2:3, 0:128].partition_broadcast(128))
        cp(DVE, IDB, IDF); cp(DVE, TRIB, cc("TriNeg")); cp(DVE, BO1B, cc("BO1")); cp(DVE, ONEB, cc("ones")); ts(DVE, NONEB, cc("ones"), -1.0, ALU.mult)
        cp(DVE, MSKB, CST[:, CO["MKcat"]:CO["MKcat"] + 128])
        ts(DVE, pv("nw0", 0, 16), pv("w0", 0, 16), -1.0, ALU.mult)
        ts(DVE, pv("na0", 0, 16), pv("a0", 0, 16), -1.0, ALU.mult)
        ts(DVE, pv("omka", 0, 16), pv("k_a", 0, 16), -1.0, ALU.mult, 1.0, ALU.add)
        memset(DVE, SF, 0.0); memset(POOL, SBD, 0.0)

        CBC = XN
        for layer, (adaw, normg, bname) in enumerate([(ada_a, "a_norm_g", "ada_b_a"), (ada_b, "b_norm_g", "ada_b_b")]):
            adav = adaw.rearrange("(k p) c -> p k c", p=128)
            stg3 = V(STG.ap.rearrange("p (k c) -> p k c", k=KC), STG.bufs)
            for cch in range(8):
                load(stg3, adav[:, :, cch * 256:(cch + 1) * 256])
                for bi in range(2):
                    blk = cch * 2 + bi
                    o = ps(7, 0)[:, blk * 3:(blk + 1) * 3]
                    for kc in range(KC):
                        mm(o, stg3[:, kc, bi * 128:(bi + 1) * 128], C3[:, kc, :], start=(kc == 0), stop=(kc == KC - 1))
            adp = V(ps(7, 0).ap[:, 0:48].rearrange("p (b s) -> p b s", s=3), ps(7, 0).bufs)
            for s in range(3):
                tt(DVE, MOD[:, layer, s, 1, :], adp[:, 0:8, s], pv(bname, 0, 8), ALU.add)
                tt(DVE, MOD[:, layer, s, 0, :], adp[:, 8:16, s], pv(bname, 8, 8), ALU.add)
                ts(DVE, MOD[:, layer, s, 0, :], MOD[:, layer, s, 0, :], 1.0, ALU.add)
                tt(DVE, MOD[:, layer, s, 0, :], MOD[:, layer, s, 0, :], pv(normg, 0, 8), ALU.mult)
            load(GBC[1], rowv[layer:layer + 1, :].partition_broadcast(128))
            cbc3 = V(CBC.ap.rearrange("p (k c) -> p k c", k=KC), CBC.bufs)
            for s in range(3):
                for kc in range(KC):
                    ts(DVE, cbc3[:, kc, :], cc("ones"), C3[:, kc, s:s + 1], ALU.mult)
                for cch in range(4):
                    load(stg3, adav[:, :, 2048 + cch * 256:2048 + (cch + 1) * 256])
                    o = ps(cch % 2, 0, 2)
                    for kc in range(KC):
                        mm(o, cbc3[:, kc, :], stg3[:, kc, :], start=(kc == 0), stop=(kc == KC - 1))
                    tt(DVE, GBC[0][:, cch * 256:(cch + 1) * 256], o, GBC[1][:, cch * 256:(cch + 1) * 256], ALU.add)
                store(GSC[layer, s], GBC[0], [bGSC])

        cvt_i = [0]

        def convert(src_ap, dst_ap, wb, ncol=2048, shape3=None):
            load(STG[:, 0:ncol], src_ap)
            eng = [DVE, ACT, POOL][cvt_i[0] % 3]
            cvt_i[0] += 1
            cp(eng, STB[:, 0:ncol], STG[:, 0:ncol])
            src = STB[:, 0:ncol]
            if shape3 is not None:
                src = V(src.ap.rearrange("p (a c) -> p a c", a=shape3), src.bufs)
            store(dst_ap, src, [wb])

        for kc in range(KC):
            rows = slice(kc * 128, (kc + 1) * 128)
            for j in range(4):
                convert(w_in[rows, j * E:(j + 1) * E], WINS.rearrange("a p j k c -> p a j k c")[:, :, j, kc, :], bWINS, shape3=16)
            for half in range(2):
                convert(kvw[rows, half * E:(half + 1) * E],
                        KVWS.rearrange("g p k c -> p g k c")[:, half * 4:(half + 1) * 4, kc, :], bKVWS, shape3=4)
            for qz in range(2):
                convert(bwin[rows, qz * E:(qz + 1) * E], BWS.rearrange("h p q k c -> p h q k c")[:, :, qz, kc, :], bBWS, shape3=16)
        for li, wo in enumerate([wout_a, wout_b]):
            for kc in range(16):
                convert(wo[kc * 128:(kc + 1) * 128, :], WOS[li].rearrange("h p k c -> p h k c")[:, :, kc, :], bWOS[li],
                        ncol=1024, shape3=2)
        for i, wsrc in enumerate([w1, a1]):
            load(V(STG.ap[:, 0:512].rearrange("p (k c) -> p k c", k=KC), STG.bufs), wsrc.rearrange("(k p) c -> p k c", p=128))
            cp(DVE, W1A1[:, :, i, :], V(STG.ap[:, 0:512].rearrange("p (k c) -> p k c", k=KC), STG.bufs))
        for i, wsrc in enumerate([w2, a2]):
            load(STG[0:64, :], wsrc)
            cp(DVE, W2A2[:, i, :], STG[0:64, :])

        if do_samples:
            for b in range(2):
                for blk in range(PAST // 128):
                    r0 = b * PAST + blk * 128
                    load(STG, cv[r0:r0 + 128, :])
                    cp([DVE, POOL][blk % 2], STB, STG)
                    store(VSS[b, blk * 128:(blk + 1) * 128, :], STB, [bVS[1 + b]])
                for h in range(NHEAD):
                    stg3 = V(STG.ap[:, 0:1024].rearrange("p (n c) -> p n c", n=8), STG.bufs)
                    load(stg3, ck[b * PAST:(b + 1) * PAST, h * 128:(h + 1) * 128].rearrange("(n p) c -> p n c", p=128))
                    for half in range(2):
                        o = ps(half, 0, 4)
                        for q in range(4):
                            tr(o[:, q * 128:(q + 1) * 128], stg3[:, half * 4 + q, :], IDF)
                        cp([ACT, DVE][half], STB[:, half * 512:(half + 1) * 512], o)
                    store(KTSS[b, h, :, 0:PAST], STB[:, 0:1024], [bKT[1 + b]])

        def gelem(i):
            return [ACT, DVE][i % 2]

        def token_tile(seq, t0, n, x_src, y_dst, k_dst, v_dst, KTS_s, VS_s, koff, first, last, wkv_dst, sh_dst, swkv_src):
            bKTs, bVSs = bKT[seq], bVS[seq]
            L = int(np.log2(n))
            load(XT[:n, :], x_src)
            if first:
                load(GBC[0], GSC[0, seq], [bGSC]); load(GBC[1], GSC[1, seq], [bGSC])
                cp(DVE, HT[:, :, 0], HP[:, :, seq])
                if swkv_src is None:
                    memset(DVE, SF, 0.0); memset(POOL, SBD, 0.0)
                else:
                    for g4 in range(4):
                        sg = V(STG.ap[0:64, :].rearrange("p (h k) -> p h k", h=32), STG.bufs)
                        if g4 == 0:
                            load(sg, swkv_src.rearrange("h v k -> v h k"))
                        o = ps(7, 0, 2)
                        for q in range(4):
                            pr = g4 * 4 + q
                            tr(o[:, q * 64:(q + 1) * 64], V(STG.ap[0:64, pr * 128:(pr + 1) * 128], STG.bufs), IDF[0:64, 0:64])
                        o3 = V(o.ap.rearrange("p (a v) -> p a v", v=64), o.bufs)
                        cp(DVE, SF[:, g4 * 4:(g4 + 1) * 4, :], o3)
                        if g4 == 0:
                            memset(POOL, SBD, 0.0)
                        cp(DVE, SBD[0:64, g4 * 4:(g4 + 1) * 4, 0:64], o3[0:64])
                        cp(DVE, SBD[64:128, g4 * 4:(g4 + 1) * 4, 64:128], o3[64:128])

            def rms_to_T(dsts):
                act(XN[:n, :], XT[:n, :], AF.Square)
                P.op(DVE, lambda e: e.reduce_sum(out=SSQ.ap[:n, 0:1], in_=XN.ap[:n, :], axis=AX.X), reads=XN.bufs, writes=SSQ.bufs)
                act(SSQ[:n, 1:2], SSQ[:n, 0:1], AF.Ln, bias=pv("eps_rms")[:n], scale=1.0 / D)
                act(SSQ[:n, 1:2], SSQ[:n, 1:2], AF.Exp, scale=-0.5)
                ts(DVE, XN[:n, :], XT[:n, :], SSQ[:n, 1:2], ALU.mult)
                for half in range(2):
                    bank = [7, 5][half]
                    for q in range(4):
                        kc = half * 4 + q
                        tr(ps(bank, q)[:, :n], XN[:n, kc * 128:(kc + 1) * 128], IDF[:n, :n])
                    for q in range(4):
                        kc = half * 4 + q
                        for (dst, sc, bi) in dsts:
                            if bi is None:
                                act(dst(kc), ps(bank, q)[:, :n], AF.Identity, scale=sc(kc))
                            else:
                                act(dst(kc), ps(bank, q)[:, :n], AF.Identity, bias=bi(kc), scale=sc(kc))

            rms_to_T([(lambda kc: HT[:, kc, 1:1 + n], lambda kc: MOD[:, 0, seq, 0, kc:kc + 1], lambda kc: MOD[:, 0, seq, 1, kc:kc + 1])])
            tt(DVE, DXT[:, :, :n], HT[:, :, 0:n], HT[:, :, 1:1 + n], ALU.subtract)
            for j, mun in enumerate(["mu_r", "mu_k", "mu_v", "mu_z", "mu_w", "mu_a"]):
                for kc in range(KC):
                    stt(DVE, MIX[j][:, kc, :n], DXT[:, kc, :n], pv(mun, kc), HT[:, kc, 1:1 + n], ALU.mult, ALU.add)
            if last and sh_dst is not None:
                store(sh_dst, HT[:, :, n], [bOUT])
            cp(DVE, HT[:, :, 0], HT[:, :, n])
            for i in range(2):
                o = ps(1, i)[0:64, :n]
                for kc in range(KC):
                    mm(o, W1A1[:, kc, i, :], MIX[4 + i][:, kc, :n], start=(kc == 0), stop=(kc == KC - 1))
            act(WST[:, :n], ps(1, 0)[0:64, :n], AF.Exp, scale=2.0)
            ts(DVE, WST[:, :n], WST[:, :n], 1.0, ALU.add)
            recip(WST[:, :n], WST[:, :n])
            ts(DVE, HWA[:, 0, :n], WST[:, :n], -2.0, ALU.mult, 1.0, ALU.add)
            cp(ACT, HWA[:, 1, :n], ps(1, 1)[0:64, :n])

            def wout_load(li, q):
                load(WOUTQ[q % 2], WOS[li][q // 2][:, :, (q % 2) * 256:(q % 2 + 1) * 256], [bWOS[li]])

            def genA(pr, par):
                Wt = WIN[pr % 2]
                if pr == 0:
                    load(Wt, WINS[pr], [bWINS])
                if pr + 1 < NPAIR:
                    load(WIN[(pr + 1) % 2], WINS[pr + 1], [bWINS])
                pc = slice(pr * 128, (pr + 1) * 128)
                for j in range(4):
                    for kc in range(KC):
                        mm(ps(0, j)[:, :n], Wt[:, j, kc, :], MIX[j][:, kc, :n], start=(kc == 0), stop=(kc == KC - 1))
                    yield
                for i in range(2):
                    mm(ps(1, i)[:, :n], W2A2[:, i, pc], HWA[:, i, :n])
                PRr, PRk, PRv, PRz = (ps(0, j)[:, :n] for j in range(4))
                T1, T2, T3, T5, T6, T7, T9, G, EGI, EGM = (t[:, :n] for t in TT[:10])
                EG = EGP[par][:, :n]
                TZ = SZP[par][:, :n]
                KRp, BTTp, KTTp = KR[par], BTT[par], KTT[par]
                pe = lambda nm: pv(nm, pr)
                act(T1, ps(1, 0)[:, :n], AF.Exp, bias=pe("nw0"), scale=-1.0)
                act(T2, ps(1, 1)[:, :n], AF.Exp, bias=pe("na0"), scale=-1.0)
                yield
                ts(DVE, T1, T1, 1.0, ALU.add); recip(T1, T1)
                ts(POOL, T2, T2, 1.0, ALU.add); recip(T2, T2)
                yield
                P.op(DVE, lambda e, G=G, T1=T1: e.tensor_tensor_scan(out=G.ap, data0=cc("ones")[:, :n].ap, data1=T1.ap, initial=0.0,
                                                                      op0=ALU.mult, op1=ALU.add), reads=rb(T1, CST), writes=G.bufs)
                act(EG, G, AF.Exp, scale=-DECAY_C)
                act(EGI, G, AF.Exp, scale=DECAY_C)
                tt(POOL, EGM, G, T1, ALU.subtract)
                act(EGM, EGM, AF.Exp, scale=-DECAY_C)
                yield
                ts(DVE, T3, PRk, pe("k_k"), ALU.mult)
                act(T4B[:, :n], T3, AF.Square)
                mm(ps(1, 2)[:, :n], BO1B, T4B[:, :n])
                yield
                act(T5, ps(1, 2)[:, :n], AF.Ln, bias=pv("eps_l2"))
                act(T5, T5, AF.Exp, scale=-0.5)
                tt(DVE, T3, T3, T5, ALU.mult)
                yield
                ts(POOL, T6, T2, pe("k_a"), ALU.mult, pe("omka"), ALU.add)
                tt(DVE, T6, PRk, T6, ALU.mult)
                tt(POOL, T7, T3, T2, ALU.mult)
                yield
                tt(DVE, KRp[:, 0, :n], T3, EGM, ALU.mult)
                tt(DVE, KRp[:, 1, :n], PRr, EG, ALU.mult)
                tt(POOL, BTTp[:, :n], T7, EGI, ALU.mult)
                tt(POOL, KTTp[:, :n], T6, EGI, ALU.mult)
                yield
                for hh in range(2):
                    hm = pv("hm%d" % hh)
                    ts(POOL, KRM[par][hh][:, :n], KRp[:, 0, :n], hm, ALU.mult)
                    ts(POOL, BTM[par][hh][:, :n], BTTp[:, :n], hm, ALU.mult)
                    ts(POOL, KTM[par][hh][:, :n], KTTp[:, :n], hm, ALU.mult)
                    yield
                stt(DVE, T8B[:, :n], PRr, pe("r_k"), T6, ALU.mult, ALU.mult)
                mm(ps(1, 3)[:, :n], BO1B, T8B[:, :n])
                cp(ACT, T9, PRv)
                yield
                tr(ps(6, 2)[:n, :], T9, IDF)
                tr(ps(6, 0)[:n, 0:128], KTTp[:, :n], IDB)
                tr(ps(6, 1)[:n, 0:128], BTTp[:, :n], IDB)
                tt(DVE, BONV[par][:, :n], ps(1, 3)[:, :n], T9, ALU.mult)
                yield
                cp(ACT, VTOK[par][:n, :], ps(6, 2)[:n, :])
                cp(DVE, KTOK[par][:n, :], ps(6, 0)[:n, 0:128])
                act(NBTOK[par][:n, :], ps(6, 1)[:n, 0:128], AF.Identity, scale=-1.0)
                yield
                act(TZ, PRz, AF.Exp, scale=-1.0)
                ts(DVE, TZ, TZ, 1.0, ALU.add); recip(TZ, TZ)
                tt(DVE, TZ, PRz, TZ, ALU.mult)
                yield

            def genB(pr, par):
                KRp, BTTp, KTTp = KR[par], BTT[par], KTT[par]
                VT, KT, NBT = VTOK[par], KTOK[par], NBTOK[par]
                pe = lambda nm: pv(nm, pr)
                psN = V(ps(4, 0, 2).ap.rearrange("p (h c) -> p h c", h=2), ps(4, 0, 2).bufs)
                psB = V(PB[2][:, :].rearrange("p (h s c) -> p h s c", h=2, s=2), PBb[2][:])
                psK = V(PB[3][:, :].rearrange("p (h s c) -> p h s c", h=2, s=2), PBb[3][:])
                for hh in range(2):
                    mm(psN[:n, hh, :n], KRM[par][hh][:, :n], BTTp[:, :n])
                    for s2 in range(2):
                        mm(psB[:n, hh, s2, :n], BTM[par][hh][:, :n], KRp[:, s2, :n])
                        mm(psK[:n, hh, s2, :n], KTM[par][hh][:, :n], KRp[:, s2, :n])
                bc = lambda name, off: V(CST.ap[:n, CO[name] + off:CO[name] + off + n].unsqueeze(1).to_broadcast([n, 2, n]), CST.bufs)
                tt(DVE, MALL[:n, 0, 0, :, :n], psN[:n, :, :n], bc("ML", 0), ALU.mult)
                tt(DVE, MALL[:n, 0, 1, :, :n], psB[:n, :, 0, :n], bc("MUcat", 0), ALU.mult)
                yield
                tt(DVE, ARBT[:n, :, :n], psB[:n, :, 1, :n], bc("MUcat", 128), ALU.mult)
                tt(DVE, AAKT[:n, :, :n], psK[:n, :, 0, :n], bc("MKcat", 0), ALU.mult)
                tt(DVE, ARKT[:n, :, :n], psK[:n, :, 1, :n], bc("MKcat", 128), ALU.mult)
                psX = ps(4, 2)
                for hh in range(2):
                    hs = slice(hh * 64, hh * 64 + 64)
                    mm(psX[:n, hs], KRp[:, 0, :n], SBD[:, pr, hs], start=(hh == 0), stop=False, nochk=True)
                    mm(psX[:n, hs], AAKT[:n, hh, :n], VT[:n, hs], start=False, stop=False, nochk=True)
                cp(ACT, XB[0][:n, :], psX[:n, :])
                yield
                psM = V(PB[5][:, :].rearrange("p (s h c) -> p s h c", s=2, h=2), PBb[5][:])
                for k in range(L):
                    for hh in range(2):
                        hs = slice(hh * 64, hh * 64 + 64)
                        mm(psX[:n, hs], MALL[:n, k, 1, hh, :n], XB[k % 2][:n, hs], start=False, stop=(k == L - 1 and hh == 1), nochk=True)
                    if k < L - 1:
                        lastk = (k == L - 2)
                        for hh in range(2):
                            if not lastk:
                                mm(psM[:n, 0, hh, :n], MALL[:n, k, 1, hh, :n], MALL[:n, k, 0, hh, :n])
                            mm(psM[:n, 1, hh, :n], MALL[:n, k, 0, hh, :n], MALL[:n, k, 1, hh, :n])
                        if not lastk:
                            cp(DVE, MALL[:n, k + 1, :, :, :n], psM[:n, :, :, :n])
                        else:
                            cp(DVE, MALL[:n, k + 1, 1, :, :n], psM[:n, 1, :, :n])
                    cp(ACT, XB[(k + 1) % 2][:n, :], psX[:n, :])
                    yield
                U = XB[L % 2]
                psY = ps(4, 3)
                mm(psY[:, :n], SBD[:, pr, :], KRp[:, 1, :n], start=True, stop=False, nochk=True)
                for hh in range(2):
                    hs = slice(hh * 64, hh * 64 + 64)
                    mm(psY[hs, :n], VT[:n, hs], ARKT[:n, hh, :n], start=False, stop=False, nochk=True)
                    mm(psY[hs, :n], U[:n, hs], ARBT[:n, hh, :n], start=False, stop=True, nochk=True)
                psS = ps(7, 1)
                for hh in range(2):
                    hs = slice(hh * 64, hh * 64 + 64)
                    mm(psS[hs, 0:64], KT[:n, hs], VT[:n, hs], start=True, stop=False)
                    mm(psS[hs, 0:64], NBT[:n, hs], U[:n, hs], start=False, stop=True)
                Y = CY[:, :n]
                C2 = CT2[:, :n]
                C5 = CT5[:, :n]
                cp(ACT, Y, psY[:, :n])
                yield
                tt(DVE, SF[:, pr, :], psS[:, 0:64], SF[:, pr, :], ALU.add)
                ts(DVE, SF[:, pr, :], SF[:, pr, :], EGP[par][:, n - 1:n], ALU.mult)
                cp(ACT, SBD[0:64, pr, 0:64], SF[0:64, pr, :])
                cp(ACT, SBD[64:128, pr, 64:128], SF[64:128, pr, :])
                mm(ps(7, 2)[:, :n], cc("BO64"), Y)
                yield
                tt(DVE, Y, Y, ps(7, 2)[:, :n], ALU.subtract)
                act(C2, Y, AF.Square)
                mm(ps(7, 3)[:, :n], cc("BO64"), C2)
                yield
                act(C5, ps(7, 3)[:, :n], AF.Ln, bias=pv("eps_gn"))
                act(C5, C5, AF.Exp, scale=-0.5)
                tt(DVE, Y, Y, C5, ALU.mult)
                yield
                ts(DVE, Y, Y, pe("ln_g"), ALU.mult, pe("ln_b"), ALU.add)
                tt(POOL, Y, Y, BONV[par][:, :n], ALU.add)
                tt(DVE, YG[:, pr, :n], Y, SZP[par][:, :n], ALU.mult)
                yield

            def drain(g):
                for _ in g:
                    pass

            def interleave(ga, gb):
                a_done = ga is None
                b_done = False
                while not (a_done and b_done):
                    if not b_done:
                        try:
                            next(gb)
                        except StopIteration:
                            b_done = True
                    if not a_done:
                        try:
                            next(ga)
                        except StopIteration:
                            a_done = True

            wout_load(0, 0)
            npairs = 0 if 'pair' in os.environ.get('KSKIP', '') else NPAIR
            if npairs:
                drain(genA(0, 0))
            for pr in range(npairs):
                interleave(genA(pr + 1, (pr + 1) % 2) if pr + 1 < NPAIR else None, genB(pr, pr % 2))

            if last and wkv_dst is not None:
                for g4 in range(4):
                    o = ps(7, 0, 4)
                    for q in range(4):
                        pr = g4 * 4 + q
                        tr(o[0:64, q * 128:(q + 1) * 128], SF[:, pr, :], IDF)
                    cp(ACT, STG[0:64, g4 * 512:(g4 + 1) * 512], o[0:64, :])
                store(wkv_dst.rearrange("h v k -> v h k"), V(STG.ap[0:64, :].rearrange("p (h k) -> p h k", h=32), STG.bufs), [bOUT])

            def out_proj(li, src):
                for q in range(4):
                    if q + 1 < 4:
                        wout_load(li, q + 1)
                    Wq = WOUTQ[q % 2]
                    o = ps(q % 2, 0, 2)
                    for pr in range(16):
                        mm(o[:n, :], src[:, pr, :n], Wq[:, pr, :], start=(pr == 0), stop=(pr == 15))
                    hc = slice(q * 256, (q + 1) * 256)
                    tt(DVE, XN[:n, hc], o[:n, :], GBC[li][:n, hc], ALU.mult)
                    tt(POOL, XT[:n, hc], XT[:n, hc], XN[:n, hc], ALU.add)

            out_proj(0, YG)

            rms_to_T([(lambda kc: XKT[:, kc, :n], lambda kc: pv("kv_norm_g", kc), None),
                      (lambda kc: H1T[:, kc, :n], lambda kc: MOD[:, 1, seq, 0, kc:kc + 1], lambda kc: MOD[:, 1, seq, 1, kc:kc + 1])])
            load(KVW[0], KVWS[0], [bKVWS])
            for cg in range(8):
                Wk = KVW[cg % 2]
                if cg + 1 < 8:
                    load(KVW[(cg + 1) % 2], KVWS[cg + 1], [bKVWS])
                o = ps(2 + cg % 2, 0, 4)
                for kc in range(KC):
                    mm(o[:n, :], XKT[:, kc, :n], Wk[:, kc, :], start=(kc == 0), stop=(kc == KC - 1))
                cs = slice((cg % 4) * 512, (cg % 4 + 1) * 512)
                if cg < 4:
                    ko = KOUT[cg % 2]
                    act(SQ[:n, :], o[:n, :], AF.Square)
                    P.op(DVE, lambda e, cg=cg: e.reduce_sum(out=SS.ap[:n, 0:4], in_=SQ.ap[:n, :].rearrange("p (h c) -> p h c", h=4), axis=AX.X),
                         reads=SQ.bufs, writes=SS.bufs)
                    act(SS[:n, 4:8], SS[:n, 0:4], AF.Ln, bias=pv("eps_rms")[:n], scale=1.0 / 128)
                    act(SS[:n, 4:8], SS[:n, 4:8], AF.Exp, scale=-0.5)
                    o3 = V(o.ap[:n, :].rearrange("p (h c) -> p h c", h=4), o.bufs)
                    ko3 = V(ko.ap[:n, :].rearrange("p (h c) -> p h c", h=4), ko.bufs)
                    tt(DVE, ko3, o3, V(SS.ap[:n, 4:8].unsqueeze(2).to_broadcast([n, 4, 128]), SS.bufs), ALU.mult)
                    tt(POOL, ko3, ko3, V(KGBC.ap[:n, :].unsqueeze(1).to_broadcast([n, 4, 128]), KGBC.bufs), ALU.mult)
                    store(k_dst[:, cs], ko[:n, :], [bOUT])
                    kst = KTST[cg % 2]
                    ob = ps(7, 0, 4)
                    for q in range(4):
                        tr(ob[:, q * 128:q * 128 + n], ko[:n, q * 128:(q + 1) * 128], IDF[:n, :n])
                    cp(ACT, kst[:, :, :n], V(ob.ap.rearrange("p (h c) -> p h c", h=4)[:, :, :n], ob.bufs))
                    store(KTS_s[cg * 4:(cg + 1) * 4, :, koff + t0:koff + t0 + n].rearrange("h p c -> p h c"), kst[:, :, :n], [bKTs])
                else:
                    vo = VOUT[cg % 2]
                    cp(ACT, vo[:n, :], o[:n, :])
                    cp(DVE, VBt[cg % 2][:n, :], o[:n, :])
                    store(v_dst[:, cs], vo[:n, :], [bOUT])
                    store(VS_s[koff + t0:koff + t0 + n, cs], VBt[cg % 2][:n, :], [bVSs])

            nk_total = koff + t0 + n
            nkb = (nk_total + 127) // 128
            OG = YG
            wout_load(1, 0)
            def genQ(h, par):
                Bw = BW[h % 2]
                if h == 0:
                    load(Bw, BWS[h], [bBWS])
                if h + 1 < NHEAD:
                    load(BW[(h + 1) % 2], BWS[h + 1], [bBWS])
                psQ, psZg, psSS = ps(par, 0), ps(par, 1), ps(par, 2)
                for kc in range(KC):
                    mm(psQ[:, :n], Bw[:, 0, kc, :], H1T[:, kc, :n], start=(kc == 0), stop=(kc == KC - 1))
                yield
                for kc in range(KC):
                    mm(psZg[:, :n], Bw[:, 1, kc, :], H1T[:, kc, :n], start=(kc == 0), stop=(kc == KC - 1))
                q1, q2 = QE1[:, :n], QE2[:, :n]
                act(q1, psQ[:, :n], AF.Square)
                yield
                mm(psSS[:, :n], cc("ones"), q1)
                act(q2, psSS[:, :n], AF.Ln, bias=pv("eps_rms"), scale=1.0 / 128)
                yield
                act(q2, q2, AF.Exp, scale=-0.5)
                ts(DVE, q2, q2, pv("q_gain"), ALU.mult, 128 ** -0.5, ALU.mult)
                yield
                tt(DVE, QTP[par][:, :n], psQ[:, :n], q2, ALU.mult)
                sz = SZQ[par][:, :n]
                act(sz, psZg[:, :n], AF.Exp, scale=-1.0)
                yield
                ts(DVE, sz, sz, 1.0, ALU.add); recip(sz, sz)
                tt(DVE, sz, psZg[:, :n], sz, ALU.mult)
                yield

            def genAtt(h, par):
                QT = QTP[par]
                psO = ps(4, 0)
                load(KTH[:, 0:nk_total], KTS_s[h, :, 0:nk_total], [bKTs])
                nfull = nk_total // 128
                if nfull > 0:
                    load(VH[:, 0:nfull, :], VS_s[0:nfull * 128, h * 128:(h + 1) * 128].rearrange("(b p) c -> p b c", p=128), [bVSs])
                rem = nk_total - nfull * 128
                if rem > 0:
                    load(VH[0:rem, nfull, :], VS_s[nfull * 128:nk_total, h * 128:(h + 1) * 128], [bVSs])
                memset(POOL, CB[:, :n], 0.0)
                blocks = list(range(nkb - 1, -1, -1))
                groups = [[blocks[0]]] + [blocks[i:i + 4] for i in range(1, len(blocks), 4)]
                ng = len(groups)
                PZB = [5, 6, 7]

                def pzv(g):
                    return V(PB[PZB[g % 3]][:, :].rearrange("p (a c) -> p a c", a=4), PBb[PZB[g % 3]][:])

                def S1(g):
                    pz = pzv(g)
                    for i, kb in enumerate(groups[g]):
                        ks = min(128, nk_total - kb * 128)
                        mm(pz[:ks, i, :n], KTH[:, kb * 128:kb * 128 + ks], QT[:, :n], start=(i == 0), stop=False, nochk=True)

                def S2(g):
                    pz = pzv(g)
                    G = len(groups[g])
                    kb0 = groups[g][0]
                    ks = min(128, nk_total - kb0 * 128)
                    sp = SPB[g % 2]
                    act(E1[:ks, 0:G, :n], pz[:ks, 0:G, :n], AF.Exp)
                    act(sp[:ks, 0:G, :n], E1[:ks, 0:G, :n], AF.Ln, bias=pv("one")[:ks])
                    if g == 0:
                        tt(POOL, sp[:ks, 0, :n], sp[:ks, 0, :n], MSKB[:ks, :n], ALU.mult)

                def S3(g):
                    pz = pzv(g)
                    G = len(groups[g])
                    kb0 = groups[g][0]
                    ks = min(128, nk_total - kb0 * 128)
                    sp = SPB[g % 2]
                    at = AT[g % 2]
                    for i in range(G):
                        mm(pz[:ks, i, :n], TRIB[:ks, :ks], sp[:ks, i, :n], start=False, stop=False, nochk=True)
                        for i2 in range(i):
                            mm(pz[:ks, i, :n], NONEB[:ks, :ks], sp[:ks, i2, :n], start=False, stop=False, nochk=True)
                    lastg = (g == ng - 1)
                    if not lastg:
                        pcb = ps(2 + g % 2, 0)
                        for i in range(G):
                            mm(pcb[:, :n], ONEB[:ks, :], sp[:ks, i, :n], start=(i == 0), stop=(i == G - 1))
                    cbb = V(CB.ap[:ks, :n].unsqueeze(1).to_broadcast([ks, G, n]), CB.bufs)
                    tt(DVE, ESt[:ks, 0:G, :n], pz[:ks, 0:G, :n], cbb, ALU.subtract)
                    act(at[:ks, 0:G, :n], ESt[:ks, 0:G, :n], AF.Exp)
                    if g == 0:
                        tt(POOL, at[:ks, 0, :n], at[:ks, 0, :n], MSKB[:ks, :n], ALU.mult)
                    if not lastg:
                        tt(DVE, CB[:, :n], CB[:, :n], pcb[:, :n], ALU.add)
                    for i, kb in enumerate(groups[g]):
                        mm(psO[:, :n], VH[:ks, kb, :], at[:ks, i, :n], start=(g == 0 and i == 0), stop=(lastg and i == G - 1), nochk=True)

                S1(0)
                if ng > 1:
                    S1(1)
                S2(0)
                yield
                for g in range(ng):
                    if g + 2 < ng:
                        S1(g + 2)
                    if g + 1 < ng:
                        S2(g + 1)
                    S3(g)
                    yield
                tt(DVE, OG[:, h, :n], psO[:, :n], SZQ[par][:, :n], ALU.mult)
                yield

            nheads = 0 if 'att' in os.environ.get('KSKIP', '') else NHEAD
            if nheads:
                drain(genQ(0, 0))
            for h in range(nheads):
                interleave(genQ(h + 1, (h + 1) % 2) if h + 1 < NHEAD else None, genAtt(h, h % 2))
            out_proj(1, OG)
            store(y_dst, XT[:n, :], [bOUT])

        if do_samples:
            for b in range(2):
                r = slice(b * DSEQ, (b + 1) * DSEQ)
                token_tile(1 + b, 0, DSEQ, xs[r, :], ys[r, :], kso[r, :], vso[r, :], KTSS[b], VSS[b], PAST, True, True,
                           wkvs[b], shs[b], swkv[b])
        for ti in range(n_prompt_tiles):
            r = slice(ti * 128, (ti + 1) * 128)
            token_tile(0, ti * 128, 128, xp[r, :], yp[r, :], kp[r, :], vp[r, :], KTSP, VSP, 0, ti == 0, ti == n_prompt_tiles - 1,
                       wkvp, shp, None)

        P.final_all()
        nse = [max(1, (c + EPOCH - 1) // EPOCH) for c in P.cnt]
        esems = [[es.enter_context(nc.semaphore(f"s{ENG_NAMES[i]}{k}")) for k in range(nse[i])] for i in range(5)]
        dsems = [es.enter_context(nc.semaphore(f"d{k}")) for k in range(NDMA)]
        block = es.enter_context(nc.Block())
        P.emit(block, esems, dsems)
    return nc, P


_CACHE = {}


def make_in_maps(inp, n_cores=8):
    f = lambda a: np.ascontiguousarray(np.asarray(a, np.float32))
    cst = make_consts()
    pvec = make_pvec(inp)
    rowv = np.zeros((3, D), np.float32)
    rowv[0] = np.asarray(inp["a_ada_b"][0][2 * D:3 * D], np.float32)
    rowv[1] = np.asarray(inp["b_ada_b"][0][2 * D:3 * D], np.float32)
    rowv[2, :128] = np.asarray(inp["k_gain"], np.float32)
    shared = {
        "pvec": pvec, "cst": cst, "rowv": rowv,
        "ada_a": f(inp["a_ada_w"][0]), "ada_b": f(inp["b_ada_w"][0]),
        "w_in": f(inp["a_w_in"][0]), "w1": f(inp["a_w1"][0]), "a1": f(inp["a_a1"][0]),
        "w2": f(inp["a_w2"][0]), "a2": f(inp["a_a2"][0]), "wout_a": f(inp["a_w_out"][0]),
        "kvw": f(inp["kv_w"]), "bwin": f(inp["b_w_in"][0]), "wout_b": f(inp["b_w_out"][0]),
    }
    maps = []
    for i in range(n_cores):
        c3 = np.stack([np.asarray(inp["c_prompt"][i], np.float32),
                       np.asarray(inp["c_sample"][2 * i], np.float32),
                       np.asarray(inp["c_sample"][2 * i + 1], np.float32)], axis=0)
        c3T = np.ascontiguousarray(c3.reshape(3, KC, 128).transpose(2, 1, 0))
        hp = np.zeros((3, D), np.float32)
        hp[1] = inp["state_shift"][0, 2 * i]
        hp[2] = inp["state_shift"][0, 2 * i + 1]
        hpT = np.ascontiguousarray(hp.reshape(3, KC, 128).transpose(2, 1, 0))
        m = dict(shared)
        m.update({
            "xp": f(inp["x_prompt"][i]),
            "xs": f(np.asarray(inp["x_sample"][2 * i:2 * i + 2]).reshape(2 * DSEQ, D)),
            "ck": f(np.asarray(inp["cache_k"][2 * i:2 * i + 2]).reshape(2 * PAST, E)),
            "cv": f(np.asarray(inp["cache_v"][2 * i:2 * i + 2]).reshape(2 * PAST, E)),
            "swkv": f(inp["state_wkv"][0, 2 * i:2 * i + 2]),
            "hprev": hpT, "c3T": c3T,
        })
        maps.append(m)
    return maps


def kernel(**inputs):
    n = 8
    if "nc" not in _CACHE:
        _CACHE["nc"] = build_program()[0]
    nc = _CACHE["nc"]
    maps = make_in_maps(inputs, n)
    res = run_bass_kernel_spmd(nc, maps, core_ids=list(range(n)))
    R = res.results
    B, BD = 8, 16
    y_prompt = np.stack([R[i]["yp"] for i in range(n)], 0).astype(np.float32)
    y_sample = np.concatenate([R[i]["ys"].reshape(2, DSEQ, D) for i in range(n)], 0).astype(np.float32)
    k_prompt = np.stack([R[i]["kp"].reshape(SEQ, 16, 128) for i in range(n)], 0).astype(np.float32)
    v_prompt = np.stack([R[i]["vp"].reshape(SEQ, 16, 128) for i in range(n)], 0).astype(np.float32)
    wkv_prompt = np.stack([R[i]["wkvp"] for i in range(n)], 0)[None].astype(np.float32)
    shift_prompt = np.stack([R[i]["shp"].T.reshape(D) for i in range(n)], 0)[None].astype(np.float32)
    k_sample = np.concatenate([R[i]["kso"].reshape(2, DSEQ, 16, 128) for i in range(n)], 0).astype(np.float32)
    v_sample = np.concatenate([R[i]["vso"].reshape(2, DSEQ, 16, 128) for i in range(n)], 0).astype(np.float32)
    wkv_sample = np.concatenate([R[i]["wkvs"] for i in range(n)], 0)[None].astype(np.float32)
    shift_sample = np.concatenate([np.stack([R[i]["shs"][b].T.reshape(D) for b in range(2)], 0) for i in range(n)], 0)[None].astype(np.float32)
    return (y_prompt, y_sample, k_prompt, v_prompt, wkv_prompt, shift_prompt,
            k_sample, v_sample, wkv_sample, shift_sample)
```

```python
import os
import numpy as np
from contextlib import ExitStack
import concourse.bass as bass
import concourse.mybir as mybir
from concourse.bass_utils import run_bass_kernel_spmd

F32 = mybir.dt.float32
BF16 = mybir.dt.bfloat16
ALU = mybir.AluOpType
AF = mybir.ActivationFunctionType
AX = mybir.AxisListType

PE, ACT, DVE, POOL, SP = 0, 1, 2, 3, 4
ENG_NAMES = ["tensor", "scalar", "vector", "gpsimd", "sync"]
EPOCH = 20000
NDMA = 24

D = 1024
E = 2048
SEQ = 4096
DSEQ = 32
PAST = 1024
NPAIR = 16
NHEAD = 16
KC = 8
DECAY_C = 0.6065306597126334


class Buf:
    __slots__ = ("name", "w", "r", "excl")

    def __init__(self, name="", excl=False):
        self.name = name
        self.w = {}
        self.r = {}
        self.excl = excl


class V:
    __slots__ = ("ap", "bufs")

    def __init__(self, ap, bufs):
        self.ap = ap
        self.bufs = bufs

    def __getitem__(self, k):
        return V(self.ap[k], self.bufs)


class Prog:
    def __init__(self):
        self.ops = [[] for _ in range(5)]
        self.cnt = [0] * 5
        self.seen = [dict() for _ in range(5)]
        self.dma_i = 0
        self.dma_cnt = [0] * NDMA
        self.total = 0
        self.limit = None
        self.log = []

    def _need(self, eng, key, ticket, waits):
        if key == PE and eng == PE:
            return
        s = self.seen[eng]
        if s.get(key, 0) >= ticket:
            return
        s[key] = ticket
        waits.append((key, ticket))

    def _deps(self, eng, reads, writes, waits):
        for b in reads:
            for k, t in b.w.items():
                self._need(eng, k, t, waits)
            if b.excl:
                for k, t in b.r.items():
                    if k != eng:
                        self._need(eng, k, t, waits)
        for b in writes:
            for k, t in b.w.items():
                self._need(eng, k, t, waits)
            for k, t in b.r.items():
                self._need(eng, k, t, waits)

    def op(self, eng, fn, reads=(), writes=()):
        self.total += 1
        if self.limit is not None and self.total > self.limit:
            return
        waits = []
        self._deps(eng, reads, writes, waits)
        self.cnt[eng] += 1
        t = self.cnt[eng]
        self.ops[eng].append([waits, fn, t, False])
        for b in reads:
            b.r[eng] = t
        for b in writes:
            b.w = {eng: t}
            b.r = {}

    def dma(self, fn, reads=(), writes=(), eng=SP):
        self.total += 1
        if self.limit is not None and self.total > self.limit:
            return
        slot = self.dma_i % NDMA
        self.dma_i += 1
        key = ("d", slot)
        waits = []
        if self.dma_cnt[slot] > 0:
            self._need(eng, key, self.dma_cnt[slot], waits)
        self._deps(eng, reads, writes, waits)
        self.dma_cnt[slot] += 1
        t = self.dma_cnt[slot]
        self.ops[eng].append([waits, fn, ("d", slot, t), True])
        for b in reads:
            b.r[key] = t
        for b in writes:
            b.w = {key: t}
            b.r = {}

    def final_all(self, eng=SP):
        waits = []
        for slot in range(NDMA):
            if self.dma_cnt[slot] > 0:
                waits.append((("d", slot), self.dma_cnt[slot]))
        for k in range(4):
            if self.cnt[k] > 0:
                waits.append((k, self.cnt[k]))
        self.ops[eng].append([waits, None, None, False])

    def emit(self, block, esems, dsems):
        prog = self

        def semval(key, t):
            if isinstance(key, tuple):
                return dsems[key[1]], 16 * t
            ep = (t - 1) // EPOCH
            return esems[key][ep], t - ep * EPOCH

        def run(eng_idx):
            def body(e):
                for waits, fn, sig, isdma in prog.ops[eng_idx]:
                    for key, t in waits:
                        s, v = semval(key, t)
                        e.wait_ge(s, v)
                    if fn is None:
                        continue
                    ins = fn(e)
                    if isdma:
                        ins.then_inc(dsems[sig[1]], 16)
                    else:
                        ep = (sig - 1) // EPOCH
                        ins.then_inc(esems[eng_idx][ep], 1)
            return body

        block.tensor(run(PE))
        block.scalar(run(ACT))
        block.vector(run(DVE))
        block.gpsimd(run(POOL))
        block.sync(run(SP))


PV_E = ["w0", "a0", "k_k", "k_a", "r_k", "ln_g", "ln_b"]
PV_D = ["a_norm_g", "mu_r", "mu_k", "mu_v", "mu_z", "mu_w", "mu_a", "kv_norm_g", "b_norm_g"]
PVO = {}
_o = 0
for _n in PV_E:
    PVO[_n] = _o
    _o += 16
for _n in PV_D:
    PVO[_n] = _o
    _o += 8
PVO["ada_b_a"] = _o; _o += 24
PVO["ada_b_b"] = _o; _o += 24
PVO["k_gain"] = _o; _o += 1
PVO["q_gain"] = _o; _o += 1
PVO["eps_rms"] = _o; _o += 1
PVO["eps_gn"] = _o; _o += 1
PVO["eps_l2"] = _o; _o += 1
PVO["one"] = _o; _o += 1
PVO["hm0"] = _o; _o += 1
PVO["hm1"] = _o; _o += 1
PVO["nw0"] = _o; _o += 16
PVO["na0"] = _o; _o += 16
PVO["omka"] = _o; _o += 16
NPV = _o

CO = {"ident": 0, "ML": 128, "MUcat": 256, "MKcat": 512, "TriNeg": 768, "BO64": 896, "BO1": 1024, "ones": 1152}
NCST = 1280


def _fm(v, ncol):
    return np.ascontiguousarray(np.asarray(v, np.float32).reshape(ncol, 128).T)


def make_consts():
    c = np.zeros((128, NCST), np.float32)
    i = np.arange(128)
    c[:, 0:128] = np.eye(128, dtype=np.float32)
    c[:, 128:256] = -(i[None, :] < i[:, None]).astype(np.float32)
    up_strict = (i[:, None] < i[None, :]).astype(np.float32)
    up_incl = (i[:, None] <= i[None, :]).astype(np.float32)
    c[:, 256:384] = -up_strict
    c[:, 384:512] = -up_incl
    c[:, 512:640] = up_strict
    c[:, 640:768] = up_incl
    c[:, 768:896] = -(i[:, None] >= i[None, :]).astype(np.float32)
    blk = (i[:, None] // 64 == i[None, :] // 64).astype(np.float32)
    c[:, 896:1024] = blk / 64.0
    c[:, 1024:1152] = blk
    c[:, 1152:1280] = 1.0
    return c


def make_pvec(inp):
    pv = np.zeros((128, NPV), np.float32)
    src_e = {"w0": inp["a_w0"][0], "a0": inp["a_a0"][0], "k_k": inp["a_k_k"][0], "k_a": inp["a_k_a"][0],
             "r_k": inp["a_r_k"][0].reshape(-1), "ln_g": inp["a_ln_g"][0], "ln_b": inp["a_ln_b"][0]}
    for n in PV_E:
        pv[:, PVO[n]:PVO[n] + 16] = _fm(src_e[n], 16)
    mu = inp["a_mu_in"][0]
    src_d = {"a_norm_g": inp["a_norm_g"][0], "mu_r": mu[0], "mu_k": mu[1], "mu_v": mu[2], "mu_z": mu[3],
             "mu_w": inp["a_mu_w"][0], "mu_a": inp["a_mu_a"][0], "kv_norm_g": inp["kv_norm_g"],
             "b_norm_g": inp["b_norm_g"][0]}
    for n in PV_D:
        pv[:, PVO[n]:PVO[n] + 8] = _fm(src_d[n], 8)
    pv[:, PVO["ada_b_a"]:PVO["ada_b_a"] + 24] = _fm(inp["a_ada_b"][0], 24)
    pv[:, PVO["ada_b_b"]:PVO["ada_b_b"] + 24] = _fm(inp["b_ada_b"][0], 24)
    pv[:, PVO["k_gain"]] = np.asarray(inp["k_gain"], np.float32)
    pv[:, PVO["q_gain"]] = np.asarray(inp["b_q_gain"][0], np.float32)
    pv[:, PVO["eps_rms"]] = 1e-6
    pv[:, PVO["eps_gn"]] = 64e-5
    pv[:, PVO["eps_l2"]] = 1e-12
    pv[:, PVO["one"]] = 1.0
    pv[:64, PVO["hm0"]] = 1.0
    pv[64:, PVO["hm1"]] = 1.0
    return pv


def build_program(n_prompt_tiles=32, do_samples=True, limit=None):
    nc = bass.Bass("TRN2", target_bir_lowering=False)
    P = Prog()
    P.limit = limit

    def din(name, shape, dt=F32):
        return nc.dram_tensor(name, list(shape), dt, kind="ExternalInput").ap()

    def dout(name, shape, dt=F32):
        return nc.dram_tensor(name, list(shape), dt, kind="ExternalOutput").ap()

    def dscr(name, shape, dt=BF16):
        return nc.dram_tensor(name, list(shape), dt, kind="Internal").ap()

    xp = din("xp", [SEQ, D]); xs = din("xs", [2 * DSEQ, D])
    ck = din("ck", [2 * PAST, E]); cv = din("cv", [2 * PAST, E])
    swkv = din("swkv", [2, 32, 64, 64])
    hprev = din("hprev", [128, KC, 3]); c3T = din("c3T", [128, KC, 3])
    pvec = din("pvec", [128, NPV]); cst = din("cst", [128, NCST])
    rowv = din("rowv", [3, D])
    ada_a = din("ada_a", [D, 3 * D]); ada_b = din("ada_b", [D, 3 * D])
    w_in = din("w_in", [D, 4 * E]); w1 = din("w1", [D, 64]); a1 = din("a1", [D, 64])
    w2 = din("w2", [64, E]); a2 = din("a2", [64, E])
    wout_a = din("wout_a", [E, D]); kvw = din("kvw", [D, 2 * E]); bwin = din("bwin", [D, 2 * E])
    wout_b = din("wout_b", [E, D])

    yp = dout("yp", [SEQ, D]); ys = dout("ys", [2 * DSEQ, D])
    kp = dout("kp", [SEQ, E]); vp = dout("vp", [SEQ, E])
    wkvp = dout("wkvp", [32, 64, 64]); shp = dout("shp", [128, KC])
    kso = dout("kso", [2 * DSEQ, E]); vso = dout("vso", [2 * DSEQ, E])
    wkvs = dout("wkvs", [2, 32, 64, 64]); shs = dout("shs", [2, 128, KC])

    WINS = dscr("WINS", [NPAIR, 128, 4, KC, 128])
    KVWS = dscr("KVWS", [8, 128, KC, 512])
    BWS = dscr("BWS", [NHEAD, 128, 2, KC, 128])
    WOS = [dscr("WOSa", [2, 128, 16, 512]), dscr("WOSb", [2, 128, 16, 512])]
    GSC = dscr("GSC", [2, 3, 128, D], F32)
    KTSP = dscr("KTSP", [NHEAD, 128, SEQ]); VSP = dscr("VSP", [SEQ, E])
    KTSS = dscr("KTSS", [2, NHEAD, 128, PAST + DSEQ]); VSS = dscr("VSS", [2, PAST + DSEQ, E])
    bWINS, bKVWS, bBWS, bWOS, bGSC = Buf(), Buf(), Buf(), [Buf(), Buf()], Buf()
    bKT = [Buf(), Buf(), Buf()]
    bVS = [Buf(), Buf(), Buf()]
    bOUT = Buf("outs")

    es = ExitStack()
    with es:
        def T(name, shape, dt=F32):
            t = es.enter_context(nc.sbuf_tensor(name, list(shape), dt))
            return V(t[tuple(slice(None) for _ in shape)], [Buf(name)])

        PB = []
        PBb = []
        for i in range(8):
            t = es.enter_context(nc.psum_tensor(f"pb{i}", [128, 512], F32))
            PB.append(t)
            PBb.append([Buf(f"pb{i}", excl=True)] * 4)

        def ps(i, s0, ns=1):
            w = 128
            return V(PB[i][:, s0 * w:(s0 + ns) * w], PBb[i][s0:s0 + ns])

        def ps4(i, inner):
            return V(PB[i][:, :].rearrange("p (a b) -> p a b", b=inner), PBb[i][:])

        def rb(*vs):
            out = []
            for v in vs:
                if isinstance(v, V):
                    out.extend(v.bufs)
            return out

        def A(x):
            return x.ap if isinstance(x, V) else x

        def mm(out, lhsT, rhs, start=True, stop=True, nochk=False):
            P.op(PE, lambda e: e.matmul(out.ap, lhsT=lhsT.ap, rhs=rhs.ap, start=start, stop=stop, skip_group_check=nochk),
                 reads=rb(lhsT, rhs), writes=out.bufs)

        def tr(out, in_, ident):
            P.op(PE, lambda e: e.matmul(out.ap, lhsT=in_.ap, rhs=ident.ap, start=True, stop=True),
                 reads=rb(in_, ident), writes=out.bufs)

        def act(out, in_, func, bias=0.0, scale=1.0, eng=ACT):
            P.op(ACT, lambda e: e.activation(out=out.ap, in_=in_.ap, func=func, bias=A(bias), scale=A(scale)),
                 reads=rb(in_, bias, scale), writes=out.bufs)

        EH = {DVE: "vector", POOL: "gpsimd"}

        def cp(eng, out, in_):
            if eng == ACT:
                P.op(ACT, lambda e: e.activation(out=out.ap, in_=in_.ap, func=AF.Identity), reads=rb(in_), writes=out.bufs)
            else:
                P.op(eng, lambda e: e.tensor_copy(out=out.ap, in_=in_.ap), reads=rb(in_), writes=out.bufs)

        def tt(eng, out, in0, in1, op):
            P.op(eng, lambda e: e.tensor_tensor(out=out.ap, in0=in0.ap, in1=in1.ap, op=op),
                 reads=rb(in0, in1), writes=out.bufs)

        def ts(eng, out, in0, s1, op0, s2=None, op1=None):
            if op1 is None:
                P.op(eng, lambda e: e.tensor_scalar(out=out.ap, in0=in0.ap, scalar1=A(s1), scalar2=None, op0=op0),
                     reads=rb(in0, s1), writes=out.bufs)
            else:
                P.op(eng, lambda e: e.tensor_scalar(out=out.ap, in0=in0.ap, scalar1=A(s1), scalar2=A(s2), op0=op0, op1=op1),
                     reads=rb(in0, s1, s2), writes=out.bufs)

        def stt(eng, out, in0, scalar, in1, op0, op1):
            P.op(eng, lambda e: e.scalar_tensor_tensor(out=out.ap, in0=in0.ap, scalar=A(scalar), in1=in1.ap, op0=op0, op1=op1),
                 reads=rb(in0, scalar, in1), writes=out.bufs)

        def recip(out, in_):
            P.op(DVE, lambda e: e.reciprocal(out=out.ap, in_=in_.ap), reads=rb(in_), writes=out.bufs)

        def memset(eng, out, val):
            P.op(eng, lambda e: e.memset(out.ap, val), writes=out.bufs)

        def dma(out_ap, in_ap, reads=(), writes=()):
            P.dma(lambda e: e.dma_start(out=out_ap, in_=in_ap, allow_slow_non_contiguous=True), reads=list(reads), writes=list(writes))

        def load(dst, src_ap, rbufs=()):
            dma(dst.ap, src_ap, reads=rbufs, writes=dst.bufs)

        def store(dst_ap, src, wbufs=()):
            dma(dst_ap, src.ap, reads=src.bufs, writes=list(wbufs))

        CST = T("CST", [128, NCST]); PVt = T("PV", [128, NPV]); C3 = T("C3", [128, KC, 3]); HP = T("HP", [128, KC, 3])
        IDB = T("IDB", [128, 128], BF16); TRIB = T("TRIB", [128, 128], BF16); BO1B = T("BO1B", [128, 128], BF16)
        ONEB = T("ONEB", [128, 128], BF16); MSKB = T("MSKB", [128, 128], BF16)
        MOD = T("MOD", [128, 2, 3, 2, KC])
        GBC = [T("GBC0", [128, D]), T("GBC1", [128, D])]
        KGBC = T("KGBC", [128, 128])
        W1A1 = T("W1A1", [128, KC, 2, 64], BF16); W2A2 = T("W2A2", [64, 2, E], BF16)
        STG = T("STG", [128, 2048]); STB = T("STB", [128, 2048], BF16)
        XT = T("XT", [128, D]); XN = T("XN", [128, D])
        HT = T("HT", [128, KC, 129]); DXT = T("DXT", [128, KC, 128])
        MIX = [T(f"MIX{j}", [128, KC, 128], BF16) for j in range(6)]
        HWA = T("HWA", [64, 2, 128], BF16)
        WIN = [T(f"WIN{i}", [128, 4, KC, 128], BF16) for i in range(2)]
        TT = [T(f"T{i}", [128, 128]) for i in range(10)]
        KR = [T(f"KR{i}", [128, 2, 128], BF16) for i in range(2)]
        BTT = [T(f"BTT{i}", [128, 128], BF16) for i in range(2)]; KTT = [T(f"KTT{i}", [128, 128], BF16) for i in range(2)]
        T4B = T("T4B", [128, 128], BF16); T8B = T("T8B", [128, 128], BF16)
        VTOK = [T(f"VTOK{i}", [128, 128], BF16) for i in range(2)]; KTOK = [T(f"KTOK{i}", [128, 128], BF16) for i in range(2)]
        NBTOK = [T(f"NBTOK{i}", [128, 128], BF16) for i in range(2)]
        EGP = [T(f"EGP{i}", [128, 128]) for i in range(3)]; SZP = [T(f"SZP{i}", [128, 128]) for i in range(3)]
        BONV = [T(f"BONV{i}", [128, 128]) for i in range(3)]
        CY = T("CY", [128, 128]); CT2 = T("CT2", [128, 128]); CT5 = T("CT5", [128, 128]); PSS = T("PSS", [128, 64])
        MALL = T("MALL", [128, 2, 2, 2, 128], BF16)
        ARBT = T("ARBT", [128, 2, 128], BF16); AAKT = T("AAKT", [128, 2, 128], BF16); ARKT = T("ARKT", [128, 2, 128], BF16)
        XB = [T("XB0", [128, 128], BF16), T("XB1", [128, 128], BF16)]
        YG = T("YG", [128, 16, 128], BF16)
        WOUTQ = [T(f"WOUTQ{i}", [128, 16, 256], BF16) for i in range(2)]
        KVW = [T(f"KVW{i}", [128, KC, 512], BF16) for i in range(2)]
        XKT = T("XKT", [128, KC, 128], BF16); H1T = T("H1T", [128, KC, 128], BF16)
        SQ = T("SQ", [128, 512]); KOUT = [T(f"KOUT{i}", [128, 512]) for i in range(2)]
        VOUT = [T(f"VOUT{i}", [128, 512]) for i in range(2)]
        VBt = [T(f"VB{i}", [128, 512], BF16) for i in range(2)]
        KTST = [T(f"KTST{i}", [128, 4, 128], BF16) for i in range(2)]
        SS = T("SS", [128, 8]); SSQ = T("SSQ", [128, 2])
        SF = T("SF", [128, NPAIR, 64]); SBD = T("SBD", [128, NPAIR, 128], BF16)
        sf_b = [Buf(f"sf{i}") for i in range(NPAIR)]; sbd_b = [Buf(f"sbd{i}") for i in range(NPAIR)]
        SFp = lambda pr: V(SF.ap[:, pr, :], [sf_b[pr]])
        SBDp = lambda pr: V(SBD.ap[:, pr, :], [sbd_b[pr]])
        SF = V(SF.ap, sf_b); SBD = V(SBD.ap, sbd_b)
        KRM = [[T(f"KRM{p}{i}", [128, 128], BF16) for i in range(2)] for p in range(2)]
        BTM = [[T(f"BTM{p}{i}", [128, 128], BF16) for i in range(2)] for p in range(2)]
        KTM = [[T(f"KTM{p}{i}", [128, 128], BF16) for i in range(2)] for p in range(2)]
        BW = [T(f"BW{i}", [128, 2, KC, 128], BF16) for i in range(2)]
        QTP = [T(f"QTP{i}", [128, 128], BF16) for i in range(2)]
        SZQ = [T(f"SZQ{i}", [128, 128]) for i in range(2)]
        QE1 = T("QE1", [128, 128]); QE2 = T("QE2", [128, 128])
        NKMAX = SEQ // 128
        KTH = T("KTH", [128, SEQ], BF16); VH = T("VH", [128, NKMAX, 128], BF16)
        E1 = T("E1", [128, 4, 128]); SPB = [T("SPB0", [128, 4, 128], BF16), T("SPB1", [128, 4, 128], BF16)]
        ESt = T("ES", [128, 4, 128])
        AT = [T("AT0", [128, 4, 128], BF16), T("AT1", [128, 4, 128], BF16)]
        NONEB = T("NONEB", [128, 128], BF16)
        CB = T("CB", [128, 128])
        WST = T("WST", [64, 128])

        def pv(name, k=None, n=1):
            o = PVO[name] + (0 if k is None else k)
            return PVt[:, o:o + n]

        def cc(name, w=128):
            return CST[:, CO[name]:CO[name] + w]

        IDF = cc("ident")

        load(CST, cst); load(PVt, pvec); load(C3, c3T); load(HP, hprev)
        load(KGBC, rowv[2:3, 0:128].partition_broadcast(128))
        cp(DVE, IDB, IDF); cp(DVE, TRIB, cc("TriNeg")); cp(DVE, BO1B, cc("BO1")); cp(DVE, ONEB, cc("ones")); ts(DVE, NONEB, cc("ones"), -1.0, ALU.mult)
        cp(DVE, MSKB, CST[:, CO["MKcat"]:CO["MKcat"] + 128])
        ts(DVE, pv("nw0", 0, 16), pv("w0", 0, 16), -1.0, ALU.mult)
        ts(DVE, pv("na0", 0, 16), pv("a0", 0, 16), -1.0, ALU.mult)
        ts(DVE, pv("omka", 0, 16), pv("k_a", 0, 16), -1.0, ALU.mult, 1.0, ALU.add)
        memset(DVE, SF, 0.0); memset(POOL, SBD, 0.0)

        CBC = XN
        for layer, (adaw, normg, bname) in enumerate([(ada_a, "a_norm_g", "ada_b_a"), (ada_b, "b_norm_g", "ada_b_b")]):
            adav = adaw.rearrange("(k p) c -> p k c", p=128)
            stg3 = V(STG.ap.rearrange("p (k c) -> p k c", k=KC), STG.bufs)
            for cch in range(8):
                load(stg3, adav[:, :, cch * 256:(cch + 1) * 256])
                for bi in range(2):
                    blk = cch * 2 + bi
                    o = ps(7, 0)[:, blk * 3:(blk + 1) * 3]
                    for kc in range(KC):
                        mm(o, stg3[:, kc, bi * 128:(bi + 1) * 128], C3[:, kc, :], start=(kc == 0), stop=(kc == KC - 1))
            adp = V(ps(7, 0).ap[:, 0:48].rearrange("p (b s) -> p b s", s=3), ps(7, 0).bufs)
            for s in range(3):
                tt(DVE, MOD[:, layer, s, 1, :], adp[:, 0:8, s], pv(bname, 0, 8), ALU.add)
                tt(DVE, MOD[:, layer, s, 0, :], adp[:, 8:16, s], pv(bname, 8, 8), ALU.add)
                ts(DVE, MOD[:, layer, s, 0, :], MOD[:, layer, s, 0, :], 1.0, ALU.add)
                tt(DVE, MOD[:, layer, s, 0, :], MOD[:, layer, s, 0, :], pv(normg, 0, 8), ALU.mult)
            load(GBC[1], rowv[layer:layer + 1, :].partition_broadcast(128))
            cbc3 = V(CBC.ap.rearrange("p (k c) -> p k c", k=KC), CBC.bufs)
            for s in range(3):
                for kc in range(KC):
                    ts(DVE, cbc3[:, kc, :], cc("ones"), C3[:, kc, s:s + 1], ALU.mult)
                for cch in range(4):
                    load(stg3, adav[:, :, 2048 + cch * 256:2048 + (cch + 1) * 256])
                    o = ps(cch % 2, 0, 2)
                    for kc in range(KC):
                        mm(o, cbc3[:, kc, :], stg3[:, kc, :], start=(kc == 0), stop=(kc == KC - 1))
                    tt(DVE, GBC[0][:, cch * 256:(cch + 1) * 256], o, GBC[1][:, cch * 256:(cch + 1) * 256], ALU.add)
                store(GSC[layer, s], GBC[0], [bGSC])

        cvt_i = [0]

        def convert(src_ap, dst_ap, wb, ncol=2048, shape3=None):
            load(STG[:, 0:ncol], src_ap)
            eng = [DVE, ACT, POOL][cvt_i[0] % 3]
            cvt_i[0] += 1
            cp(eng, STB[:, 0:ncol], STG[:, 0:ncol])
            src = STB[:, 0:ncol]
            if shape3 is not None:
                src = V(src.ap.rearrange("p (a c) -> p a c", a=shape3), src.bufs)
            store(dst_ap, src, [wb])

        for kc in range(KC):
            rows = slice(kc * 128, (kc + 1) * 128)
            for j in range(4):
                convert(w_in[rows, j * E:(j + 1) * E], WINS.rearrange("a p j k c -> p a j k c")[:, :, j, kc, :], bWINS, shape3=16)
            for half in range(2):
                convert(kvw[rows, half * E:(half + 1) * E],
                        KVWS.rearrange("g p k c -> p g k c")[:, half * 4:(half + 1) * 4, kc, :], bKVWS, shape3=4)
            for qz in range(2):
                convert(bwin[rows, qz * E:(qz + 1) * E], BWS.rearrange("h p q k c -> p h q k c")[:, :, qz, kc, :], bBWS, shape3=16)
        for li, wo in enumerate([wout_a, wout_b]):
            for kc in range(16):
                convert(wo[kc * 128:(kc + 1) * 128, :], WOS[li].rearrange("h p k c -> p h k c")[:, :, kc, :], bWOS[li],
                        ncol=1024, shape3=2)
        for i, wsrc in enumerate([w1, a1]):
            load(V(STG.ap[:, 0:512].rearrange("p (k c) -> p k c", k=KC), STG.bufs), wsrc.rearrange("(k p) c -> p k c", p=128))
            cp(DVE, W1A1[:, :, i, :], V(STG.ap[:, 0:512].rearrange("p (k c) -> p k c", k=KC), STG.bufs))
        for i, wsrc in enumerate([w2, a2]):
            load(STG[0:64, :], wsrc)
            cp(DVE, W2A2[:, i, :], STG[0:64, :])

        if do_samples:
            for b in range(2):
                for blk in range(PAST // 128):
                    r0 = b * PAST + blk * 128
                    load(STG, cv[r0:r0 + 128, :])
                    cp([DVE, POOL][blk % 2], STB, STG)
                    store(VSS[b, blk * 128:(blk + 1) * 128, :], STB, [bVS[1 + b]])
                for h in range(NHEAD):
                    stg3 = V(STG.ap[:, 0:1024].rearrange("p (n c) -> p n c", n=8), STG.bufs)
                    load(stg3, ck[b * PAST:(b + 1) * PAST, h * 128:(h + 1) * 128].rearrange("(n p) c -> p n c", p=128))
                    for half in range(2):
                        o = ps(half, 0, 4)
                        for q in range(4):
                            tr(o[:, q * 128:(q + 1) * 128], stg3[:, half * 4 + q, :], IDF)
                        cp([ACT, DVE][half], STB[:, half * 512:(half + 1) * 512], o)
                    store(KTSS[b, h, :, 0:PAST], STB[:, 0:1024], [bKT[1 + b]])

        def gelem(i):
            return [ACT, DVE][i % 2]

        def token_tile(seq, t0, n, x_src, y_dst, k_dst, v_dst, KTS_s, VS_s, koff, first, last, wkv_dst, sh_dst, swkv_src):
            bKTs, bVSs = bKT[seq], bVS[seq]
            L = int(np.log2(n))
            load(XT[:n, :], x_src)
            if first:
                load(GBC[0], GSC[0, seq], [bGSC]); load(GBC[1], GSC[1, seq], [bGSC])
                cp(DVE, HT[:, :, 0], HP[:, :, seq])
                if swkv_src is None:
                    memset(DVE, SF, 0.0); memset(POOL, SBD, 0.0)
                else:
                    for g4 in range(4):
                        sg = V(STG.ap[0:64, :].rearrange("p (h k) -> p h k", h=32), STG.bufs)
                        if g4 == 0:
                            load(sg, swkv_src.rearrange("h v k -> v h k"))
                        o = ps(7, 0, 2)
                        for q in range(4):
                            pr = g4 * 4 + q
                            tr(o[:, q * 64:(q + 1) * 64], V(STG.ap[0:64, pr * 128:(pr + 1) * 128], STG.bufs), IDF[0:64, 0:64])
                        o3 = V(o.ap.rearrange("p (a v) -> p a v", v=64), o.bufs)
                        cp(DVE, SF[:, g4 * 4:(g4 + 1) * 4, :], o3)
                        if g4 == 0:
                            memset(POOL, SBD, 0.0)
                        cp(DVE, SBD[0:64, g4 * 4:(g4 + 1) * 4, 0:64], o3[0:64])
                        cp(DVE, SBD[64:128, g4 * 4:(g4 + 1) * 4, 64:128], o3[64:128])

            def rms_to_T(dsts):
                act(XN[:n, :], XT[:n, :], AF.Square)
                P.op(DVE, lambda e: e.reduce_sum(out=SSQ.ap[:n, 0:1], in_=XN.ap[:n, :], axis=AX.X), reads=XN.bufs, writes=SSQ.bufs)
                act(SSQ[:n, 1:2], SSQ[:n, 0:1], AF.Ln, bias=pv("eps_rms")[:n], scale=1.0 / D)
                act(SSQ[:n, 1:2], SSQ[:n, 1:2], AF.Exp, scale=-0.5)
                ts(DVE, XN[:n, :], XT[:n, :], SSQ[:n, 1:2], ALU.mult)
                for half in range(2):
                    bank = [7, 5][half]
                    for q in range(4):
                        kc = half * 4 + q
                        tr(ps(bank, q)[:, :n], XN[:n, kc * 128:(kc + 1) * 128], IDF[:n, :n])
                    for q in range(4):
                        kc = half * 4 + q
                        for (dst, sc, bi) in dsts:
                            if bi is None:
                                act(dst(kc), ps(bank, q)[:, :n], AF.Identity, scale=sc(kc))
                            else:
                                act(dst(kc), ps(bank, q)[:, :n], AF.Identity, bias=bi(kc), scale=sc(kc))

            rms_to_T([(lambda kc: HT[:, kc, 1:1 + n], lambda kc: MOD[:, 0, seq, 0, kc:kc + 1], lambda kc: MOD[:, 0, seq, 1, kc:kc + 1])])
            tt(DVE, DXT[:, :, :n], HT[:, :, 0:n], HT[:, :, 1:1 + n], ALU.subtract)
            for j, mun in enumerate(["mu_r", "mu_k", "mu_v", "mu_z", "mu_w", "mu_a"]):
                for kc in range(KC):
                    stt(DVE, MIX[j][:, kc, :n], DXT[:, kc, :n], pv(mun, kc), HT[:, kc, 1:1 + n], ALU.mult, ALU.add)
            if last and sh_dst is not None:
                store(sh_dst, HT[:, :, n], [bOUT])
            cp(DVE, HT[:, :, 0], HT[:, :, n])
            for i in range(2):
                o = ps(1, i)[0:64, :n]
                for kc in range(KC):
                    mm(o, W1A1[:, kc, i, :], MIX[4 + i][:, kc, :n], start=(kc == 0), stop=(kc == KC - 1))
            act(WST[:, :n], ps(1, 0)[0:64, :n], AF.Exp, scale=2.0)
            ts(DVE, WST[:, :n], WST[:, :n], 1.0, ALU.add)
            recip(WST[:, :n], WST[:, :n])
            ts(DVE, HWA[:, 0, :n], WST[:, :n], -2.0, ALU.mult, 1.0, ALU.add)
            cp(ACT, HWA[:, 1, :n], ps(1, 1)[0:64, :n])

            def wout_load(li, q):
                load(WOUTQ[q % 2], WOS[li][q // 2][:, :, (q % 2) * 256:(q % 2 + 1) * 256], [bWOS[li]])

            def genA(pr, par):
                Wt = WIN[pr % 2]
                if pr == 0:
                    load(Wt, WINS[pr], [bWINS])
                if pr + 1 < NPAIR:
                    load(WIN[(pr + 1) % 2], WINS[pr + 1], [bWINS])
                pc = slice(pr * 128, (pr + 1) * 128)
                for j in range(4):
                    for kc in range(KC):
                        mm(ps(0, j)[:, :n], Wt[:, j, kc, :], MIX[j][:, kc, :n], start=(kc == 0), stop=(kc == KC - 1))
                    yield
                for i in range(2):
                    mm(ps(1, i)[:, :n], W2A2[:, i, pc], HWA[:, i, :n])
                PRr, PRk, PRv, PRz = (ps(0, j)[:, :n] for j in range(4))
                T1, T2, T3, T5, T6, T7, T9, G, EGI, EGM = (t[:, :n] for t in TT[:10])
                EG = EGP[pr % 3][:, :n]
                TZ = SZP[pr % 3][:, :n]
                KRp, BTTp, KTTp = KR[par], BTT[par], KTT[par]
                pe = lambda nm: pv(nm, pr)
                act(T1, ps(1, 0)[:, :n], AF.Exp, bias=pe("nw0"), scale=-1.0)
                act(T2, ps(1, 1)[:, :n], AF.Exp, bias=pe("na0"), scale=-1.0)
                yield
                ts(DVE, T1, T1, 1.0, ALU.add); recip(T1, T1)
                ts(POOL, T2, T2, 1.0, ALU.add); recip(T2, T2)
                yield
                P.op(DVE, lambda e, G=G, T1=T1: e.tensor_tensor_scan(out=G.ap, data0=cc("ones")[:, :n].ap, data1=T1.ap, initial=0.0,
                                                                      op0=ALU.mult, op1=ALU.add), reads=rb(T1, CST), writes=G.bufs)
                act(EG, G, AF.Exp, scale=-DECAY_C)
                act(EGI, G, AF.Exp, scale=DECAY_C)
                tt(POOL, EGM, G, T1, ALU.subtract)
                act(EGM, EGM, AF.Exp, scale=-DECAY_C)
                yield
                ts(DVE, T3, PRk, pe("k_k"), ALU.mult)
                act(T4B[:, :n], T3, AF.Square)
                mm(ps(1, 2)[:, :n], BO1B, T4B[:, :n])
                yield
                act(T5, ps(1, 2)[:, :n], AF.Ln, bias=pv("eps_l2"))
                act(T5, T5, AF.Exp, scale=-0.5)
                tt(DVE, T3, T3, T5, ALU.mult)
                yield
                ts(POOL, T6, T2, pe("k_a"), ALU.mult, pe("omka"), ALU.add)
                tt(DVE, T6, PRk, T6, ALU.mult)
                tt(POOL, T7, T3, T2, ALU.mult)
                yield
                tt(DVE, KRp[:, 0, :n], T3, EGM, ALU.mult)
                tt(DVE, KRp[:, 1, :n], PRr, EG, ALU.mult)
                tt(POOL, BTTp[:, :n], T7, EGI, ALU.mult)
                tt(POOL, KTTp[:, :n], T6, EGI, ALU.mult)
                yield
                for hh in range(2):
                    hm = pv("hm%d" % hh)
                    ts(POOL, KRM[par][hh][:, :n], KRp[:, 0, :n], hm, ALU.mult)
                    ts(POOL, BTM[par][hh][:, :n], BTTp[:, :n], hm, ALU.mult)
                    ts(POOL, KTM[par][hh][:, :n], KTTp[:, :n], hm, ALU.mult)
                    yield
                stt(DVE, T8B[:, :n], PRr, pe("r_k"), T6, ALU.mult, ALU.mult)
                mm(ps(1, 3)[:, :n], BO1B, T8B[:, :n])
                cp(ACT, T9, PRv)
                yield
                tr(ps(6, 2)[:n, :], T9, IDF)
                tr(ps(6, 0)[:n, 0:128], KTTp[:, :n], IDB)
                tr(ps(6, 1)[:n, 0:128], BTTp[:, :n], IDB)
                tt(DVE, BONV[pr % 3][:, :n], ps(1, 3)[:, :n], T9, ALU.mult)
                yield
                cp(ACT, VTOK[par][:n, :], ps(6, 2)[:n, :])
                cp(DVE, KTOK[par][:n, :], ps(6, 0)[:n, 0:128])
                act(NBTOK[par][:n, :], ps(6, 1)[:n, 0:128], AF.Identity, scale=-1.0)
                yield
                act(TZ, PRz, AF.Exp, scale=-1.0)
                ts(DVE, TZ, TZ, 1.0, ALU.add); recip(TZ, TZ)
                tt(DVE, TZ, PRz, TZ, ALU.mult)
                yield

            def genB(pr, par):
                KRp, BTTp, KTTp = KR[par], BTT[par], KTT[par]
                VT, KT, NBT = VTOK[par], KTOK[par], NBTOK[par]
                pe = lambda nm: pv(nm, pr)
                psN = V(ps(4, 0, 2).ap.rearrange("p (h c) -> p h c", h=2), ps(4, 0, 2).bufs)
                psB = V(PB[2][:, :].rearrange("p (h s c) -> p h s c", h=2, s=2), PBb[2][:])
                psK = V(PB[3][:, :].rearrange("p (h s c) -> p h s c", h=2, s=2), PBb[3][:])
                for hh in range(2):
                    mm(psN[:n, hh, :n], KRM[par][hh][:, :n], BTTp[:, :n])
                    for s2 in range(2):
                        mm(psB[:n, hh, s2, :n], BTM[par][hh][:, :n], KRp[:, s2, :n])
                        mm(psK[:n, hh, s2, :n], KTM[par][hh][:, :n], KRp[:, s2, :n])
                bc = lambda name, off: V(CST.ap[:n, CO[name] + off:CO[name] + off + n].unsqueeze(1).to_broadcast([n, 2, n]), CST.bufs)
                tt(DVE, MALL[:n, 0, 0, :, :n], psN[:n, :, :n], bc("ML", 0), ALU.mult)
                tt(DVE, MALL[:n, 0, 1, :, :n], psB[:n, :, 0, :n], bc("MUcat", 0), ALU.mult)
                yield
                tt(DVE, ARBT[:n, :, :n], psB[:n, :, 1, :n], bc("MUcat", 128), ALU.mult)
                tt(DVE, AAKT[:n, :, :n], psK[:n, :, 0, :n], bc("MKcat", 0), ALU.mult)
                tt(DVE, ARKT[:n, :, :n], psK[:n, :, 1, :n], bc("MKcat", 128), ALU.mult)
                psX = ps(4, 2)
                for hh in range(2):
                    hs = slice(hh * 64, hh * 64 + 64)
                    mm(psX[:n, hs], KRp[:, 0, :n], SBDp(pr)[:, hs], start=(hh == 0), stop=False, nochk=True)
                    mm(psX[:n, hs], AAKT[:n, hh, :n], VT[:n, hs], start=False, stop=False, nochk=True)
                cp(ACT, XB[0][:n, :], psX[:n, :])
                yield
                psM = V(PB[5][:, :].rearrange("p (s h c) -> p s h c", s=2, h=2), PBb[5][:])
                for k in range(L):
                    for hh in range(2):
                        hs = slice(hh * 64, hh * 64 + 64)
                        mm(psX[:n, hs], MALL[:n, k % 2, 1, hh, :n], XB[k % 2][:n, hs], start=False, stop=(k == L - 1 and hh == 1), nochk=True)
                    if k < L - 1:
                        lastk = (k == L - 2)
                        for hh in range(2):
                            if not lastk:
                                mm(psM[:n, 0, hh, :n], MALL[:n, k % 2, 1, hh, :n], MALL[:n, k % 2, 0, hh, :n])
                            mm(psM[:n, 1, hh, :n], MALL[:n, k % 2, 0, hh, :n], MALL[:n, k % 2, 1, hh, :n])
                        if not lastk:
                            cp(ACT, MALL[:n, (k + 1) % 2, :, :, :n], psM[:n, :, :, :n])
                        else:
                            cp(ACT, MALL[:n, (k + 1) % 2, 1, :, :n], psM[:n, 1, :, :n])
                    cp(DVE, XB[(k + 1) % 2][:n, :], psX[:n, :])
                    yield
                U = XB[L % 2]
                psY = ps(4, 3)
                mm(psY[:, :n], SBDp(pr), KRp[:, 1, :n], start=True, stop=False, nochk=True)
                for hh in range(2):
                    hs = slice(hh * 64, hh * 64 + 64)
                    mm(psY[hs, :n], VT[:n, hs], ARKT[:n, hh, :n], start=False, stop=False, nochk=True)
                    mm(psY[hs, :n], U[:n, hs], ARBT[:n, hh, :n], start=False, stop=True, nochk=True)
                psS = ps(7, 1)
                for hh in range(2):
                    hs = slice(hh * 64, hh * 64 + 64)
                    mm(psS[hs, 0:64], KT[:n, hs], VT[:n, hs], start=True, stop=False)
                    mm(psS[hs, 0:64], NBT[:n, hs], U[:n, hs], start=False, stop=True)
                cp(ACT, CY[:, :n], psY[:, :n])
                cp(DVE, PSS[:, :], psS[:, 0:64])
                yield

            def genB2(pr, par):
                pe = lambda nm: pv(nm, pr)
                Y = CY[:, :n]
                C2 = CT2[:, :n]
                C5 = CT5[:, :n]
                sfp = SFp(pr)
                sbp = SBDp(pr)
                tt(DVE, sfp, PSS[:, :], sfp, ALU.add)
                ts(DVE, sfp, sfp, EGP[pr % 3][:, n - 1:n], ALU.mult)
                cp(ACT, sbp[0:64, 0:64], sfp[0:64, :])
                cp(ACT, sbp[64:128, 64:128], sfp[64:128, :])
                mm(ps(7, 2)[:, :n], cc("BO64"), Y)
                yield
                tt(DVE, Y, Y, ps(7, 2)[:, :n], ALU.subtract)
                act(C2, Y, AF.Square)
                mm(ps(7, 3)[:, :n], cc("BO64"), C2)
                yield
                act(C5, ps(7, 3)[:, :n], AF.Ln, bias=pv("eps_gn"))
                act(C5, C5, AF.Exp, scale=-0.5)
                tt(DVE, Y, Y, C5, ALU.mult)
                yield
                ts(DVE, Y, Y, pe("ln_g"), ALU.mult, pe("ln_b"), ALU.add)
                tt(POOL, Y, Y, BONV[pr % 3][:, :n], ALU.add)
                tt(DVE, YG[:, pr, :n], Y, SZP[pr % 3][:, :n], ALU.mult)
                yield

            def drain(g):
                for _ in g:
                    pass

            def interleave(*gens):
                gens = [g for g in gens if g is not None]
                while gens:
                    for g in list(gens):
                        try:
                            next(g)
                        except StopIteration:
                            gens.remove(g)

            wout_load(0, 0)
            npairs = 0 if 'pair' in os.environ.get('KSKIP', '') else NPAIR
            if npairs:
                drain(genA(0, 0))
            for pr in range(npairs):
                interleave(genB(pr, pr % 2),
                           genA(pr + 1, (pr + 1) % 2) if pr + 1 < NPAIR else None,
                           genB2(pr - 1, (pr - 1) % 2) if pr >= 1 else None)
            if npairs:
                drain(genB2(npairs - 1, (npairs - 1) % 2))

            if last and wkv_dst is not None:
                for g4 in range(4):
                    o = ps(7, 0, 4)
                    for q in range(4):
                        pr = g4 * 4 + q
                        tr(o[0:64, q * 128:(q + 1) * 128], SF[:, pr, :], IDF)
                    cp(ACT, STG[0:64, g4 * 512:(g4 + 1) * 512], o[0:64, :])
                store(wkv_dst.rearrange("h v k -> v h k"), V(STG.ap[0:64, :].rearrange("p (h k) -> p h k", h=32), STG.bufs), [bOUT])

            def out_proj(li, src):
                for q in range(4):
                    if q + 1 < 4:
                        wout_load(li, q + 1)
                    Wq = WOUTQ[q % 2]
                    o = ps(q % 2, 0, 2)
                    for pr in range(16):
                        mm(o[:n, :], src[:, pr, :n], Wq[:, pr, :], start=(pr == 0), stop=(pr == 15))
                    hc = slice(q * 256, (q + 1) * 256)
                    tt(DVE, XN[:n, hc], o[:n, :], GBC[li][:n, hc], ALU.mult)
                    tt(POOL, XT[:n, hc], XT[:n, hc], XN[:n, hc], ALU.add)

            out_proj(0, YG)

            rms_to_T([(lambda kc: XKT[:, kc, :n], lambda kc: pv("kv_norm_g", kc), None),
                      (lambda kc: H1T[:, kc, :n], lambda kc: MOD[:, 1, seq, 0, kc:kc + 1], lambda kc: MOD[:, 1, seq, 1, kc:kc + 1])])
            load(KVW[0], KVWS[0], [bKVWS])
            for cg in range(8):
                Wk = KVW[cg % 2]
                if cg + 1 < 8:
                    load(KVW[(cg + 1) % 2], KVWS[cg + 1], [bKVWS])
                o = ps(2 + cg % 2, 0, 4)
                for kc in range(KC):
                    mm(o[:n, :], XKT[:, kc, :n], Wk[:, kc, :], start=(kc == 0), stop=(kc == KC - 1))
                cs = slice((cg % 4) * 512, (cg % 4 + 1) * 512)
                if cg < 4:
                    ko = KOUT[cg % 2]
                    act(SQ[:n, :], o[:n, :], AF.Square)
                    P.op(DVE, lambda e, cg=cg: e.reduce_sum(out=SS.ap[:n, 0:4], in_=SQ.ap[:n, :].rearrange("p (h c) -> p h c", h=4), axis=AX.X),
                         reads=SQ.bufs, writes=SS.bufs)
                    act(SS[:n, 4:8], SS[:n, 0:4], AF.Ln, bias=pv("eps_rms")[:n], scale=1.0 / 128)
                    act(SS[:n, 4:8], SS[:n, 4:8], AF.Exp, scale=-0.5)
                    o3 = V(o.ap[:n, :].rearrange("p (h c) -> p h c", h=4), o.bufs)
                    ko3 = V(ko.ap[:n, :].rearrange("p (h c) -> p h c", h=4), ko.bufs)
                    tt(DVE, ko3, o3, V(SS.ap[:n, 4:8].unsqueeze(2).to_broadcast([n, 4, 128]), SS.bufs), ALU.mult)
                    tt(POOL, ko3, ko3, V(KGBC.ap[:n, :].unsqueeze(1).to_broadcast([n, 4, 128]), KGBC.bufs), ALU.mult)
                    store(k_dst[:, cs], ko[:n, :], [bOUT])
                    kst = KTST[cg % 2]
                    ob = ps(7, 0, 4)
                    for q in range(4):
                        tr(ob[:, q * 128:q * 128 + n], ko[:n, q * 128:(q + 1) * 128], IDF[:n, :n])
                    cp(ACT, kst[:, :, :n], V(ob.ap.rearrange("p (h c) -> p h c", h=4)[:, :, :n], ob.bufs))
                    store(KTS_s[cg * 4:(cg + 1) * 4, :, koff + t0:koff + t0 + n].rearrange("h p c -> p h c"), kst[:, :, :n], [bKTs])
                else:
                    vo = VOUT[cg % 2]
                    cp(ACT, vo[:n, :], o[:n, :])
                    cp(DVE, VBt[cg % 2][:n, :], o[:n, :])
                    store(v_dst[:, cs], vo[:n, :], [bOUT])
                    store(VS_s[koff + t0:koff + t0 + n, cs], VBt[cg % 2][:n, :], [bVSs])

            nk_total = koff + t0 + n
            nkb = (nk_total + 127) // 128
            OG = YG
            wout_load(1, 0)
            def genQ(h, par):
                Bw = BW[h % 2]
                if h == 0:
                    load(Bw, BWS[h], [bBWS])
                if h + 1 < NHEAD:
                    load(BW[(h + 1) % 2], BWS[h + 1], [bBWS])
                psQ, psZg, psSS = ps(par, 0), ps(par, 1), ps(par, 2)
                for kc in range(KC):
                    mm(psQ[:, :n], Bw[:, 0, kc, :], H1T[:, kc, :n], start=(kc == 0), stop=(kc == KC - 1))
                yield
                for kc in range(KC):
                    mm(psZg[:, :n], Bw[:, 1, kc, :], H1T[:, kc, :n], start=(kc == 0), stop=(kc == KC - 1))
                q1, q2 = QE1[:, :n], QE2[:, :n]
                act(q1, psQ[:, :n], AF.Square)
                yield
                mm(psSS[:, :n], cc("ones"), q1)
                act(q2, psSS[:, :n], AF.Ln, bias=pv("eps_rms"), scale=1.0 / 128)
                yield
                act(q2, q2, AF.Exp, scale=-0.5)
                ts(DVE, q2, q2, pv("q_gain"), ALU.mult, 128 ** -0.5, ALU.mult)
                yield
                tt(DVE, QTP[par][:, :n], psQ[:, :n], q2, ALU.mult)
                sz = SZQ[par][:, :n]
                act(sz, psZg[:, :n], AF.Exp, scale=-1.0)
                yield
                ts(DVE, sz, sz, 1.0, ALU.add); recip(sz, sz)
                tt(DVE, sz, psZg[:, :n], sz, ALU.mult)
                yield

            def genAtt(h, par):
                QT = QTP[par]
                psO = ps(4, 0)
                load(KTH[:, 0:nk_total], KTS_s[h, :, 0:nk_total], [bKTs])
                nfull = nk_total // 128
                if nfull > 0:
                    load(VH[:, 0:nfull, :], VS_s[0:nfull * 128, h * 128:(h + 1) * 128].rearrange("(b p) c -> p b c", p=128), [bVSs])
                rem = nk_total - nfull * 128
                if rem > 0:
                    load(VH[0:rem, nfull, :], VS_s[nfull * 128:nk_total, h * 128:(h + 1) * 128], [bVSs])
                memset(POOL, CB[:, :n], 0.0)
                blocks = list(range(nkb - 1, -1, -1))
                groups = [[blocks[0]]] + [blocks[i:i + 4] for i in range(1, len(blocks), 4)]
                ng = len(groups)
                PZB = [5, 6, 7]

                def pzv(g):
                    return V(PB[PZB[g % 3]][:, :].rearrange("p (a c) -> p a c", a=4), PBb[PZB[g % 3]][:])

                def S1(g):
                    pz = pzv(g)
                    for i, kb in enumerate(groups[g]):
                        ks = min(128, nk_total - kb * 128)
                        mm(pz[:ks, i, :n], KTH[:, kb * 128:kb * 128 + ks], QT[:, :n], start=(i == 0), stop=False, nochk=True)

                def S2(g):
                    pz = pzv(g)
                    G = len(groups[g])
                    kb0 = groups[g][0]
                    ks = min(128, nk_total - kb0 * 128)
                    sp = SPB[g % 2]
                    act(E1[:ks, 0:G, :n], pz[:ks, 0:G, :n], AF.Exp)
                    act(sp[:ks, 0:G, :n], E1[:ks, 0:G, :n], AF.Ln, bias=pv("one")[:ks])
                    if g == 0:
                        tt(POOL, sp[:ks, 0, :n], sp[:ks, 0, :n], MSKB[:ks, :n], ALU.mult)

                def S3(g):
                    pz = pzv(g)
                    G = len(groups[g])
                    kb0 = groups[g][0]
                    ks = min(128, nk_total - kb0 * 128)
                    sp = SPB[g % 2]
                    at = AT[g % 2]
                    for i in range(G):
                        mm(pz[:ks, i, :n], TRIB[:ks, :ks], sp[:ks, i, :n], start=False, stop=False, nochk=True)
                        for i2 in range(i):
                            mm(pz[:ks, i, :n], NONEB[:ks, :ks], sp[:ks, i2, :n], start=False, stop=False, nochk=True)
                    lastg = (g == ng - 1)
                    if not lastg:
                        pcb = ps(2 + g % 2, 0)
                        for i in range(G):
                            mm(pcb[:, :n], ONEB[:ks, :], sp[:ks, i, :n], start=(i == 0), stop=(i == G - 1))
                    cbb = V(CB.ap[:ks, :n].unsqueeze(1).to_broadcast([ks, G, n]), CB.bufs)
                    tt(DVE, ESt[:ks, 0:G, :n], pz[:ks, 0:G, :n], cbb, ALU.subtract)
                    act(at[:ks, 0:G, :n], ESt[:ks, 0:G, :n], AF.Exp)
                    if g == 0:
                        tt(POOL, at[:ks, 0, :n], at[:ks, 0, :n], MSKB[:ks, :n], ALU.mult)
                    if not lastg:
                        tt(DVE, CB[:, :n], CB[:, :n], pcb[:, :n], ALU.add)
                    for i, kb in enumerate(groups[g]):
                        mm(psO[:, :n], VH[:ks, kb, :], at[:ks, i, :n], start=(g == 0 and i == 0), stop=(lastg and i == G - 1), nochk=True)

                S1(0)
                if ng > 1:
                    S1(1)
                S2(0)
                yield
                for g in range(ng):
                    if g + 2 < ng:
                        S1(g + 2)
                    if g + 1 < ng:
                        S2(g + 1)
                    S3(g)
                    yield
                tt(DVE, OG[:, h, :n], psO[:, :n], SZQ[par][:, :n], ALU.mult)
                yield

            nheads = 0 if 'att' in os.environ.get('KSKIP', '') else NHEAD
            if nheads:
                drain(genQ(0, 0))
            for h in range(nheads):
                interleave(genAtt(h, h % 2), genQ(h + 1, (h + 1) % 2) if h + 1 < NHEAD else None)
            out_proj(1, OG)
            store(y_dst, XT[:n, :], [bOUT])

        if do_samples:
            for b in range(2):
                r = slice(b * DSEQ, (b + 1) * DSEQ)
                token_tile(1 + b, 0, DSEQ, xs[r, :], ys[r, :], kso[r, :], vso[r, :], KTSS[b], VSS[b], PAST, True, True,
                           wkvs[b], shs[b], swkv[b])
        for ti in range(n_prompt_tiles):
            r = slice(ti * 128, (ti + 1) * 128)
            token_tile(0, ti * 128, 128, xp[r, :], yp[r, :], kp[r, :], vp[r, :], KTSP, VSP, 0, ti == 0, ti == n_prompt_tiles - 1,
                       wkvp, shp, None)

        P.final_all()
        nse = [max(1, (c + EPOCH - 1) // EPOCH) for c in P.cnt]
        esems = [[es.enter_context(nc.semaphore(f"s{ENG_NAMES[i]}{k}")) for k in range(nse[i])] for i in range(5)]
        dsems = [es.enter_context(nc.semaphore(f"d{k}")) for k in range(NDMA)]
        block = es.enter_context(nc.Block())
        P.emit(block, esems, dsems)
    return nc, P


_CACHE = {}


def make_in_maps(inp, n_cores=8):
    f = lambda a: np.ascontiguousarray(np.asarray(a, np.float32))
    cst = make_consts()
    pvec = make_pvec(inp)
    rowv = np.zeros((3, D), np.float32)
    rowv[0] = np.asarray(inp["a_ada_b"][0][2 * D:3 * D], np.float32)
    rowv[1] = np.asarray(inp["b_ada_b"][0][2 * D:3 * D], np.float32)
    rowv[2, :128] = np.asarray(inp["k_gain"], np.float32)
    shared = {
        "pvec": pvec, "cst": cst, "rowv": rowv,
        "ada_a": f(inp["a_ada_w"][0]), "ada_b": f(inp["b_ada_w"][0]),
        "w_in": f(inp["a_w_in"][0]), "w1": f(inp["a_w1"][0]), "a1": f(inp["a_a1"][0]),
        "w2": f(inp["a_w2"][0]), "a2": f(inp["a_a2"][0]), "wout_a": f(inp["a_w_out"][0]),
        "kvw": f(inp["kv_w"]), "bwin": f(inp["b_w_in"][0]), "wout_b": f(inp["b_w_out"][0]),
    }
    maps = []
    for i in range(n_cores):
        c3 = np.stack([np.asarray(inp["c_prompt"][i], np.float32),
                       np.asarray(inp["c_sample"][2 * i], np.float32),
                       np.asarray(inp["c_sample"][2 * i + 1], np.float32)], axis=0)
        c3T = np.ascontiguousarray(c3.reshape(3, KC, 128).transpose(2, 1, 0))
        hp = np.zeros((3, D), np.float32)
        hp[1] = inp["state_shift"][0, 2 * i]
        hp[2] = inp["state_shift"][0, 2 * i + 1]
        hpT = np.ascontiguousarray(hp.reshape(3, KC, 128).transpose(2, 1, 0))
        m = dict(shared)
        m.update({
            "xp": f(inp["x_prompt"][i]),
            "xs": f(np.asarray(inp["x_sample"][2 * i:2 * i + 2]).reshape(2 * DSEQ, D)),
            "ck": f(np.asarray(inp["cache_k"][2 * i:2 * i + 2]).reshape(2 * PAST, E)),
            "cv": f(np.asarray(inp["cache_v"][2 * i:2 * i + 2]).reshape(2 * PAST, E)),
            "swkv": f(inp["state_wkv"][0, 2 * i:2 * i + 2]),
            "hprev": hpT, "c3T": c3T,
        })
        maps.append(m)
    return maps


def kernel(**inputs):
    n = 8
    if "nc" not in _CACHE:
        _CACHE["nc"] = build_program()[0]
    nc = _CACHE["nc"]
    maps = make_in_maps(inputs, n)
    res = run_bass_kernel_spmd(nc, maps, core_ids=list(range(n)))
    R = res.results
    B, BD = 8, 16
    y_prompt = np.stack([R[i]["yp"] for i in range(n)], 0).astype(np.float32)
    y_sample = np.concatenate([R[i]["ys"].reshape(2, DSEQ, D) for i in range(n)], 0).astype(np.float32)
    k_prompt = np.stack([R[i]["kp"].reshape(SEQ, 16, 128) for i in range(n)], 0).astype(np.float32)
    v_prompt = np.stack([R[i]["vp"].reshape(SEQ, 16, 128) for i in range(n)], 0).astype(np.float32)
    wkv_prompt = np.stack([R[i]["wkvp"] for i in range(n)], 0)[None].astype(np.float32)
    shift_prompt = np.stack([R[i]["shp"].T.reshape(D) for i in range(n)], 0)[None].astype(np.float32)
    k_sample = np.concatenate([R[i]["kso"].reshape(2, DSEQ, 16, 128) for i in range(n)], 0).astype(np.float32)
    v_sample = np.concatenate([R[i]["vso"].reshape(2, DSEQ, 16, 128) for i in range(n)], 0).astype(np.float32)
    wkv_sample = np.concatenate([R[i]["wkvs"] for i in range(n)], 0)[None].astype(np.float32)
    shift_sample = np.concatenate([np.stack([R[i]["shs"][b].T.reshape(D) for b in range(2)], 0) for i in range(n)], 0)[None].astype(np.float32)
    return (y_prompt, y_sample, k_prompt, v_prompt, wkv_prompt, shift_prompt,
            k_sample, v_sample, wkv_sample, shift_sample)
```

```python
import os
import numpy as np
from contextlib import ExitStack
import concourse.bass as bass
import concourse.mybir as mybir
from concourse.bass_utils import run_bass_kernel_spmd

F32 = mybir.dt.float32
BF16 = mybir.dt.bfloat16
ALU = mybir.AluOpType
AF = mybir.ActivationFunctionType
AX = mybir.AxisListType

PE, ACT, DVE, POOL, SP = 0, 1, 2, 3, 4
ENG_NAMES = ["tensor", "scalar", "vector", "gpsimd", "sync"]
EPOCH = 20000
NDMA = 24

D = 1024
E = 2048
SEQ = 4096
DSEQ = 32
PAST = 1024
NPAIR = 16
NHEAD = 16
KC = 8
DECAY_C = 0.6065306597126334


class Buf:
    __slots__ = ("name", "w", "r", "excl")

    def __init__(self, name="", excl=False):
        self.name = name
        self.w = {}
        self.r = {}
        self.excl = excl


class V:
    __slots__ = ("ap", "bufs")

    def __init__(self, ap, bufs):
        self.ap = ap
        self.bufs = bufs

    def __getitem__(self, k):
        return V(self.ap[k], self.bufs)


class Prog:
    def __init__(self):
        self.ops = [[] for _ in range(5)]
        self.cnt = [0] * 5
        self.seen = [dict() for _ in range(5)]
        self.dma_i = 0
        self.dma_cnt = [0] * NDMA
        self.total = 0
        self.limit = None
        self.log = []

    def _need(self, eng, key, ticket, waits):
        if key == PE and eng == PE:
            return
        s = self.seen[eng]
        if s.get(key, 0) >= ticket:
            return
        s[key] = ticket
        waits.append((key, ticket))

    def _deps(self, eng, reads, writes, waits):
        for b in reads:
            for k, t in b.w.items():
                self._need(eng, k, t, waits)
            if b.excl:
                for k, t in b.r.items():
                    if k != eng:
                        self._need(eng, k, t, waits)
        for b in writes:
            for k, t in b.w.items():
                self._need(eng, k, t, waits)
            for k, t in b.r.items():
                self._need(eng, k, t, waits)

    def op(self, eng, fn, reads=(), writes=()):
        self.total += 1
        if self.limit is not None and self.total > self.limit:
            return
        waits = []
        self._deps(eng, reads, writes, waits)
        self.cnt[eng] += 1
        t = self.cnt[eng]
        self.ops[eng].append([waits, fn, t, False])
        for b in reads:
            b.r[eng] = t
        for b in writes:
            b.w = {eng: t}
            b.r = {}

    def dma(self, fn, reads=(), writes=(), eng=SP):
        self.total += 1
        if self.limit is not None and self.total > self.limit:
            return
        slot = self.dma_i % NDMA
        self.dma_i += 1
        key = ("d", slot)
        waits = []
        if self.dma_cnt[slot] > 0:
            self._need(eng, key, self.dma_cnt[slot], waits)
        self._deps(eng, reads, writes, waits)
        self.dma_cnt[slot] += 1
        t = self.dma_cnt[slot]
        self.ops[eng].append([waits, fn, ("d", slot, t), True])
        for b in reads:
            b.r[key] = t
        for b in writes:
            b.w = {key: t}
            b.r = {}

    def final_all(self, eng=SP):
        waits = []
        for slot in range(NDMA):
            if self.dma_cnt[slot] > 0:
                waits.append((("d", slot), self.dma_cnt[slot]))
        for k in range(4):
            if self.cnt[k] > 0:
                waits.append((k, self.cnt[k]))
        self.ops[eng].append([waits, None, None, False])

    def emit(self, block, esems, dsems):
        prog = self

        def semval(key, t):
            if isinstance(key, tuple):
                return dsems[key[1]], 16 * t
            ep = (t - 1) // EPOCH
            return esems[key][ep], t - ep * EPOCH

        def run(eng_idx):
            def body(e):
                for waits, fn, sig, isdma in prog.ops[eng_idx]:
                    for key, t in waits:
                        s, v = semval(key, t)
                        e.wait_ge(s, v)
                    if fn is None:
                        continue
                    ins = fn(e)
                    if isdma:
                        ins.then_inc(dsems[sig[1]], 16)
                    else:
                        ep = (sig - 1) // EPOCH
                        ins.then_inc(esems[eng_idx][ep], 1)
            return body

        block.tensor(run(PE))
        block.scalar(run(ACT))
        block.vector(run(DVE))
        block.gpsimd(run(POOL))
        block.sync(run(SP))


PV_E = ["w0", "a0", "k_k", "k_a", "r_k", "ln_g", "ln_b"]
PV_D = ["a_norm_g", "mu_r", "mu_k", "mu_v", "mu_z", "mu_w", "mu_a", "kv_norm_g", "b_norm_g"]
PVO = {}
_o = 0
for _n in PV_E:
    PVO[_n] = _o
    _o += 16
for _n in PV_D:
    PVO[_n] = _o
    _o += 8
PVO["ada_b_a"] = _o; _o += 24
PVO["ada_b_b"] = _o; _o += 24
PVO["k_gain"] = _o; _o += 1
PVO["q_gain"] = _o; _o += 1
PVO["eps_rms"] = _o; _o += 1
PVO["eps_gn"] = _o; _o += 1
PVO["eps_l2"] = _o; _o += 1
PVO["one"] = _o; _o += 1
PVO["hm0"] = _o; _o += 1
PVO["hm1"] = _o; _o += 1
PVO["nw0"] = _o; _o += 16
PVO["na0"] = _o; _o += 16
PVO["omka"] = _o; _o += 16
NPV = _o

CO = {"ident": 0, "ML": 128, "MUcat": 256, "MKcat": 512, "TriNeg": 768, "BO64": 896, "BO1": 1024, "ones": 1152}
NCST = 1280


def _fm(v, ncol):
    return np.ascontiguousarray(np.asarray(v, np.float32).reshape(ncol, 128).T)


def make_consts():
    c = np.zeros((128, NCST), np.float32)
    i = np.arange(128)
    c[:, 0:128] = np.eye(128, dtype=np.float32)
    c[:, 128:256] = -(i[None, :] < i[:, None]).astype(np.float32)
    up_strict = (i[:, None] < i[None, :]).astype(np.float32)
    up_incl = (i[:, None] <= i[None, :]).astype(np.float32)
    c[:, 256:384] = -up_strict
    c[:, 384:512] = -up_incl
    c[:, 512:640] = up_strict
    c[:, 640:768] = up_incl
    c[:, 768:896] = -(i[:, None] >= i[None, :]).astype(np.float32)
    blk = (i[:, None] // 64 == i[None, :] // 64).astype(np.float32)
    c[:, 896:1024] = blk / 64.0
    c[:, 1024:1152] = blk
    c[:, 1152:1280] = 1.0
    return c


def make_pvec(inp):
    pv = np.zeros((128, NPV), np.float32)
    src_e = {"w0": inp["a_w0"][0], "a0": inp["a_a0"][0], "k_k": inp["a_k_k"][0], "k_a": inp["a_k_a"][0],
             "r_k": inp["a_r_k"][0].reshape(-1), "ln_g": inp["a_ln_g"][0], "ln_b": inp["a_ln_b"][0]}
    for n in PV_E:
        pv[:, PVO[n]:PVO[n] + 16] = _fm(src_e[n], 16)
    mu = inp["a_mu_in"][0]
    src_d = {"a_norm_g": inp["a_norm_g"][0], "mu_r": mu[0], "mu_k": mu[1], "mu_v": mu[2], "mu_z": mu[3],
             "mu_w": inp["a_mu_w"][0], "mu_a": inp["a_mu_a"][0], "kv_norm_g": inp["kv_norm_g"],
             "b_norm_g": inp["b_norm_g"][0]}
    for n in PV_D:
        pv[:, PVO[n]:PVO[n] + 8] = _fm(src_d[n], 8)
    pv[:, PVO["ada_b_a"]:PVO["ada_b_a"] + 24] = _fm(inp["a_ada_b"][0], 24)
    pv[:, PVO["ada_b_b"]:PVO["ada_b_b"] + 24] = _fm(inp["b_ada_b"][0], 24)
    pv[:, PVO["k_gain"]] = np.asarray(inp["k_gain"], np.float32)
    pv[:, PVO["q_gain"]] = np.asarray(inp["b_q_gain"][0], np.float32)
    pv[:, PVO["eps_rms"]] = 1e-6
    pv[:, PVO["eps_gn"]] = 64e-5
    pv[:, PVO["eps_l2"]] = 1e-12
    pv[:, PVO["one"]] = 1.0
    pv[:64, PVO["hm0"]] = 1.0
    pv[64:, PVO["hm1"]] = 1.0
    return pv


def build_program(n_prompt_tiles=32, do_samples=True, limit=None):
    nc = bass.Bass("TRN2", target_bir_lowering=False)
    P = Prog()
    P.limit = limit

    def din(name, shape, dt=F32):
        return nc.dram_tensor(name, list(shape), dt, kind="ExternalInput").ap()

    def dout(name, shape, dt=F32):
        return nc.dram_tensor(name, list(shape), dt, kind="ExternalOutput").ap()

    def dscr(name, shape, dt=BF16):
        return nc.dram_tensor(name, list(shape), dt, kind="Internal").ap()

    xp = din("xp", [SEQ, D]); xs = din("xs", [2 * DSEQ, D])
    ck = din("ck", [2 * PAST, E]); cv = din("cv", [2 * PAST, E])
    swkv = din("swkv", [2, 32, 64, 64])
    hprev = din("hprev", [128, KC, 3]); c3T = din("c3T", [128, KC, 3])
    pvec = din("pvec", [128, NPV]); cst = din("cst", [128, NCST])
    rowv = din("rowv", [3, D])
    ada_a = din("ada_a", [D, 3 * D]); ada_b = din("ada_b", [D, 3 * D])
    w_in = din("w_in", [D, 4 * E]); w1 = din("w1", [D, 64]); a1 = din("a1", [D, 64])
    w2 = din("w2", [64, E]); a2 = din("a2", [64, E])
    wout_a = din("wout_a", [E, D]); kvw = din("kvw", [D, 2 * E]); bwin = din("bwin", [D, 2 * E])
    wout_b = din("wout_b", [E, D])

    yp = dout("yp", [SEQ, D]); ys = dout("ys", [2 * DSEQ, D])
    kp = dout("kp", [SEQ, E]); vp = dout("vp", [SEQ, E])
    wkvp = dout("wkvp", [32, 64, 64]); shp = dout("shp", [128, KC])
    kso = dout("kso", [2 * DSEQ, E]); vso = dout("vso", [2 * DSEQ, E])
    wkvs = dout("wkvs", [2, 32, 64, 64]); shs = dout("shs", [2, 128, KC])

    WINS = dscr("WINS", [NPAIR, 128, 4, KC, 128])
    KVWS = dscr("KVWS", [8, 128, KC, 512])
    BWS = dscr("BWS", [NHEAD, 128, 2, KC, 128])
    WOS = [dscr("WOSa", [2, 128, 16, 512]), dscr("WOSb", [2, 128, 16, 512])]
    GSC = dscr("GSC", [2, 3, 128, D], F32)
    KTSP = dscr("KTSP", [NHEAD, 128, SEQ]); VSP = dscr("VSP", [SEQ, E])
    KTSS = dscr("KTSS", [2, NHEAD, 128, PAST + DSEQ]); VSS = dscr("VSS", [2, PAST + DSEQ, E])
    bWINS, bKVWS, bBWS, bWOS, bGSC = Buf(), Buf(), Buf(), [Buf(), Buf()], Buf()
    bKT = [Buf(), Buf(), Buf()]
    bVS = [Buf(), Buf(), Buf()]
    bOUT = Buf("outs")

    es = ExitStack()
    with es:
        def T(name, shape, dt=F32):
            t = es.enter_context(nc.sbuf_tensor(name, list(shape), dt))
            return V(t[tuple(slice(None) for _ in shape)], [Buf(name)])

        PB = []
        PBb = []
        for i in range(8):
            t = es.enter_context(nc.psum_tensor(f"pb{i}", [128, 512], F32))
            PB.append(t)
            PBb.append([Buf(f"pb{i}", excl=True)] * 4)

        def ps(i, s0, ns=1):
            w = 128
            return V(PB[i][:, s0 * w:(s0 + ns) * w], PBb[i][s0:s0 + ns])

        def ps4(i, inner):
            return V(PB[i][:, :].rearrange("p (a b) -> p a b", b=inner), PBb[i][:])

        def rb(*vs):
            out = []
            for v in vs:
                if isinstance(v, V):
                    out.extend(v.bufs)
            return out

        def A(x):
            return x.ap if isinstance(x, V) else x

        def mm(out, lhsT, rhs, start=True, stop=True, nochk=False):
            P.op(PE, lambda e: e.matmul(out.ap, lhsT=lhsT.ap, rhs=rhs.ap, start=start, stop=stop, skip_group_check=nochk),
                 reads=rb(lhsT, rhs), writes=out.bufs)

        def tr(out, in_, ident):
            P.op(PE, lambda e: e.matmul(out.ap, lhsT=in_.ap, rhs=ident.ap, start=True, stop=True),
                 reads=rb(in_, ident), writes=out.bufs)

        def act(out, in_, func, bias=0.0, scale=1.0, eng=ACT):
            P.op(ACT, lambda e: e.activation(out=out.ap, in_=in_.ap, func=func, bias=A(bias), scale=A(scale)),
                 reads=rb(in_, bias, scale), writes=out.bufs)

        EH = {DVE: "vector", POOL: "gpsimd"}

        def cp(eng, out, in_):
            if eng == ACT:
                P.op(ACT, lambda e: e.activation(out=out.ap, in_=in_.ap, func=AF.Identity), reads=rb(in_), writes=out.bufs)
            else:
                P.op(eng, lambda e: e.tensor_copy(out=out.ap, in_=in_.ap), reads=rb(in_), writes=out.bufs)

        def tt(eng, out, in0, in1, op):
            P.op(eng, lambda e: e.tensor_tensor(out=out.ap, in0=in0.ap, in1=in1.ap, op=op),
                 reads=rb(in0, in1), writes=out.bufs)

        def ts(eng, out, in0, s1, op0, s2=None, op1=None):
            if op1 is None:
                P.op(eng, lambda e: e.tensor_scalar(out=out.ap, in0=in0.ap, scalar1=A(s1), scalar2=None, op0=op0),
                     reads=rb(in0, s1), writes=out.bufs)
            else:
                P.op(eng, lambda e: e.tensor_scalar(out=out.ap, in0=in0.ap, scalar1=A(s1), scalar2=A(s2), op0=op0, op1=op1),
                     reads=rb(in0, s1, s2), writes=out.bufs)

        def stt(eng, out, in0, scalar, in1, op0, op1):
            P.op(eng, lambda e: e.scalar_tensor_tensor(out=out.ap, in0=in0.ap, scalar=A(scalar), in1=in1.ap, op0=op0, op1=op1),
                 reads=rb(in0, scalar, in1), writes=out.bufs)

        def recip(out, in_):
            P.op(DVE, lambda e: e.reciprocal(out=out.ap, in_=in_.ap), reads=rb(in_), writes=out.bufs)

        def memset(eng, out, val):
            P.op(eng, lambda e: e.memset(out.ap, val), writes=out.bufs)

        def dma(out_ap, in_ap, reads=(), writes=(), eng=SP):
            P.dma(lambda e: e.dma_start(out=out_ap, in_=in_ap, allow_slow_non_contiguous=True), reads=list(reads), writes=list(writes), eng=eng)

        def load(dst, src_ap, rbufs=()):
            dma(dst.ap, src_ap, reads=rbufs, writes=dst.bufs)

        def store(dst_ap, src, wbufs=()):
            dma(dst_ap, src.ap, reads=src.bufs, writes=list(wbufs), eng=POOL)

        CST = T("CST", [128, NCST]); PVt = T("PV", [128, NPV]); C3 = T("C3", [128, KC, 3]); HP = T("HP", [128, KC, 3])
        IDB = T("IDB", [128, 128], BF16); TRIB = T("TRIB", [128, 128], BF16); BO1B = T("BO1B", [128, 128], BF16)
        ONEB = T("ONEB", [128, 128], BF16); MSKB = T("MSKB", [128, 128], BF16)
        MOD = T("MOD", [128, 2, 3, 2, KC])
        GBC = [T("GBC0", [128, D]), T("GBC1", [128, D])]
        KGBC = T("KGBC", [128, 128])
        W1A1 = T("W1A1", [128, KC, 2, 64], BF16); W2A2 = T("W2A2", [64, 2, E], BF16)
        STG = T("STG", [128, 2048]); STB = T("STB", [128, 2048], BF16)
        XT = T("XT", [128, D]); XN = T("XN", [128, D])
        HT = T("HT", [128, KC, 129]); DXT = T("DXT", [128, KC, 128])
        MIX = [T(f"MIX{j}", [128, KC, 128], BF16) for j in range(6)]
        HWA = T("HWA", [64, 2, 128], BF16)
        WIN = [T(f"WIN{i}", [128, 4, KC, 128], BF16) for i in range(2)]
        TT = [T(f"T{i}", [128, 128]) for i in range(10)]
        KR = [T(f"KR{i}", [128, 2, 128], BF16) for i in range(2)]
        BTT = [T(f"BTT{i}", [128, 128], BF16) for i in range(2)]; KTT = [T(f"KTT{i}", [128, 128], BF16) for i in range(2)]
        T4B = T("T4B", [128, 128], BF16); T8B = T("T8B", [128, 128], BF16)
        VTOK = [T(f"VTOK{i}", [128, 128], BF16) for i in range(2)]; KTOK = [T(f"KTOK{i}", [128, 128], BF16) for i in range(2)]
        NBTOK = [T(f"NBTOK{i}", [128, 128], BF16) for i in range(2)]
        EGP = [T(f"EGP{i}", [128, 128]) for i in range(3)]; SZP = [T(f"SZP{i}", [128, 128]) for i in range(3)]
        BONV = [T(f"BONV{i}", [128, 128]) for i in range(3)]
        CY = T("CY", [128, 128]); CT2 = T("CT2", [128, 128]); CT5 = T("CT5", [128, 128]); PSS = T("PSS", [128, 64])
        MALL = T("MALL", [128, 2, 2, 2, 128], BF16)
        ARBT = T("ARBT", [128, 2, 128], BF16); AAKT = T("AAKT", [128, 2, 128], BF16); ARKT = T("ARKT", [128, 2, 128], BF16)
        XB = [T("XB0", [128, 128], BF16), T("XB1", [128, 128], BF16)]
        YG = T("YG", [128, 16, 128], BF16)
        WOUTQ = [T(f"WOUTQ{i}", [128, 16, 256], BF16) for i in range(2)]
        KVW = [T(f"KVW{i}", [128, KC, 512], BF16) for i in range(2)]
        XKT = T("XKT", [128, KC, 128], BF16); H1T = T("H1T", [128, KC, 128], BF16)
        SQ = T("SQ", [128, 512]); KOUT = [T(f"KOUT{i}", [128, 512]) for i in range(2)]
        VOUT = [T(f"VOUT{i}", [128, 512]) for i in range(2)]
        VBt = [T(f"VB{i}", [128, 512], BF16) for i in range(2)]
        KTST = [T(f"KTST{i}", [128, 4, 128], BF16) for i in range(2)]
        SS = T("SS", [128, 8]); SSQ = T("SSQ", [128, 2])
        SF = T("SF", [128, NPAIR, 64]); SBD = T("SBD", [128, NPAIR, 128], BF16)
        sf_b = [Buf(f"sf{i}") for i in range(NPAIR)]; sbd_b = [Buf(f"sbd{i}") for i in range(NPAIR)]
        SFp = lambda pr: V(SF.ap[:, pr, :], [sf_b[pr]])
        SBDp = lambda pr: V(SBD.ap[:, pr, :], [sbd_b[pr]])
        SF = V(SF.ap, sf_b); SBD = V(SBD.ap, sbd_b)
        KRM = [[T(f"KRM{p}{i}", [128, 128], BF16) for i in range(2)] for p in range(2)]
        BTM = [[T(f"BTM{p}{i}", [128, 128], BF16) for i in range(2)] for p in range(2)]
        KTM = [[T(f"KTM{p}{i}", [128, 128], BF16) for i in range(2)] for p in range(2)]
        BW = [T(f"BW{i}", [128, 2, KC, 128], BF16) for i in range(2)]
        QTP = [T(f"QTP{i}", [128, 128], BF16) for i in range(2)]
        SZQ = [T(f"SZQ{i}", [128, 128]) for i in range(2)]
        QE1 = T("QE1", [128, 128]); QE2 = T("QE2", [128, 128])
        NKMAX = SEQ // 128
        KTH = T("KTH", [128, SEQ], BF16); VH = T("VH", [128, NKMAX, 128], BF16)
        E1 = T("E1", [128, 4, 128]); SPB = [T("SPB0", [128, 4, 128], BF16), T("SPB1", [128, 4, 128], BF16)]
        ESt = T("ES", [128, 4, 128])
        AT = [T("AT0", [128, 4, 128], BF16), T("AT1", [128, 4, 128], BF16)]
        NONEB = T("NONEB", [128, 128], BF16)
        CB = T("CB", [128, 128])
        WST = T("WST", [64, 128])

        def pv(name, k=None, n=1):
            o = PVO[name] + (0 if k is None else k)
            return PVt[:, o:o + n]

        def cc(name, w=128):
            return CST[:, CO[name]:CO[name] + w]

        IDF = cc("ident")

        load(CST, cst); load(PVt, pvec); load(C3, c3T); load(HP, hprev)
        load(KGBC, rowv[2:3, 0:128].partition_broadcast(128))
        cp(DVE, IDB, IDF); cp(DVE, TRIB, cc("TriNeg")); cp(DVE, BO1B, cc("BO1")); cp(DVE, ONEB, cc("ones")); ts(DVE, NONEB, cc("ones"), -1.0, ALU.mult)
        cp(DVE, MSKB, CST[:, CO["MKcat"]:CO["MKcat"] + 128])
        ts(DVE, pv("nw0", 0, 16), pv("w0", 0, 16), -1.0, ALU.mult)
        ts(DVE, pv("na0", 0, 16), pv("a0", 0, 16), -1.0, ALU.mult)
        ts(DVE, pv("omka", 0, 16), pv("k_a", 0, 16), -1.0, ALU.mult, 1.0, ALU.add)
        memset(DVE, SF, 0.0); memset(POOL, SBD, 0.0)

        CBC = XN
        for layer, (adaw, normg, bname) in enumerate([(ada_a, "a_norm_g", "ada_b_a"), (ada_b, "b_norm_g", "ada_b_b")]):
            adav = adaw.rearrange("(k p) c -> p k c", p=128)
            stg3 = V(STG.ap.rearrange("p (k c) -> p k c", k=KC), STG.bufs)
            for cch in range(8):
                load(stg3, adav[:, :, cch * 256:(cch + 1) * 256])
                for bi in range(2):
                    blk = cch * 2 + bi
                    o = ps(7, 0)[:, blk * 3:(blk + 1) * 3]
                    for kc in range(KC):
                        mm(o, stg3[:, kc, bi * 128:(bi + 1) * 128], C3[:, kc, :], start=(kc == 0), stop=(kc == KC - 1))
            adp = V(ps(7, 0).ap[:, 0:48].rearrange("p (b s) -> p b s", s=3), ps(7, 0).bufs)
            for s in range(3):
                tt(DVE, MOD[:, layer, s, 1, :], adp[:, 0:8, s], pv(bname, 0, 8), ALU.add)
                tt(DVE, MOD[:, layer, s, 0, :], adp[:, 8:16, s], pv(bname, 8, 8), ALU.add)
                ts(DVE, MOD[:, layer, s, 0, :], MOD[:, layer, s, 0, :], 1.0, ALU.add)
                tt(DVE, MOD[:, layer, s, 0, :], MOD[:, layer, s, 0, :], pv(normg, 0, 8), ALU.mult)
            load(GBC[1], rowv[layer:layer + 1, :].partition_broadcast(128))
            cbc3 = V(CBC.ap.rearrange("p (k c) -> p k c", k=KC), CBC.bufs)
            for s in range(3):
                for kc in range(KC):
                    ts(DVE, cbc3[:, kc, :], cc("ones"), C3[:, kc, s:s + 1], ALU.mult)
                for cch in range(4):
                    load(stg3, adav[:, :, 2048 + cch * 256:2048 + (cch + 1) * 256])
                    o = ps(cch % 2, 0, 2)
                    for kc in range(KC):
                        mm(o, cbc3[:, kc, :], stg3[:, kc, :], start=(kc == 0), stop=(kc == KC - 1))
                    tt(DVE, GBC[0][:, cch * 256:(cch + 1) * 256], o, GBC[1][:, cch * 256:(cch + 1) * 256], ALU.add)
                store(GSC[layer, s], GBC[0], [bGSC])

        cvt_i = [0]

        def convert(src_ap, dst_ap, wb, ncol=2048, shape3=None):
            load(STG[:, 0:ncol], src_ap)
            eng = [DVE, ACT, POOL][cvt_i[0] % 3]
            cvt_i[0] += 1
            cp(eng, STB[:, 0:ncol], STG[:, 0:ncol])
            src = STB[:, 0:ncol]
            if shape3 is not None:
                src = V(src.ap.rearrange("p (a c) -> p a c", a=shape3), src.bufs)
            store(dst_ap, src, [wb])

        for kc in range(KC):
            rows = slice(kc * 128, (kc + 1) * 128)
            for j in range(4):
                convert(w_in[rows, j * E:(j + 1) * E], WINS.rearrange("a p j k c -> p a j k c")[:, :, j, kc, :], bWINS, shape3=16)
            for half in range(2):
                convert(kvw[rows, half * E:(half + 1) * E],
                        KVWS.rearrange("g p k c -> p g k c")[:, half * 4:(half + 1) * 4, kc, :], bKVWS, shape3=4)
            for qz in range(2):
                convert(bwin[rows, qz * E:(qz + 1) * E], BWS.rearrange("h p q k c -> p h q k c")[:, :, qz, kc, :], bBWS, shape3=16)
        for li, wo in enumerate([wout_a, wout_b]):
            for kc in range(16):
                convert(wo[kc * 128:(kc + 1) * 128, :], WOS[li].rearrange("h p k c -> p h k c")[:, :, kc, :], bWOS[li],
                        ncol=1024, shape3=2)
        for i, wsrc in enumerate([w1, a1]):
            load(V(STG.ap[:, 0:512].rearrange("p (k c) -> p k c", k=KC), STG.bufs), wsrc.rearrange("(k p) c -> p k c", p=128))
            cp(DVE, W1A1[:, :, i, :], V(STG.ap[:, 0:512].rearrange("p (k c) -> p k c", k=KC), STG.bufs))
        for i, wsrc in enumerate([w2, a2]):
            load(STG[0:64, :], wsrc)
            cp(DVE, W2A2[:, i, :], STG[0:64, :])

        if do_samples:
            for b in range(2):
                for blk in range(PAST // 128):
                    r0 = b * PAST + blk * 128
                    load(STG, cv[r0:r0 + 128, :])
                    cp([DVE, POOL][blk % 2], STB, STG)
                    store(VSS[b, blk * 128:(blk + 1) * 128, :], STB, [bVS[1 + b]])
                for h in range(NHEAD):
                    stg3 = V(STG.ap[:, 0:1024].rearrange("p (n c) -> p n c", n=8), STG.bufs)
                    load(stg3, ck[b * PAST:(b + 1) * PAST, h * 128:(h + 1) * 128].rearrange("(n p) c -> p n c", p=128))
                    for half in range(2):
                        o = ps(half, 0, 4)
                        for q in range(4):
                            tr(o[:, q * 128:(q + 1) * 128], stg3[:, half * 4 + q, :], IDF)
                        cp([ACT, DVE][half], STB[:, half * 512:(half + 1) * 512], o)
                    store(KTSS[b, h, :, 0:PAST], STB[:, 0:1024], [bKT[1 + b]])

        def gelem(i):
            return [ACT, DVE][i % 2]

        def token_tile(seq, t0, n, x_src, y_dst, k_dst, v_dst, KTS_s, VS_s, koff, first, last, wkv_dst, sh_dst, swkv_src):
            bKTs, bVSs = bKT[seq], bVS[seq]
            L = int(np.log2(n))
            load(XT[:n, :], x_src)
            if first:
                load(GBC[0], GSC[0, seq], [bGSC]); load(GBC[1], GSC[1, seq], [bGSC])
                cp(DVE, HT[:, :, 0], HP[:, :, seq])
                if swkv_src is None:
                    memset(DVE, SF, 0.0); memset(POOL, SBD, 0.0)
                else:
                    for g4 in range(4):
                        sg = V(STG.ap[0:64, :].rearrange("p (h k) -> p h k", h=32), STG.bufs)
                        if g4 == 0:
                            load(sg, swkv_src.rearrange("h v k -> v h k"))
                        o = ps(7, 0, 2)
                        for q in range(4):
                            pr = g4 * 4 + q
                            tr(o[:, q * 64:(q + 1) * 64], V(STG.ap[0:64, pr * 128:(pr + 1) * 128], STG.bufs), IDF[0:64, 0:64])
                        o3 = V(o.ap.rearrange("p (a v) -> p a v", v=64), o.bufs)
                        cp(DVE, SF[:, g4 * 4:(g4 + 1) * 4, :], o3)
                        if g4 == 0:
                            memset(POOL, SBD, 0.0)
                        cp(DVE, SBD[0:64, g4 * 4:(g4 + 1) * 4, 0:64], o3[0:64])
                        cp(DVE, SBD[64:128, g4 * 4:(g4 + 1) * 4, 64:128], o3[64:128])

            def rms_to_T(dsts):
                act(XN[:n, :], XT[:n, :], AF.Square)
                P.op(DVE, lambda e: e.reduce_sum(out=SSQ.ap[:n, 0:1], in_=XN.ap[:n, :], axis=AX.X), reads=XN.bufs, writes=SSQ.bufs)
                act(SSQ[:n, 1:2], SSQ[:n, 0:1], AF.Ln, bias=pv("eps_rms")[:n], scale=1.0 / D)
                act(SSQ[:n, 1:2], SSQ[:n, 1:2], AF.Exp, scale=-0.5)
                ts(DVE, XN[:n, :], XT[:n, :], SSQ[:n, 1:2], ALU.mult)
                for half in range(2):
                    bank = [7, 5][half]
                    for q in range(4):
                        kc = half * 4 + q
                        tr(ps(bank, q)[:, :n], XN[:n, kc * 128:(kc + 1) * 128], IDF[:n, :n])
                    for q in range(4):
                        kc = half * 4 + q
                        for (dst, sc, bi) in dsts:
                            if bi is None:
                                act(dst(kc), ps(bank, q)[:, :n], AF.Identity, scale=sc(kc))
                            else:
                                act(dst(kc), ps(bank, q)[:, :n], AF.Identity, bias=bi(kc), scale=sc(kc))

            rms_to_T([(lambda kc: HT[:, kc, 1:1 + n], lambda kc: MOD[:, 0, seq, 0, kc:kc + 1], lambda kc: MOD[:, 0, seq, 1, kc:kc + 1])])
            tt(DVE, DXT[:, :, :n], HT[:, :, 0:n], HT[:, :, 1:1 + n], ALU.subtract)
            for j, mun in enumerate(["mu_r", "mu_k", "mu_v", "mu_z", "mu_w", "mu_a"]):
                for kc in range(KC):
                    stt(DVE, MIX[j][:, kc, :n], DXT[:, kc, :n], pv(mun, kc), HT[:, kc, 1:1 + n], ALU.mult, ALU.add)
            if last and sh_dst is not None:
                store(sh_dst, HT[:, :, n], [bOUT])
            cp(DVE, HT[:, :, 0], HT[:, :, n])
            for i in range(2):
                o = ps(1, i)[0:64, :n]
                for kc in range(KC):
                    mm(o, W1A1[:, kc, i, :], MIX[4 + i][:, kc, :n], start=(kc == 0), stop=(kc == KC - 1))
            act(WST[:, :n], ps(1, 0)[0:64, :n], AF.Exp, scale=2.0)
            ts(DVE, WST[:, :n], WST[:, :n], 1.0, ALU.add)
            recip(WST[:, :n], WST[:, :n])
            ts(DVE, HWA[:, 0, :n], WST[:, :n], -2.0, ALU.mult, 1.0, ALU.add)
            cp(ACT, HWA[:, 1, :n], ps(1, 1)[0:64, :n])

            def wout_load(li, q):
                load(WOUTQ[q % 2], WOS[li][q // 2][:, :, (q % 2) * 256:(q % 2 + 1) * 256], [bWOS[li]])

            def genA(pr, par):
                Wt = WIN[pr % 2]
                if pr == 0:
                    load(Wt, WINS[pr], [bWINS])
                if pr + 1 < NPAIR:
                    load(WIN[(pr + 1) % 2], WINS[pr + 1], [bWINS])
                pc = slice(pr * 128, (pr + 1) * 128)
                for j in range(4):
                    for kc in range(KC):
                        mm(ps(0, j)[:, :n], Wt[:, j, kc, :], MIX[j][:, kc, :n], start=(kc == 0), stop=(kc == KC - 1))
                    yield
                for i in range(2):
                    mm(ps(1, i)[:, :n], W2A2[:, i, pc], HWA[:, i, :n])
                PRr, PRk, PRv, PRz = (ps(0, j)[:, :n] for j in range(4))
                T1, T2, T3, T5, T6, T7, T9, G, EGI, EGM = (t[:, :n] for t in TT[:10])
                EG = EGP[pr % 3][:, :n]
                TZ = SZP[pr % 3][:, :n]
                KRp, BTTp, KTTp = KR[par], BTT[par], KTT[par]
                pe = lambda nm: pv(nm, pr)
                act(T1, ps(1, 0)[:, :n], AF.Exp, bias=pe("nw0"), scale=-1.0)
                act(T2, ps(1, 1)[:, :n], AF.Exp, bias=pe("na0"), scale=-1.0)
                yield
                ts(DVE, T1, T1, 1.0, ALU.add); recip(T1, T1)
                ts(POOL, T2, T2, 1.0, ALU.add); recip(T2, T2)
                yield
                P.op(DVE, lambda e, G=G, T1=T1: e.tensor_tensor_scan(out=G.ap, data0=cc("ones")[:, :n].ap, data1=T1.ap, initial=0.0,
                                                                      op0=ALU.mult, op1=ALU.add), reads=rb(T1, CST), writes=G.bufs)
                act(EG, G, AF.Exp, scale=-DECAY_C)
                act(EGI, G, AF.Exp, scale=DECAY_C)
                tt(POOL, EGM, G, T1, ALU.subtract)
                act(EGM, EGM, AF.Exp, scale=-DECAY_C)
                yield
                ts(DVE, T3, PRk, pe("k_k"), ALU.mult)
                act(T4B[:, :n], T3, AF.Square)
                mm(ps(1, 2)[:, :n], BO1B, T4B[:, :n])
                yield
                act(T5, ps(1, 2)[:, :n], AF.Ln, bias=pv("eps_l2"))
                act(T5, T5, AF.Exp, scale=-0.5)
                tt(DVE, T3, T3, T5, ALU.mult)
                yield
                ts(POOL, T6, T2, pe("k_a"), ALU.mult, pe("omka"), ALU.add)
                tt(DVE, T6, PRk, T6, ALU.mult)
                tt(POOL, T7, T3, T2, ALU.mult)
                yield
                tt(DVE, KRp[:, 0, :n], T3, EGM, ALU.mult)
                tt(DVE, KRp[:, 1, :n], PRr, EG, ALU.mult)
                tt(POOL, BTTp[:, :n], T7, EGI, ALU.mult)
                tt(POOL, KTTp[:, :n], T6, EGI, ALU.mult)
                yield
                for hh in range(2):
                    hm = pv("hm%d" % hh)
                    ts(POOL, KRM[par][hh][:, :n], KRp[:, 0, :n], hm, ALU.mult)
                    ts(POOL, BTM[par][hh][:, :n], BTTp[:, :n], hm, ALU.mult)
                    ts(POOL, KTM[par][hh][:, :n], KTTp[:, :n], hm, ALU.mult)
                    yield
                stt(DVE, T8B[:, :n], PRr, pe("r_k"), T6, ALU.mult, ALU.mult)
                mm(ps(1, 3)[:, :n], BO1B, T8B[:, :n])
                cp(ACT, T9, PRv)
                yield
                tr(ps(6, 2)[:n, :], T9, IDF)
                tr(ps(6, 0)[:n, 0:128], KTTp[:, :n], IDB)
                tr(ps(6, 1)[:n, 0:128], BTTp[:, :n], IDB)
                tt(DVE, BONV[pr % 3][:, :n], ps(1, 3)[:, :n], T9, ALU.mult)
                yield
                cp(ACT, VTOK[par][:n, :], ps(6, 2)[:n, :])
                cp(DVE, KTOK[par][:n, :], ps(6, 0)[:n, 0:128])
                act(NBTOK[par][:n, :], ps(6, 1)[:n, 0:128], AF.Identity, scale=-1.0)
                yield
                act(TZ, PRz, AF.Exp, scale=-1.0)
                ts(DVE, TZ, TZ, 1.0, ALU.add); recip(TZ, TZ)
                tt(DVE, TZ, PRz, TZ, ALU.mult)
                yield

            def genB(pr, par):
                KRp, BTTp, KTTp = KR[par], BTT[par], KTT[par]
                VT, KT, NBT = VTOK[par], KTOK[par], NBTOK[par]
                pe = lambda nm: pv(nm, pr)
                psN = V(ps(4, 0, 2).ap.rearrange("p (h c) -> p h c", h=2), ps(4, 0, 2).bufs)
                psB = V(PB[2][:, :].rearrange("p (h s c) -> p h s c", h=2, s=2), PBb[2][:])
                psK = V(PB[3][:, :].rearrange("p (h s c) -> p h s c", h=2, s=2), PBb[3][:])
                for hh in range(2):
                    mm(psN[:n, hh, :n], KRM[par][hh][:, :n], BTTp[:, :n])
                    for s2 in range(2):
                        mm(psB[:n, hh, s2, :n], BTM[par][hh][:, :n], KRp[:, s2, :n])
                        mm(psK[:n, hh, s2, :n], KTM[par][hh][:, :n], KRp[:, s2, :n])
                bc = lambda name, off: V(CST.ap[:n, CO[name] + off:CO[name] + off + n].unsqueeze(1).to_broadcast([n, 2, n]), CST.bufs)
                tt(DVE, MALL[:n, 0, 0, :, :n], psN[:n, :, :n], bc("ML", 0), ALU.mult)
                tt(DVE, MALL[:n, 0, 1, :, :n], psB[:n, :, 0, :n], bc("MUcat", 0), ALU.mult)
                yield
                tt(DVE, ARBT[:n, :, :n], psB[:n, :, 1, :n], bc("MUcat", 128), ALU.mult)
                tt(DVE, AAKT[:n, :, :n], psK[:n, :, 0, :n], bc("MKcat", 0), ALU.mult)
                tt(DVE, ARKT[:n, :, :n], psK[:n, :, 1, :n], bc("MKcat", 128), ALU.mult)
                psX = ps(4, 2)
                for hh in range(2):
                    hs = slice(hh * 64, hh * 64 + 64)
                    mm(psX[:n, hs], KRp[:, 0, :n], SBDp(pr)[:, hs], start=(hh == 0), stop=False, nochk=True)
                    mm(psX[:n, hs], AAKT[:n, hh, :n], VT[:n, hs], start=False, stop=False, nochk=True)
                cp(ACT, XB[0][:n, :], psX[:n, :])
                yield
                psM = V(PB[5][:, :].rearrange("p (s h c) -> p s h c", s=2, h=2), PBb[5][:])
                for k in range(L):
                    for hh in range(2):
                        hs = slice(hh * 64, hh * 64 + 64)
                        mm(psX[:n, hs], MALL[:n, k % 2, 1, hh, :n], XB[k % 2][:n, hs], start=False, stop=(k == L - 1 and hh == 1), nochk=True)
                    if k < L - 1:
                        lastk = (k == L - 2)
                        for hh in range(2):
                            if not lastk:
                                mm(psM[:n, 0, hh, :n], MALL[:n, k % 2, 1, hh, :n], MALL[:n, k % 2, 0, hh, :n])
                            mm(psM[:n, 1, hh, :n], MALL[:n, k % 2, 0, hh, :n], MALL[:n, k % 2, 1, hh, :n])
                        if not lastk:
                            cp(ACT, MALL[:n, (k + 1) % 2, :, :, :n], psM[:n, :, :, :n])
                        else:
                            cp(ACT, MALL[:n, (k + 1) % 2, 1, :, :n], psM[:n, 1, :, :n])
                    cp(DVE, XB[(k + 1) % 2][:n, :], psX[:n, :])
                    yield
                U = XB[L % 2]
                psY = ps(4, 3)
                mm(psY[:, :n], SBDp(pr), KRp[:, 1, :n], start=True, stop=False, nochk=True)
                for hh in range(2):
                    hs = slice(hh * 64, hh * 64 + 64)
                    mm(psY[hs, :n], VT[:n, hs], ARKT[:n, hh, :n], start=False, stop=False, nochk=True)
                    mm(psY[hs, :n], U[:n, hs], ARBT[:n, hh, :n], start=False, stop=True, nochk=True)
                psS = ps(7, 1)
                for hh in range(2):
                    hs = slice(hh * 64, hh * 64 + 64)
                    mm(psS[hs, 0:64], KT[:n, hs], VT[:n, hs], start=True, stop=False)
                    mm(psS[hs, 0:64], NBT[:n, hs], U[:n, hs], start=False, stop=True)
                cp(ACT, CY[:, :n], psY[:, :n])
                cp(DVE, PSS[:, :], psS[:, 0:64])
                yield

            def genB2(pr, par):
                pe = lambda nm: pv(nm, pr)
                Y = CY[:, :n]
                C2 = CT2[:, :n]
                C5 = CT5[:, :n]
                sfp = SFp(pr)
                sbp = SBDp(pr)
                tt(DVE, sfp, PSS[:, :], sfp, ALU.add)
                ts(DVE, sfp, sfp, EGP[pr % 3][:, n - 1:n], ALU.mult)
                cp(ACT, sbp[0:64, 0:64], sfp[0:64, :])
                cp(ACT, sbp[64:128, 64:128], sfp[64:128, :])
                mm(ps(7, 2)[:, :n], cc("BO64"), Y)
                yield
                tt(DVE, Y, Y, ps(7, 2)[:, :n], ALU.subtract)
                act(C2, Y, AF.Square)
                mm(ps(7, 3)[:, :n], cc("BO64"), C2)
                yield
                act(C5, ps(7, 3)[:, :n], AF.Ln, bias=pv("eps_gn"))
                act(C5, C5, AF.Exp, scale=-0.5)
                tt(DVE, Y, Y, C5, ALU.mult)
                yield
                ts(DVE, Y, Y, pe("ln_g"), ALU.mult, pe("ln_b"), ALU.add)
                tt(POOL, Y, Y, BONV[pr % 3][:, :n], ALU.add)
                tt(DVE, YG[:, pr, :n], Y, SZP[pr % 3][:, :n], ALU.mult)
                yield

            def drain(g):
                for _ in g:
                    pass

            def interleave(*gens):
                gens = [g for g in gens if g is not None]
                while gens:
                    for g in list(gens):
                        try:
                            next(g)
                        except StopIteration:
                            gens.remove(g)

            wout_load(0, 0)
            npairs = 0 if 'pair' in os.environ.get('KSKIP', '') else NPAIR
            if npairs:
                drain(genA(0, 0))
            for pr in range(npairs):
                interleave(genB(pr, pr % 2),
                           genA(pr + 1, (pr + 1) % 2) if pr + 1 < NPAIR else None,
                           genB2(pr - 1, (pr - 1) % 2) if pr >= 1 else None)
            if npairs:
                drain(genB2(npairs - 1, (npairs - 1) % 2))

            if last and wkv_dst is not None:
                for g4 in range(4):
                    o = ps(7, 0, 4)
                    for q in range(4):
                        pr = g4 * 4 + q
                        tr(o[0:64, q * 128:(q + 1) * 128], SF[:, pr, :], IDF)
                    cp(ACT, STG[0:64, g4 * 512:(g4 + 1) * 512], o[0:64, :])
                store(wkv_dst.rearrange("h v k -> v h k"), V(STG.ap[0:64, :].rearrange("p (h k) -> p h k", h=32), STG.bufs), [bOUT])

            def out_proj(li, src):
                for q in range(4):
                    if q + 1 < 4:
                        wout_load(li, q + 1)
                    Wq = WOUTQ[q % 2]
                    o = ps(q % 2, 0, 2)
                    for pr in range(16):
                        mm(o[:n, :], src[:, pr, :n], Wq[:, pr, :], start=(pr == 0), stop=(pr == 15))
                    hc = slice(q * 256, (q + 1) * 256)
                    tt(DVE, XN[:n, hc], o[:n, :], GBC[li][:n, hc], ALU.mult)
                    tt(POOL, XT[:n, hc], XT[:n, hc], XN[:n, hc], ALU.add)

            out_proj(0, YG)

            rms_to_T([(lambda kc: XKT[:, kc, :n], lambda kc: pv("kv_norm_g", kc), None),
                      (lambda kc: H1T[:, kc, :n], lambda kc: MOD[:, 1, seq, 0, kc:kc + 1], lambda kc: MOD[:, 1, seq, 1, kc:kc + 1])])
            load(KVW[0], KVWS[0], [bKVWS])
            for cg in range(8):
                Wk = KVW[cg % 2]
                if cg + 1 < 8:
                    load(KVW[(cg + 1) % 2], KVWS[cg + 1], [bKVWS])
                o = ps(2 + cg % 2, 0, 4)
                for kc in range(KC):
                    mm(o[:n, :], XKT[:, kc, :n], Wk[:, kc, :], start=(kc == 0), stop=(kc == KC - 1))
                cs = slice((cg % 4) * 512, (cg % 4 + 1) * 512)
                if cg < 4:
                    ko = KOUT[cg % 2]
                    act(SQ[:n, :], o[:n, :], AF.Square)
                    P.op(DVE, lambda e, cg=cg: e.reduce_sum(out=SS.ap[:n, 0:4], in_=SQ.ap[:n, :].rearrange("p (h c) -> p h c", h=4), axis=AX.X),
                         reads=SQ.bufs, writes=SS.bufs)
                    act(SS[:n, 4:8], SS[:n, 0:4], AF.Ln, bias=pv("eps_rms")[:n], scale=1.0 / 128)
                    act(SS[:n, 4:8], SS[:n, 4:8], AF.Exp, scale=-0.5)
                    o3 = V(o.ap[:n, :].rearrange("p (h c) -> p h c", h=4), o.bufs)
                    ko3 = V(ko.ap[:n, :].rearrange("p (h c) -> p h c", h=4), ko.bufs)
                    tt(DVE, ko3, o3, V(SS.ap[:n, 4:8].unsqueeze(2).to_broadcast([n, 4, 128]), SS.bufs), ALU.mult)
                    tt(POOL, ko3, ko3, V(KGBC.ap[:n, :].unsqueeze(1).to_broadcast([n, 4, 128]), KGBC.bufs), ALU.mult)
                    store(k_dst[:, cs], ko[:n, :], [bOUT])
                    kst = KTST[cg % 2]
                    ob = ps(7, 0, 4)
                    for q in range(4):
                        tr(ob[:, q * 128:q * 128 + n], ko[:n, q * 128:(q + 1) * 128], IDF[:n, :n])
                    cp(ACT, kst[:, :, :n], V(ob.ap.rearrange("p (h c) -> p h c", h=4)[:, :, :n], ob.bufs))
                    store(KTS_s[cg * 4:(cg + 1) * 4, :, koff + t0:koff + t0 + n].rearrange("h p c -> p h c"), kst[:, :, :n], [bKTs])
                else:
                    vo = VOUT[cg % 2]
                    cp(ACT, vo[:n, :], o[:n, :])
                    cp(DVE, VBt[cg % 2][:n, :], o[:n, :])
                    store(v_dst[:, cs], vo[:n, :], [bOUT])
                    store(VS_s[koff + t0:koff + t0 + n, cs], VBt[cg % 2][:n, :], [bVSs])

            nk_total = koff + t0 + n
            nkb = (nk_total + 127) // 128
            OG = YG
            wout_load(1, 0)
            def genQ(h, par):
                Bw = BW[h % 2]
                if h == 0:
                    load(Bw, BWS[h], [bBWS])
                if h + 1 < NHEAD:
                    load(BW[(h + 1) % 2], BWS[h + 1], [bBWS])
                psQ, psZg, psSS = ps(par, 0), ps(par, 1), ps(par, 2)
                for kc in range(KC):
                    mm(psQ[:, :n], Bw[:, 0, kc, :], H1T[:, kc, :n], start=(kc == 0), stop=(kc == KC - 1))
                yield
                for kc in range(KC):
                    mm(psZg[:, :n], Bw[:, 1, kc, :], H1T[:, kc, :n], start=(kc == 0), stop=(kc == KC - 1))
                q1, q2 = QE1[:, :n], QE2[:, :n]
                act(q1, psQ[:, :n], AF.Square)
                yield
                mm(psSS[:, :n], cc("ones"), q1)
                act(q2, psSS[:, :n], AF.Ln, bias=pv("eps_rms"), scale=1.0 / 128)
                yield
                act(q2, q2, AF.Exp, scale=-0.5)
                ts(DVE, q2, q2, pv("q_gain"), ALU.mult, 128 ** -0.5, ALU.mult)
                yield
                tt(DVE, QTP[par][:, :n], psQ[:, :n], q2, ALU.mult)
                sz = SZQ[par][:, :n]
                act(sz, psZg[:, :n], AF.Exp, scale=-1.0)
                yield
                ts(DVE, sz, sz, 1.0, ALU.add); recip(sz, sz)
                tt(DVE, sz, psZg[:, :n], sz, ALU.mult)
                yield

            def genAtt(h, par):
                QT = QTP[par]
                psO = ps(4, 0)
                load(KTH[:, 0:nk_total], KTS_s[h, :, 0:nk_total], [bKTs])
                nfull = nk_total // 128
                if nfull > 0:
                    load(VH[:, 0:nfull, :], VS_s[0:nfull * 128, h * 128:(h + 1) * 128].rearrange("(b p) c -> p b c", p=128), [bVSs])
                rem = nk_total - nfull * 128
                if rem > 0:
                    load(VH[0:rem, nfull, :], VS_s[nfull * 128:nk_total, h * 128:(h + 1) * 128], [bVSs])
                memset(POOL, CB[:, :n], 0.0)
                blocks = list(range(nkb - 1, -1, -1))
                groups = [[blocks[0]]] + [blocks[i:i + 4] for i in range(1, len(blocks), 4)]
                ng = len(groups)
                PZB = [5, 6, 7]

                def pzv(g):
                    return V(PB[PZB[g % 3]][:, :].rearrange("p (a c) -> p a c", a=4), PBb[PZB[g % 3]][:])

                def S1(g):
                    pz = pzv(g)
                    for i, kb in enumerate(groups[g]):
                        ks = min(128, nk_total - kb * 128)
                        mm(pz[:ks, i, :n], KTH[:, kb * 128:kb * 128 + ks], QT[:, :n], start=(i == 0), stop=False, nochk=True)

                def S2(g):
                    pz = pzv(g)
                    G = len(groups[g])
                    kb0 = groups[g][0]
                    ks = min(128, nk_total - kb0 * 128)
                    sp = SPB[g % 2]
                    act(E1[:ks, 0:G, :n], pz[:ks, 0:G, :n], AF.Exp)
                    act(sp[:ks, 0:G, :n], E1[:ks, 0:G, :n], AF.Ln, bias=pv("one")[:ks])
                    if g == 0:
                        tt(POOL, sp[:ks, 0, :n], sp[:ks, 0, :n], MSKB[:ks, :n], ALU.mult)

                def S3(g):
                    pz = pzv(g)
                    G = len(groups[g])
                    kb0 = groups[g][0]
                    ks = min(128, nk_total - kb0 * 128)
                    sp = SPB[g % 2]
                    at = AT[g % 2]
                    for i in range(G):
                        mm(pz[:ks, i, :n], TRIB[:ks, :ks], sp[:ks, i, :n], start=False, stop=False, nochk=True)
                        for i2 in range(i):
                            mm(pz[:ks, i, :n], NONEB[:ks, :ks], sp[:ks, i2, :n], start=False, stop=False, nochk=True)
                    lastg = (g == ng - 1)
                    if not lastg:
                        pcb = ps(2 + g % 2, 0)
                        for i in range(G):
                            mm(pcb[:, :n], ONEB[:ks, :], sp[:ks, i, :n], start=(i == 0), stop=(i == G - 1))
                    cbb = V(CB.ap[:ks, :n].unsqueeze(1).to_broadcast([ks, G, n]), CB.bufs)
                    tt(DVE, ESt[:ks, 0:G, :n], pz[:ks, 0:G, :n], cbb, ALU.subtract)
                    act(at[:ks, 0:G, :n], ESt[:ks, 0:G, :n], AF.Exp)
                    if g == 0:
                        tt(POOL, at[:ks, 0, :n], at[:ks, 0, :n], MSKB[:ks, :n], ALU.mult)
                    if not lastg:
                        tt(DVE, CB[:, :n], CB[:, :n], pcb[:, :n], ALU.add)
                    for i, kb in enumerate(groups[g]):
                        mm(psO[:, :n], VH[:ks, kb, :], at[:ks, i, :n], start=(g == 0 and i == 0), stop=(lastg and i == G - 1), nochk=True)

                S1(0)
                if ng > 1:
                    S1(1)
                S2(0)
                yield
                for g in range(ng):
                    if g + 2 < ng:
                        S1(g + 2)
                    if g + 1 < ng:
                        S2(g + 1)
                    S3(g)
                    yield
                tt(DVE, OG[:, h, :n], psO[:, :n], SZQ[par][:, :n], ALU.mult)
                yield

            nheads = 0 if 'att' in os.environ.get('KSKIP', '') else NHEAD
            if nheads:
                drain(genQ(0, 0))
            for h in range(nheads):
                interleave(genAtt(h, h % 2), genQ(h + 1, (h + 1) % 2) if h + 1 < NHEAD else None)
            out_proj(1, OG)
            store(y_dst, XT[:n, :], [bOUT])

        if do_samples:
            for b in range(2):
                r = slice(b * DSEQ, (b + 1) * DSEQ)
                token_tile(1 + b, 0, DSEQ, xs[r, :], ys[r, :], kso[r, :], vso[r, :], KTSS[b], VSS[b], PAST, True, True,
                           wkvs[b], shs[b], swkv[b])
        for ti in range(n_prompt_tiles):
            r = slice(ti * 128, (ti + 1) * 128)
            token_tile(0, ti * 128, 128, xp[r, :], yp[r, :], kp[r, :], vp[r, :], KTSP, VSP, 0, ti == 0, ti == n_prompt_tiles - 1,
                       wkvp, shp, None)

        P.final_all()
        nse = [max(1, (c + EPOCH - 1) // EPOCH) for c in P.cnt]
        esems = [[es.enter_context(nc.semaphore(f"s{ENG_NAMES[i]}{k}")) for k in range(nse[i])] for i in range(5)]
        dsems = [es.enter_context(nc.semaphore(f"d{k}")) for k in range(NDMA)]
        block = es.enter_context(nc.Block())
        P.emit(block, esems, dsems)
    return nc, P


_CACHE = {}


def make_in_maps(inp, n_cores=8):
    f = lambda a: np.ascontiguousarray(np.asarray(a, np.float32))
    cst = make_consts()
    pvec = make_pvec(inp)
    rowv = np.zeros((3, D), np.float32)
    rowv[0] = np.asarray(inp["a_ada_b"][0][2 * D:3 * D], np.float32)
    rowv[1] = np.asarray(inp["b_ada_b"][0][2 * D:3 * D], np.float32)
    rowv[2, :128] = np.asarray(inp["k_gain"], np.float32)
    shared = {
        "pvec": pvec, "cst": cst, "rowv": rowv,
        "ada_a": f(inp["a_ada_w"][0]), "ada_b": f(inp["b_ada_w"][0]),
        "w_in": f(inp["a_w_in"][0]), "w1": f(inp["a_w1"][0]), "a1": f(inp["a_a1"][0]),
        "w2": f(inp["a_w2"][0]), "a2": f(inp["a_a2"][0]), "wout_a": f(inp["a_w_out"][0]),
        "kvw": f(inp["kv_w"]), "bwin": f(inp["b_w_in"][0]), "wout_b": f(inp["b_w_out"][0]),
    }
    maps = []
    for i in range(n_cores):
        c3 = np.stack([np.asarray(inp["c_prompt"][i], np.float32),
                       np.asarray(inp["c_sample"][2 * i], np.float32),
                       np.asarray(inp["c_sample"][2 * i + 1], np.float32)], axis=0)
        c3T = np.ascontiguousarray(c3.reshape(3, KC, 128).transpose(2, 1, 0))
        hp = np.zeros((3, D), np.float32)
        hp[1] = inp["state_shift"][0, 2 * i]
        hp[2] = inp["state_shift"][0, 2 * i + 1]
        hpT = np.ascontiguousarray(hp.reshape(3, KC, 128).transpose(2, 1, 0))
        m = dict(shared)
        m.update({
            "xp": f(inp["x_prompt"][i]),
            "xs": f(np.asarray(inp["x_sample"][2 * i:2 * i + 2]).reshape(2 * DSEQ, D)),
            "ck": f(np.asarray(inp["cache_k"][2 * i:2 * i + 2]).reshape(2 * PAST, E)),
            "cv": f(np.asarray(inp["cache_v"][2 * i:2 * i + 2]).reshape(2 * PAST, E)),
            "swkv": f(inp["state_wkv"][0, 2 * i:2 * i + 2]),
            "hprev": hpT, "c3T": c3T,
        })
        maps.append(m)
    return maps


def kernel(**inputs):
    n = 8
    if "nc" not in _CACHE:
        _CACHE["nc"] = build_program()[0]
    nc = _CACHE["nc"]
    maps = make_in_maps(inputs, n)
    res = run_bass_kernel_spmd(nc, maps, core_ids=list(range(n)))
    R = res.results
    B, BD = 8, 16
    y_prompt = np.stack([R[i]["yp"] for i in range(n)], 0).astype(np.float32)
    y_sample = np.concatenate([R[i]["ys"].reshape(2, DSEQ, D) for i in range(n)], 0).astype(np.float32)
    k_prompt = np.stack([R[i]["kp"].reshape(SEQ, 16, 128) for i in range(n)], 0).astype(np.float32)
    v_prompt = np.stack([R[i]["vp"].reshape(SEQ, 16, 128) for i in range(n)], 0).astype(np.float32)
    wkv_prompt = np.stack([R[i]["wkvp"] for i in range(n)], 0)[None].astype(np.float32)
    shift_prompt = np.stack([R[i]["shp"].T.reshape(D) for i in range(n)], 0)[None].astype(np.float32)
    k_sample = np.concatenate([R[i]["kso"].reshape(2, DSEQ, 16, 128) for i in range(n)], 0).astype(np.float32)
    v_sample = np.concatenate([R[i]["vso"].reshape(2, DSEQ, 16, 128) for i in range(n)], 0).astype(np.float32)
    wkv_sample = np.concatenate([R[i]["wkvs"] for i in range(n)], 0)[None].astype(np.float32)
    shift_sample = np.concatenate([np.stack([R[i]["shs"][b].T.reshape(D) for b in range(2)], 0) for i in range(n)], 0)[None].astype(np.float32)
    return (y_prompt, y_sample, k_prompt, v_prompt, wkv_prompt, shift_prompt,
            k_sample, v_sample, wkv_sample, shift_sample)
```

```python
import os
import numpy as np
from contextlib import ExitStack
import concourse.bass as bass
import concourse.mybir as mybir
from concourse.bass_utils import run_bass_kernel_spmd

F32 = mybir.dt.float32
BF16 = mybir.dt.bfloat16
ALU = mybir.AluOpType
AF = mybir.ActivationFunctionType
AX = mybir.AxisListType

PE, ACT, DVE, POOL, SP = 0, 1, 2, 3, 4
ENG_NAMES = ["tensor", "scalar", "vector", "gpsimd", "sync"]
EPOCH = 20000
NDMA = 24

D = 1024
E = 2048
SEQ = 4096
DSEQ = 32
PAST = 1024
NPAIR = 16
NHEAD = 16
KC = 8
DECAY_C = 0.6065306597126334


class Buf:
    __slots__ = ("name", "w", "r", "excl")

    def __init__(self, name="", excl=False):
        self.name = name
        self.w = {}
        self.r = {}
        self.excl = excl


class V:
    __slots__ = ("ap", "bufs")

    def __init__(self, ap, bufs):
        self.ap = ap
        self.bufs = bufs

    def __getitem__(self, k):
        return V(self.ap[k], self.bufs)


class Prog:
    def __init__(self):
        self.ops = [[] for _ in range(5)]
        self.cnt = [0] * 5
        self.seen = [dict() for _ in range(5)]
        self.dma_i = 0
        self.dma_cnt = [0] * NDMA
        self.total = 0
        self.limit = None
        self.needed = [set() for _ in range(5)]
        self.log = []

    def _need(self, eng, key, ticket, waits):
        if key == PE and eng == PE:
            return
        s = self.seen[eng]
        if s.get(key, 0) >= ticket:
            return
        s[key] = ticket
        waits.append((key, ticket))
        if not isinstance(key, tuple):
            self.needed[key].add(ticket)

    def _deps(self, eng, reads, writes, waits):
        for b in reads:
            for k, t in b.w.items():
                self._need(eng, k, t, waits)
            if b.excl:
                for k, t in b.r.items():
                    if k != eng:
                        self._need(eng, k, t, waits)
        for b in writes:
            for k, t in b.w.items():
                self._need(eng, k, t, waits)
            for k, t in b.r.items():
                self._need(eng, k, t, waits)

    def op(self, eng, fn, reads=(), writes=()):
        self.total += 1
        if self.limit is not None and self.total > self.limit:
            return
        waits = []
        self._deps(eng, reads, writes, waits)
        self.cnt[eng] += 1
        t = self.cnt[eng]
        self.ops[eng].append([waits, fn, t, False])
        for b in reads:
            b.r[eng] = t
        for b in writes:
            b.w = {eng: t}
            b.r = {}

    def dma(self, fn, reads=(), writes=(), eng=SP):
        self.total += 1
        if self.limit is not None and self.total > self.limit:
            return
        slot = self.dma_i % NDMA
        self.dma_i += 1
        key = ("d", slot)
        waits = []
        if self.dma_cnt[slot] > 0:
            self._need(eng, key, self.dma_cnt[slot], waits)
        self._deps(eng, reads, writes, waits)
        self.dma_cnt[slot] += 1
        t = self.dma_cnt[slot]
        self.ops[eng].append([waits, fn, ("d", slot, t), True])
        for b in reads:
            b.r[key] = t
        for b in writes:
            b.w = {key: t}
            b.r = {}

    def final_all(self, eng=SP):
        waits = []
        for slot in range(NDMA):
            if self.dma_cnt[slot] > 0:
                waits.append((("d", slot), self.dma_cnt[slot]))
        for k in range(4):
            if self.cnt[k] > 0:
                waits.append((k, self.cnt[k]))
                self.needed[k].add(self.cnt[k])
        self.ops[eng].append([waits, None, None, False])

    def emit(self, block, esems, dsems):
        prog = self

        cm = [{t: i + 1 for i, t in enumerate(sorted(self.needed[k]))} for k in range(5)]

        def semval(key, t):
            if isinstance(key, tuple):
                return dsems[key[1]], 16 * t
            c = cm[key][t]
            ep = (c - 1) // EPOCH
            return esems[key][ep], c - ep * EPOCH

        def run(eng_idx):
            def body(e):
                for waits, fn, sig, isdma in prog.ops[eng_idx]:
                    for key, t in waits:
                        s, v = semval(key, t)
                        e.wait_ge(s, v)
                    if fn is None:
                        continue
                    ins = fn(e)
                    if isdma:
                        ins.then_inc(dsems[sig[1]], 16)
                    elif sig in cm[eng_idx]:
                        c = cm[eng_idx][sig]
                        ep = (c - 1) // EPOCH
                        ins.then_inc(esems[eng_idx][ep], 1)
            return body

        block.tensor(run(PE))
        block.scalar(run(ACT))
        block.vector(run(DVE))
        block.gpsimd(run(POOL))
        block.sync(run(SP))


PV_E = ["w0", "a0", "k_k", "k_a", "r_k", "ln_g", "ln_b"]
PV_D = ["a_norm_g", "mu_r", "mu_k", "mu_v", "mu_z", "mu_w", "mu_a", "kv_norm_g", "b_norm_g"]
PVO = {}
_o = 0
for _n in PV_E:
    PVO[_n] = _o
    _o += 16
for _n in PV_D:
    PVO[_n] = _o
    _o += 8
PVO["ada_b_a"] = _o; _o += 24
PVO["ada_b_b"] = _o; _o += 24
PVO["k_gain"] = _o; _o += 1
PVO["q_gain"] = _o; _o += 1
PVO["eps_rms"] = _o; _o += 1
PVO["eps_gn"] = _o; _o += 1
PVO["eps_l2"] = _o; _o += 1
PVO["one"] = _o; _o += 1
PVO["hm0"] = _o; _o += 1
PVO["hm1"] = _o; _o += 1
PVO["nw0"] = _o; _o += 16
PVO["na0"] = _o; _o += 16
PVO["omka"] = _o; _o += 16
NPV = _o

CO = {"ident": 0, "ML": 128, "MUcat": 256, "MKcat": 512, "TriNeg": 768, "BO64": 896, "BO1": 1024, "ones": 1152}
NCST = 1280


def _fm(v, ncol):
    return np.ascontiguousarray(np.asarray(v, np.float32).reshape(ncol, 128).T)


def make_consts():
    c = np.zeros((128, NCST), np.float32)
    i = np.arange(128)
    c[:, 0:128] = np.eye(128, dtype=np.float32)
    c[:, 128:256] = -(i[None, :] < i[:, None]).astype(np.float32)
    up_strict = (i[:, None] < i[None, :]).astype(np.float32)
    up_incl = (i[:, None] <= i[None, :]).astype(np.float32)
    c[:, 256:384] = -up_strict
    c[:, 384:512] = -up_incl
    c[:, 512:640] = up_strict
    c[:, 640:768] = up_incl
    c[:, 768:896] = -(i[:, None] >= i[None, :]).astype(np.float32)
    blk = (i[:, None] // 64 == i[None, :] // 64).astype(np.float32)
    c[:, 896:1024] = blk / 64.0
    c[:, 1024:1152] = blk
    c[:, 1152:1280] = 1.0
    return c


def make_pvec(inp):
    pv = np.zeros((128, NPV), np.float32)
    src_e = {"w0": inp["a_w0"][0], "a0": inp["a_a0"][0], "k_k": inp["a_k_k"][0], "k_a": inp["a_k_a"][0],
             "r_k": inp["a_r_k"][0].reshape(-1), "ln_g": inp["a_ln_g"][0], "ln_b": inp["a_ln_b"][0]}
    for n in PV_E:
        pv[:, PVO[n]:PVO[n] + 16] = _fm(src_e[n], 16)
    mu = inp["a_mu_in"][0]
    src_d = {"a_norm_g": inp["a_norm_g"][0], "mu_r": mu[0], "mu_k": mu[1], "mu_v": mu[2], "mu_z": mu[3],
             "mu_w": inp["a_mu_w"][0], "mu_a": inp["a_mu_a"][0], "kv_norm_g": inp["kv_norm_g"],
             "b_norm_g": inp["b_norm_g"][0]}
    for n in PV_D:
        pv[:, PVO[n]:PVO[n] + 8] = _fm(src_d[n], 8)
    pv[:, PVO["ada_b_a"]:PVO["ada_b_a"] + 24] = _fm(inp["a_ada_b"][0], 24)
    pv[:, PVO["ada_b_b"]:PVO["ada_b_b"] + 24] = _fm(inp["b_ada_b"][0], 24)
    pv[:, PVO["k_gain"]] = np.asarray(inp["k_gain"], np.float32)
    pv[:, PVO["q_gain"]] = np.asarray(inp["b_q_gain"][0], np.float32)
    pv[:, PVO["eps_rms"]] = 1e-6
    pv[:, PVO["eps_gn"]] = 64e-5
    pv[:, PVO["eps_l2"]] = 1e-12
    pv[:, PVO["one"]] = 1.0
    pv[:64, PVO["hm0"]] = 1.0
    pv[64:, PVO["hm1"]] = 1.0
    return pv


def build_program(n_prompt_tiles=32, do_samples=True, limit=None):
    nc = bass.Bass("TRN2", target_bir_lowering=False)
    P = Prog()
    P.limit = limit

    def din(name, shape, dt=F32):
        return nc.dram_tensor(name, list(shape), dt, kind="ExternalInput").ap()

    def dout(name, shape, dt=F32):
        return nc.dram_tensor(name, list(shape), dt, kind="ExternalOutput").ap()

    def dscr(name, shape, dt=BF16):
        return nc.dram_tensor(name, list(shape), dt, kind="Internal").ap()

    xp = din("xp", [SEQ, D]); xs = din("xs", [2 * DSEQ, D])
    ck = din("ck", [2 * PAST, E]); cv = din("cv", [2 * PAST, E])
    swkv = din("swkv", [2, 32, 64, 64])
    hprev = din("hprev", [128, KC, 3]); c3T = din("c3T", [128, KC, 3])
    pvec = din("pvec", [128, NPV]); cst = din("cst", [128, NCST])
    rowv = din("rowv", [3, D])
    ada_a = din("ada_a", [D, 3 * D]); ada_b = din("ada_b", [D, 3 * D])
    w_in = din("w_in", [D, 4 * E]); w1 = din("w1", [D, 64]); a1 = din("a1", [D, 64])
    w2 = din("w2", [64, E]); a2 = din("a2", [64, E])
    wout_a = din("wout_a", [E, D]); kvw = din("kvw", [D, 2 * E]); bwin = din("bwin", [D, 2 * E])
    wout_b = din("wout_b", [E, D])

    yp = dout("yp", [SEQ, D]); ys = dout("ys", [2 * DSEQ, D])
    kp = dout("kp", [SEQ, E]); vp = dout("vp", [SEQ, E])
    wkvp = dout("wkvp", [32, 64, 64]); shp = dout("shp", [128, KC])
    kso = dout("kso", [2 * DSEQ, E]); vso = dout("vso", [2 * DSEQ, E])
    wkvs = dout("wkvs", [2, 32, 64, 64]); shs = dout("shs", [2, 128, KC])

    WINS = dscr("WINS", [NPAIR, 128, 4, KC, 128])
    KVWS = dscr("KVWS", [8, 128, KC, 512])
    BWS = dscr("BWS", [NHEAD, 128, 2, KC, 128])
    WOS = [dscr("WOSa", [2, 128, 16, 512]), dscr("WOSb", [2, 128, 16, 512])]
    GSC = dscr("GSC", [2, 3, 128, D], F32)
    KTSP = dscr("KTSP", [NHEAD, 128, SEQ]); VSP = dscr("VSP", [SEQ, E])
    KTSS = dscr("KTSS", [2, NHEAD, 128, PAST + DSEQ]); VSS = dscr("VSS", [2, PAST + DSEQ, E])
    bWINS, bKVWS, bBWS, bWOS, bGSC = Buf(), Buf(), Buf(), [Buf(), Buf()], Buf()
    bKT = [Buf(), Buf(), Buf()]
    bVS = [Buf(), Buf(), Buf()]
    bOUT = Buf("outs")

    es = ExitStack()
    with es:
        def T(name, shape, dt=F32):
            t = es.enter_context(nc.sbuf_tensor(name, list(shape), dt))
            return V(t[tuple(slice(None) for _ in shape)], [Buf(name)])

        PB = []
        PBb = []
        for i in range(8):
            t = es.enter_context(nc.psum_tensor(f"pb{i}", [128, 512], F32))
            PB.append(t)
            PBb.append([Buf(f"pb{i}", excl=True)] * 4)

        def ps(i, s0, ns=1):
            w = 128
            return V(PB[i][:, s0 * w:(s0 + ns) * w], PBb[i][s0:s0 + ns])

        def ps4(i, inner):
            return V(PB[i][:, :].rearrange("p (a b) -> p a b", b=inner), PBb[i][:])

        def rb(*vs):
            out = []
            for v in vs:
                if isinstance(v, V):
                    out.extend(v.bufs)
            return out

        def A(x):
            return x.ap if isinstance(x, V) else x

        def mm(out, lhsT, rhs, start=True, stop=True, nochk=False):
            P.op(PE, lambda e: e.matmul(out.ap, lhsT=lhsT.ap, rhs=rhs.ap, start=start, stop=stop, skip_group_check=nochk),
                 reads=rb(lhsT, rhs), writes=out.bufs)

        def tr(out, in_, ident):
            P.op(PE, lambda e: e.matmul(out.ap, lhsT=in_.ap, rhs=ident.ap, start=True, stop=True),
                 reads=rb(in_, ident), writes=out.bufs)

        def act(out, in_, func, bias=0.0, scale=1.0, eng=ACT):
            P.op(ACT, lambda e: e.activation(out=out.ap, in_=in_.ap, func=func, bias=A(bias), scale=A(scale)),
                 reads=rb(in_, bias, scale), writes=out.bufs)

        EH = {DVE: "vector", POOL: "gpsimd"}

        def cp(eng, out, in_):
            if eng == ACT:
                P.op(ACT, lambda e: e.activation(out=out.ap, in_=in_.ap, func=AF.Identity), reads=rb(in_), writes=out.bufs)
            else:
                P.op(eng, lambda e: e.tensor_copy(out=out.ap, in_=in_.ap), reads=rb(in_), writes=out.bufs)

        def tt(eng, out, in0, in1, op):
            P.op(eng, lambda e: e.tensor_tensor(out=out.ap, in0=in0.ap, in1=in1.ap, op=op),
                 reads=rb(in0, in1), writes=out.bufs)

        def ts(eng, out, in0, s1, op0, s2=None, op1=None):
            if op1 is None:
                P.op(eng, lambda e: e.tensor_scalar(out=out.ap, in0=in0.ap, scalar1=A(s1), scalar2=None, op0=op0),
                     reads=rb(in0, s1), writes=out.bufs)
            else:
                P.op(eng, lambda e: e.tensor_scalar(out=out.ap, in0=in0.ap, scalar1=A(s1), scalar2=A(s2), op0=op0, op1=op1),
                     reads=rb(in0, s1, s2), writes=out.bufs)

        def stt(eng, out, in0, scalar, in1, op0, op1):
            P.op(eng, lambda e: e.scalar_tensor_tensor(out=out.ap, in0=in0.ap, scalar=A(scalar), in1=in1.ap, op0=op0, op1=op1),
                 reads=rb(in0, scalar, in1), writes=out.bufs)

        def recip(out, in_):
            P.op(DVE, lambda e: e.reciprocal(out=out.ap, in_=in_.ap), reads=rb(in_), writes=out.bufs)

        def memset(eng, out, val):
            P.op(eng, lambda e: e.memset(out.ap, val), writes=out.bufs)

        def dma(out_ap, in_ap, reads=(), writes=(), eng=SP):
            P.dma(lambda e: e.dma_start(out=out_ap, in_=in_ap, allow_slow_non_contiguous=True), reads=list(reads), writes=list(writes), eng=eng)

        def load(dst, src_ap, rbufs=()):
            dma(dst.ap, src_ap, reads=rbufs, writes=dst.bufs)

        def store(dst_ap, src, wbufs=()):
            dma(dst_ap, src.ap, reads=src.bufs, writes=list(wbufs), eng=POOL)

        CST = T("CST", [128, NCST]); PVt = T("PV", [128, NPV]); C3 = T("C3", [128, KC, 3]); HP = T("HP", [128, KC, 3])
        IDB = T("IDB", [128, 128], BF16); TRIB = T("TRIB", [128, 128], BF16); BO1B = T("BO1B", [128, 128], BF16)
        ONEB = T("ONEB", [128, 128], BF16); MSKB = T("MSKB", [128, 128], BF16)
        MOD = T("MOD", [128, 2, 3, 2, KC])
        GBC = [T("GBC0", [128, D]), T("GBC1", [128, D])]
        KGBC = T("KGBC", [128, 128])
        W1A1 = T("W1A1", [128, KC, 2, 64], BF16); W2A2 = T("W2A2", [64, 2, E], BF16)
        STG = T("STG", [128, 2048]); STB = T("STB", [128, 2048], BF16)
        XT = T("XT", [128, D]); XN = T("XN", [128, D])
        HT = T("HT", [128, KC, 129]); DXT = T("DXT", [128, KC, 128])
        MIX = [T(f"MIX{j}", [128, KC, 128], BF16) for j in range(6)]
        HWA = T("HWA", [64, 2, 128], BF16)
        WIN = [T(f"WIN{i}", [128, 4, KC, 128], BF16) for i in range(2)]
        TT = [T(f"T{i}", [128, 128]) for i in range(10)]
        KR = [T(f"KR{i}", [128, 2, 128], BF16) for i in range(2)]
        BTT = [T(f"BTT{i}", [128, 128], BF16) for i in range(2)]; KTT = [T(f"KTT{i}", [128, 128], BF16) for i in range(2)]
        T4B = T("T4B", [128, 128], BF16); T8B = T("T8B", [128, 128], BF16)
        VTOK = [T(f"VTOK{i}", [128, 128], BF16) for i in range(2)]; KTOK = [T(f"KTOK{i}", [128, 128], BF16) for i in range(2)]
        NBTOK = [T(f"NBTOK{i}", [128, 128], BF16) for i in range(2)]
        EGP = [T(f"EGP{i}", [128, 128]) for i in range(3)]; SZP = [T(f"SZP{i}", [128, 128]) for i in range(3)]
        BONV = [T(f"BONV{i}", [128, 128]) for i in range(3)]
        CY = T("CY", [128, 128]); CT2 = T("CT2", [128, 128]); CT5 = T("CT5", [128, 128]); PSS = T("PSS", [128, 64])
        MALL = T("MALL", [128, 2, 2, 2, 128], BF16)
        ARBT = T("ARBT", [128, 2, 128], BF16); AAKT = T("AAKT", [128, 2, 128], BF16); ARKT = T("ARKT", [128, 2, 128], BF16)
        XB = [T("XB0", [128, 128], BF16), T("XB1", [128, 128], BF16)]
        YG = T("YG", [128, 16, 128], BF16)
        WOUTQ = [T(f"WOUTQ{i}", [128, 16, 256], BF16) for i in range(2)]
        KVW = [T(f"KVW{i}", [128, KC, 512], BF16) for i in range(2)]
        XKT = T("XKT", [128, KC, 128], BF16); H1T = T("H1T", [128, KC, 128], BF16)
        SQ = T("SQ", [128, 512]); KOUT = [T(f"KOUT{i}", [128, 512]) for i in range(2)]
        VOUT = [T(f"VOUT{i}", [128, 512]) for i in range(2)]
        VBt = [T(f"VB{i}", [128, 512], BF16) for i in range(2)]
        KTST = [T(f"KTST{i}", [128, 4, 128], BF16) for i in range(2)]
        SS = T("SS", [128, 8]); SSQ = T("SSQ", [128, 2])
        SF = T("SF", [128, NPAIR, 64]); SBD = T("SBD", [128, NPAIR, 128], BF16)
        sf_b = [Buf(f"sf{i}") for i in range(NPAIR)]; sbd_b = [Buf(f"sbd{i}") for i in range(NPAIR)]
        SFp = lambda pr: V(SF.ap[:, pr, :], [sf_b[pr]])
        SBDp = lambda pr: V(SBD.ap[:, pr, :], [sbd_b[pr]])
        SF = V(SF.ap, sf_b); SBD = V(SBD.ap, sbd_b)
        KRM = [[T(f"KRM{p}{i}", [128, 128], BF16) for i in range(2)] for p in range(2)]
        BTM = [[T(f"BTM{p}{i}", [128, 128], BF16) for i in range(2)] for p in range(2)]
        KTM = [[T(f"KTM{p}{i}", [128, 128], BF16) for i in range(2)] for p in range(2)]
        BW = [T(f"BW{i}", [128, 2, KC, 128], BF16) for i in range(2)]
        QTP = [T(f"QTP{i}", [128, 128], BF16) for i in range(2)]
        SZQ = [T(f"SZQ{i}", [128, 128]) for i in range(2)]
        QE1 = T("QE1", [128, 128]); QE2 = T("QE2", [128, 128])
        NKMAX = SEQ // 128
        KTH = T("KTH", [128, SEQ], BF16); VH = T("VH", [128, NKMAX, 128], BF16)
        E1 = T("E1", [128, 4, 128]); SPB = [T("SPB0", [128, 4, 128], BF16), T("SPB1", [128, 4, 128], BF16)]
        ESt = T("ES", [128, 4, 128])
        AT = [T("AT0", [128, 4, 128], BF16), T("AT1", [128, 4, 128], BF16)]
        NONEB = T("NONEB", [128, 128], BF16)
        CB = T("CB", [128, 128])
        WST = T("WST", [64, 128])

        def pv(name, k=None, n=1):
            o = PVO[name] + (0 if k is None else k)
            return PVt[:, o:o + n]

        def cc(name, w=128):
            return CST[:, CO[name]:CO[name] + w]

        IDF = cc("ident")

        load(CST, cst); load(PVt, pvec); load(C3, c3T); load(HP, hprev)
        load(KGBC, rowv[2:3, 0:128].partition_broadcast(128))
        cp(DVE, IDB, IDF); cp(DVE, TRIB, cc("TriNeg")); cp(DVE, BO1B, cc("BO1")); cp(DVE, ONEB, cc("ones")); ts(DVE, NONEB, cc("ones"), -1.0, ALU.mult)
        cp(DVE, MSKB, CST[:, CO["MKcat"]:CO["MKcat"] + 128])
        ts(DVE, pv("nw0", 0, 16), pv("w0", 0, 16), -1.0, ALU.mult)
        ts(DVE, pv("na0", 0, 16), pv("a0", 0, 16), -1.0, ALU.mult)
        ts(DVE, pv("omka", 0, 16), pv("k_a", 0, 16), -1.0, ALU.mult, 1.0, ALU.add)
        memset(DVE, SF, 0.0); memset(POOL, SBD, 0.0)

        CBC = XN
        for layer, (adaw, normg, bname) in enumerate([(ada_a, "a_norm_g", "ada_b_a"), (ada_b, "b_norm_g", "ada_b_b")]):
            adav = adaw.rearrange("(k p) c -> p k c", p=128)
            stg3 = V(STG.ap.rearrange("p (k c) -> p k c", k=KC), STG.bufs)
            for cch in range(8):
                load(stg3, adav[:, :, cch * 256:(cch + 1) * 256])
                for bi in range(2):
                    blk = cch * 2 + bi
                    o = ps(7, 0)[:, blk * 3:(blk + 1) * 3]
                    for kc in range(KC):
                        mm(o, stg3[:, kc, bi * 128:(bi + 1) * 128], C3[:, kc, :], start=(kc == 0), stop=(kc == KC - 1))
            adp = V(ps(7, 0).ap[:, 0:48].rearrange("p (b s) -> p b s", s=3), ps(7, 0).bufs)
            for s in range(3):
                tt(DVE, MOD[:, layer, s, 1, :], adp[:, 0:8, s], pv(bname, 0, 8), ALU.add)
                tt(DVE, MOD[:, layer, s, 0, :], adp[:, 8:16, s], pv(bname, 8, 8), ALU.add)
                ts(DVE, MOD[:, layer, s, 0, :], MOD[:, layer, s, 0, :], 1.0, ALU.add)
                tt(DVE, MOD[:, layer, s, 0, :], MOD[:, layer, s, 0, :], pv(normg, 0, 8), ALU.mult)
            load(GBC[1], rowv[layer:layer + 1, :].partition_broadcast(128))
            cbc3 = V(CBC.ap.rearrange("p (k c) -> p k c", k=KC), CBC.bufs)
            for s in range(3):
                for kc in range(KC):
                    ts(DVE, cbc3[:, kc, :], cc("ones"), C3[:, kc, s:s + 1], ALU.mult)
                for cch in range(4):
                    load(stg3, adav[:, :, 2048 + cch * 256:2048 + (cch + 1) * 256])
                    o = ps(cch % 2, 0, 2)
                    for kc in range(KC):
                        mm(o, cbc3[:, kc, :], stg3[:, kc, :], start=(kc == 0), stop=(kc == KC - 1))
                    tt(DVE, GBC[0][:, cch * 256:(cch + 1) * 256], o, GBC[1][:, cch * 256:(cch + 1) * 256], ALU.add)
                store(GSC[layer, s], GBC[0], [bGSC])

        cvt_i = [0]

        def convert(src_ap, dst_ap, wb, ncol=2048, shape3=None):
            load(STG[:, 0:ncol], src_ap)
            eng = [DVE, ACT, POOL][cvt_i[0] % 3]
            cvt_i[0] += 1
            cp(eng, STB[:, 0:ncol], STG[:, 0:ncol])
            src = STB[:, 0:ncol]
            if shape3 is not None:
                src = V(src.ap.rearrange("p (a c) -> p a c", a=shape3), src.bufs)
            store(dst_ap, src, [wb])

        for kc in range(KC):
            rows = slice(kc * 128, (kc + 1) * 128)
            for j in range(4):
                convert(w_in[rows, j * E:(j + 1) * E], WINS.rearrange("a p j k c -> p a j k c")[:, :, j, kc, :], bWINS, shape3=16)
            for half in range(2):
                convert(kvw[rows, half * E:(half + 1) * E],
                        KVWS.rearrange("g p k c -> p g k c")[:, half * 4:(half + 1) * 4, kc, :], bKVWS, shape3=4)
            for qz in range(2):
                convert(bwin[rows, qz * E:(qz + 1) * E], BWS.rearrange("h p q k c -> p h q k c")[:, :, qz, kc, :], bBWS, shape3=16)
        for li, wo in enumerate([wout_a, wout_b]):
            for kc in range(16):
                convert(wo[kc * 128:(kc + 1) * 128, :], WOS[li].rearrange("h p k c -> p h k c")[:, :, kc, :], bWOS[li],
                        ncol=1024, shape3=2)
        for i, wsrc in enumerate([w1, a1]):
            load(V(STG.ap[:, 0:512].rearrange("p (k c) -> p k c", k=KC), STG.bufs), wsrc.rearrange("(k p) c -> p k c", p=128))
            cp(DVE, W1A1[:, :, i, :], V(STG.ap[:, 0:512].rearrange("p (k c) -> p k c", k=KC), STG.bufs))
        for i, wsrc in enumerate([w2, a2]):
            load(STG[0:64, :], wsrc)
            cp(DVE, W2A2[:, i, :], STG[0:64, :])

        if do_samples:
            for b in range(2):
                for blk in range(PAST // 128):
                    r0 = b * PAST + blk * 128
                    load(STG, cv[r0:r0 + 128, :])
                    cp([DVE, POOL][blk % 2], STB, STG)
                    store(VSS[b, blk * 128:(blk + 1) * 128, :], STB, [bVS[1 + b]])
                for h in range(NHEAD):
                    stg3 = V(STG.ap[:, 0:1024].rearrange("p (n c) -> p n c", n=8), STG.bufs)
                    load(stg3, ck[b * PAST:(b + 1) * PAST, h * 128:(h + 1) * 128].rearrange("(n p) c -> p n c", p=128))
                    for half in range(2):
                        o = ps(half, 0, 4)
                        for q in range(4):
                            tr(o[:, q * 128:(q + 1) * 128], stg3[:, half * 4 + q, :], IDF)
                        cp([ACT, DVE][half], STB[:, half * 512:(half + 1) * 512], o)
                    store(KTSS[b, h, :, 0:PAST], STB[:, 0:1024], [bKT[1 + b]])

        def gelem(i):
            return [ACT, DVE][i % 2]

        def token_tile(seq, t0, n, x_src, y_dst, k_dst, v_dst, KTS_s, VS_s, koff, first, last, wkv_dst, sh_dst, swkv_src):
            bKTs, bVSs = bKT[seq], bVS[seq]
            L = int(np.log2(n))
            load(XT[:n, :], x_src)
            if first:
                load(GBC[0], GSC[0, seq], [bGSC]); load(GBC[1], GSC[1, seq], [bGSC])
                cp(DVE, HT[:, :, 0], HP[:, :, seq])
                if swkv_src is None:
                    memset(DVE, SF, 0.0); memset(POOL, SBD, 0.0)
                else:
                    for g4 in range(4):
                        sg = V(STG.ap[0:64, :].rearrange("p (h k) -> p h k", h=32), STG.bufs)
                        if g4 == 0:
                            load(sg, swkv_src.rearrange("h v k -> v h k"))
                        o = ps(7, 0, 2)
                        for q in range(4):
                            pr = g4 * 4 + q
                            tr(o[:, q * 64:(q + 1) * 64], V(STG.ap[0:64, pr * 128:(pr + 1) * 128], STG.bufs), IDF[0:64, 0:64])
                        o3 = V(o.ap.rearrange("p (a v) -> p a v", v=64), o.bufs)
                        cp(DVE, SF[:, g4 * 4:(g4 + 1) * 4, :], o3)
                        if g4 == 0:
                            memset(POOL, SBD, 0.0)
                        cp(DVE, SBD[0:64, g4 * 4:(g4 + 1) * 4, 0:64], o3[0:64])
                        cp(DVE, SBD[64:128, g4 * 4:(g4 + 1) * 4, 64:128], o3[64:128])

            def rms_to_T(dsts):
                act(XN[:n, :], XT[:n, :], AF.Square)
                P.op(DVE, lambda e: e.reduce_sum(out=SSQ.ap[:n, 0:1], in_=XN.ap[:n, :], axis=AX.X), reads=XN.bufs, writes=SSQ.bufs)
                act(SSQ[:n, 1:2], SSQ[:n, 0:1], AF.Ln, bias=pv("eps_rms")[:n], scale=1.0 / D)
                act(SSQ[:n, 1:2], SSQ[:n, 1:2], AF.Exp, scale=-0.5)
                ts(DVE, XN[:n, :], XT[:n, :], SSQ[:n, 1:2], ALU.mult)
                for half in range(2):
                    bank = [7, 5][half]
                    for q in range(4):
                        kc = half * 4 + q
                        tr(ps(bank, q)[:, :n], XN[:n, kc * 128:(kc + 1) * 128], IDF[:n, :n])
                    for q in range(4):
                        kc = half * 4 + q
                        for (dst, sc, bi) in dsts:
                            if bi is None:
                                act(dst(kc), ps(bank, q)[:, :n], AF.Identity, scale=sc(kc))
                            else:
                                act(dst(kc), ps(bank, q)[:, :n], AF.Identity, bias=bi(kc), scale=sc(kc))

            rms_to_T([(lambda kc: HT[:, kc, 1:1 + n], lambda kc: MOD[:, 0, seq, 0, kc:kc + 1], lambda kc: MOD[:, 0, seq, 1, kc:kc + 1])])
            tt(DVE, DXT[:, :, :n], HT[:, :, 0:n], HT[:, :, 1:1 + n], ALU.subtract)
            for j, mun in enumerate(["mu_r", "mu_k", "mu_v", "mu_z", "mu_w", "mu_a"]):
                for kc in range(KC):
                    stt(DVE, MIX[j][:, kc, :n], DXT[:, kc, :n], pv(mun, kc), HT[:, kc, 1:1 + n], ALU.mult, ALU.add)
            if last and sh_dst is not None:
                store(sh_dst, HT[:, :, n], [bOUT])
            cp(DVE, HT[:, :, 0], HT[:, :, n])
            for i in range(2):
                o = ps(1, i)[0:64, :n]
                for kc in range(KC):
                    mm(o, W1A1[:, kc, i, :], MIX[4 + i][:, kc, :n], start=(kc == 0), stop=(kc == KC - 1))
            act(WST[:, :n], ps(1, 0)[0:64, :n], AF.Exp, scale=2.0)
            ts(DVE, WST[:, :n], WST[:, :n], 1.0, ALU.add)
            recip(WST[:, :n], WST[:, :n])
            ts(DVE, HWA[:, 0, :n], WST[:, :n], -2.0, ALU.mult, 1.0, ALU.add)
            cp(ACT, HWA[:, 1, :n], ps(1, 1)[0:64, :n])

            def wout_load(li, q):
                load(WOUTQ[q % 2], WOS[li][q // 2][:, :, (q % 2) * 256:(q % 2 + 1) * 256], [bWOS[li]])

            def genA(pr, par):
                Wt = WIN[pr % 2]
                if pr == 0:
                    load(Wt, WINS[pr], [bWINS])
                if pr + 1 < NPAIR:
                    load(WIN[(pr + 1) % 2], WINS[pr + 1], [bWINS])
                pc = slice(pr * 128, (pr + 1) * 128)
                for j in range(4):
                    for kc in range(KC):
                        mm(ps(0, j)[:, :n], Wt[:, j, kc, :], MIX[j][:, kc, :n], start=(kc == 0), stop=(kc == KC - 1))
                    yield
                for i in range(2):
                    mm(ps(1, i)[:, :n], W2A2[:, i, pc], HWA[:, i, :n])
                PRr, PRk, PRv, PRz = (ps(0, j)[:, :n] for j in range(4))
                T1, T2, T3, T5, T6, T7, T9, G, EGI, EGM = (t[:, :n] for t in TT[:10])
                EG = EGP[pr % 3][:, :n]
                TZ = SZP[pr % 3][:, :n]
                KRp, BTTp, KTTp = KR[par], BTT[par], KTT[par]
                pe = lambda nm: pv(nm, pr)
                act(T1, ps(1, 0)[:, :n], AF.Exp, bias=pe("nw0"), scale=-1.0)
                act(T2, ps(1, 1)[:, :n], AF.Exp, bias=pe("na0"), scale=-1.0)
                yield
                ts(DVE, T1, T1, 1.0, ALU.add); recip(T1, T1)
                ts(POOL, T2, T2, 1.0, ALU.add); recip(T2, T2)
                yield
                P.op(DVE, lambda e, G=G, T1=T1: e.tensor_tensor_scan(out=G.ap, data0=cc("ones")[:, :n].ap, data1=T1.ap, initial=0.0,
                                                                      op0=ALU.mult, op1=ALU.add), reads=rb(T1, CST), writes=G.bufs)
                act(EG, G, AF.Exp, scale=-DECAY_C)
                act(EGI, G, AF.Exp, scale=DECAY_C)
                tt(POOL, EGM, G, T1, ALU.subtract)
                act(EGM, EGM, AF.Exp, scale=-DECAY_C)
                yield
                ts(DVE, T3, PRk, pe("k_k"), ALU.mult)
                act(T4B[:, :n], T3, AF.Square)
                mm(ps(1, 2)[:, :n], BO1B, T4B[:, :n])
                yield
                act(T5, ps(1, 2)[:, :n], AF.Ln, bias=pv("eps_l2"))
                act(T5, T5, AF.Exp, scale=-0.5)
                tt(DVE, T3, T3, T5, ALU.mult)
                yield
                ts(POOL, T6, T2, pe("k_a"), ALU.mult, pe("omka"), ALU.add)
                tt(DVE, T6, PRk, T6, ALU.mult)
                tt(POOL, T7, T3, T2, ALU.mult)
                yield
                tt(DVE, KRp[:, 0, :n], T3, EGM, ALU.mult)
                tt(DVE, KRp[:, 1, :n], PRr, EG, ALU.mult)
                tt(POOL, BTTp[:, :n], T7, EGI, ALU.mult)
                tt(POOL, KTTp[:, :n], T6, EGI, ALU.mult)
                yield
                for hh in range(2):
                    hm = pv("hm%d" % hh)
                    ts(POOL, KRM[par][hh][:, :n], KRp[:, 0, :n], hm, ALU.mult)
                    ts(POOL, BTM[par][hh][:, :n], BTTp[:, :n], hm, ALU.mult)
                    ts(POOL, KTM[par][hh][:, :n], KTTp[:, :n], hm, ALU.mult)
                    yield
                stt(DVE, T8B[:, :n], PRr, pe("r_k"), T6, ALU.mult, ALU.mult)
                mm(ps(1, 3)[:, :n], BO1B, T8B[:, :n])
                cp(ACT, T9, PRv)
                yield
                tr(ps(6, 2)[:n, :], T9, IDF)
                tr(ps(6, 0)[:n, 0:128], KTTp[:, :n], IDB)
                tr(ps(6, 1)[:n, 0:128], BTTp[:, :n], IDB)
                tt(DVE, BONV[pr % 3][:, :n], ps(1, 3)[:, :n], T9, ALU.mult)
                yield
                cp(ACT, VTOK[par][:n, :], ps(6, 2)[:n, :])
                cp(DVE, KTOK[par][:n, :], ps(6, 0)[:n, 0:128])
                act(NBTOK[par][:n, :], ps(6, 1)[:n, 0:128], AF.Identity, scale=-1.0)
                yield
                act(TZ, PRz, AF.Exp, scale=-1.0)
                ts(DVE, TZ, TZ, 1.0, ALU.add); recip(TZ, TZ)
                tt(DVE, TZ, PRz, TZ, ALU.mult)
                yield

            def genB(pr, par):
                KRp, BTTp, KTTp = KR[par], BTT[par], KTT[par]
                VT, KT, NBT = VTOK[par], KTOK[par], NBTOK[par]
                pe = lambda nm: pv(nm, pr)
                psN = V(ps(4, 0, 2).ap.rearrange("p (h c) -> p h c", h=2), ps(4, 0, 2).bufs)
                psB = V(PB[2][:, :].rearrange("p (h s c) -> p h s c", h=2, s=2), PBb[2][:])
                psK = V(PB[3][:, :].rearrange("p (h s c) -> p h s c", h=2, s=2), PBb[3][:])
                for hh in range(2):
                    mm(psN[:n, hh, :n], KRM[par][hh][:, :n], BTTp[:, :n])
                    for s2 in range(2):
                        mm(psB[:n, hh, s2, :n], BTM[par][hh][:, :n], KRp[:, s2, :n])
                        mm(psK[:n, hh, s2, :n], KTM[par][hh][:, :n], KRp[:, s2, :n])
                bc = lambda name, off: V(CST.ap[:n, CO[name] + off:CO[name] + off + n].unsqueeze(1).to_broadcast([n, 2, n]), CST.bufs)
                tt(DVE, MALL[:n, 0, 0, :, :n], psN[:n, :, :n], bc("ML", 0), ALU.mult)
                tt(DVE, MALL[:n, 0, 1, :, :n], psB[:n, :, 0, :n], bc("MUcat", 0), ALU.mult)
                yield
                tt(DVE, ARBT[:n, :, :n], psB[:n, :, 1, :n], bc("MUcat", 128), ALU.mult)
                tt(DVE, AAKT[:n, :, :n], psK[:n, :, 0, :n], bc("MKcat", 0), ALU.mult)
                tt(DVE, ARKT[:n, :, :n], psK[:n, :, 1, :n], bc("MKcat", 128), ALU.mult)
                psX = ps(4, 2)
                for hh in range(2):
                    hs = slice(hh * 64, hh * 64 + 64)
                    mm(psX[:n, hs], KRp[:, 0, :n], SBDp(pr)[:, hs], start=(hh == 0), stop=False, nochk=True)
                    mm(psX[:n, hs], AAKT[:n, hh, :n], VT[:n, hs], start=False, stop=False, nochk=True)
                cp(ACT, XB[0][:n, :], psX[:n, :])
                yield
                psM = V(PB[5][:, :].rearrange("p (s h c) -> p s h c", s=2, h=2), PBb[5][:])
                for k in range(L):
                    for hh in range(2):
                        hs = slice(hh * 64, hh * 64 + 64)
                        mm(psX[:n, hs], MALL[:n, k % 2, 1, hh, :n], XB[k % 2][:n, hs], start=False, stop=(k == L - 1 and hh == 1), nochk=True)
                    if k < L - 1:
                        lastk = (k == L - 2)
                        for hh in range(2):
                            if not lastk:
                                mm(psM[:n, 0, hh, :n], MALL[:n, k % 2, 1, hh, :n], MALL[:n, k % 2, 0, hh, :n])
                            mm(psM[:n, 1, hh, :n], MALL[:n, k % 2, 0, hh, :n], MALL[:n, k % 2, 1, hh, :n])
                        if not lastk:
                            cp(ACT, MALL[:n, (k + 1) % 2, :, :, :n], psM[:n, :, :, :n])
                        else:
                            cp(ACT, MALL[:n, (k + 1) % 2, 1, :, :n], psM[:n, 1, :, :n])
                    cp(DVE, XB[(k + 1) % 2][:n, :], psX[:n, :])
                    yield
                U = XB[L % 2]
                psY = ps(4, 3)
                mm(psY[:, :n], SBDp(pr), KRp[:, 1, :n], start=True, stop=False, nochk=True)
                for hh in range(2):
                    hs = slice(hh * 64, hh * 64 + 64)
                    mm(psY[hs, :n], VT[:n, hs], ARKT[:n, hh, :n], start=False, stop=False, nochk=True)
                    mm(psY[hs, :n], U[:n, hs], ARBT[:n, hh, :n], start=False, stop=True, nochk=True)
                psS = ps(7, 1)
                for hh in range(2):
                    hs = slice(hh * 64, hh * 64 + 64)
                    mm(psS[hs, 0:64], KT[:n, hs], VT[:n, hs], start=True, stop=False)
                    mm(psS[hs, 0:64], NBT[:n, hs], U[:n, hs], start=False, stop=True)
                cp(ACT, CY[:, :n], psY[:, :n])
                cp(DVE, PSS[:, :], psS[:, 0:64])
                yield

            def genB2(pr, par):
                pe = lambda nm: pv(nm, pr)
                Y = CY[:, :n]
                C2 = CT2[:, :n]
                C5 = CT5[:, :n]
                sfp = SFp(pr)
                sbp = SBDp(pr)
                tt(DVE, sfp, PSS[:, :], sfp, ALU.add)
                ts(DVE, sfp, sfp, EGP[pr % 3][:, n - 1:n], ALU.mult)
                cp(ACT, sbp[0:64, 0:64], sfp[0:64, :])
                cp(ACT, sbp[64:128, 64:128], sfp[64:128, :])
                mm(ps(7, 2)[:, :n], cc("BO64"), Y)
                yield
                tt(DVE, Y, Y, ps(7, 2)[:, :n], ALU.subtract)
                act(C2, Y, AF.Square)
                mm(ps(7, 3)[:, :n], cc("BO64"), C2)
                yield
                act(C5, ps(7, 3)[:, :n], AF.Ln, bias=pv("eps_gn"))
                act(C5, C5, AF.Exp, scale=-0.5)
                tt(DVE, Y, Y, C5, ALU.mult)
                yield
                ts(DVE, Y, Y, pe("ln_g"), ALU.mult, pe("ln_b"), ALU.add)
                tt(POOL, Y, Y, BONV[pr % 3][:, :n], ALU.add)
                tt(DVE, YG[:, pr, :n], Y, SZP[pr % 3][:, :n], ALU.mult)
                yield

            def drain(g):
                for _ in g:
                    pass

            def interleave(*gens):
                gens = [g for g in gens if g is not None]
                while gens:
                    for g in list(gens):
                        try:
                            next(g)
                        except StopIteration:
                            gens.remove(g)

            wout_load(0, 0)
            npairs = 0 if 'pair' in os.environ.get('KSKIP', '') else NPAIR
            if npairs:
                drain(genA(0, 0))
            for pr in range(npairs):
                interleave(genB(pr, pr % 2),
                           genA(pr + 1, (pr + 1) % 2) if pr + 1 < NPAIR else None,
                           genB2(pr - 1, (pr - 1) % 2) if pr >= 1 else None)
            if npairs:
                drain(genB2(npairs - 1, (npairs - 1) % 2))

            if last and wkv_dst is not None:
                for g4 in range(4):
                    o = ps(7, 0, 4)
                    for q in range(4):
                        pr = g4 * 4 + q
                        tr(o[0:64, q * 128:(q + 1) * 128], SF[:, pr, :], IDF)
                    cp(ACT, STG[0:64, g4 * 512:(g4 + 1) * 512], o[0:64, :])
                store(wkv_dst.rearrange("h v k -> v h k"), V(STG.ap[0:64, :].rearrange("p (h k) -> p h k", h=32), STG.bufs), [bOUT])

            def out_proj(li, src):
                for q in range(4):
                    if q + 1 < 4:
                        wout_load(li, q + 1)
                    Wq = WOUTQ[q % 2]
                    o = ps(q % 2, 0, 2)
                    for pr in range(16):
                        mm(o[:n, :], src[:, pr, :n], Wq[:, pr, :], start=(pr == 0), stop=(pr == 15))
                    hc = slice(q * 256, (q + 1) * 256)
                    tt(DVE, XN[:n, hc], o[:n, :], GBC[li][:n, hc], ALU.mult)
                    tt(POOL, XT[:n, hc], XT[:n, hc], XN[:n, hc], ALU.add)

            out_proj(0, YG)

            rms_to_T([(lambda kc: XKT[:, kc, :n], lambda kc: pv("kv_norm_g", kc), None),
                      (lambda kc: H1T[:, kc, :n], lambda kc: MOD[:, 1, seq, 0, kc:kc + 1], lambda kc: MOD[:, 1, seq, 1, kc:kc + 1])])
            load(KVW[0], KVWS[0], [bKVWS])
            for cg in range(8):
                Wk = KVW[cg % 2]
                if cg + 1 < 8:
                    load(KVW[(cg + 1) % 2], KVWS[cg + 1], [bKVWS])
                o = ps(2 + cg % 2, 0, 4)
                for kc in range(KC):
                    mm(o[:n, :], XKT[:, kc, :n], Wk[:, kc, :], start=(kc == 0), stop=(kc == KC - 1))
                cs = slice((cg % 4) * 512, (cg % 4 + 1) * 512)
                if cg < 4:
                    ko = KOUT[cg % 2]
                    act(SQ[:n, :], o[:n, :], AF.Square)
                    P.op(DVE, lambda e, cg=cg: e.reduce_sum(out=SS.ap[:n, 0:4], in_=SQ.ap[:n, :].rearrange("p (h c) -> p h c", h=4), axis=AX.X),
                         reads=SQ.bufs, writes=SS.bufs)
                    act(SS[:n, 4:8], SS[:n, 0:4], AF.Ln, bias=pv("eps_rms")[:n], scale=1.0 / 128)
                    act(SS[:n, 4:8], SS[:n, 4:8], AF.Exp, scale=-0.5)
                    o3 = V(o.ap[:n, :].rearrange("p (h c) -> p h c", h=4), o.bufs)
                    ko3 = V(ko.ap[:n, :].rearrange("p (h c) -> p h c", h=4), ko.bufs)
                    tt(DVE, ko3, o3, V(SS.ap[:n, 4:8].unsqueeze(2).to_broadcast([n, 4, 128]), SS.bufs), ALU.mult)
                    tt(POOL, ko3, ko3, V(KGBC.ap[:n, :].unsqueeze(1).to_broadcast([n, 4, 128]), KGBC.bufs), ALU.mult)
                    store(k_dst[:, cs], ko[:n, :], [bOUT])
                    kst = KTST[cg % 2]
                    ob = ps(7, 0, 4)
                    for q in range(4):
                        tr(ob[:, q * 128:q * 128 + n], ko[:n, q * 128:(q + 1) * 128], IDF[:n, :n])
                    cp(ACT, kst[:, :, :n], V(ob.ap.rearrange("p (h c) -> p h c", h=4)[:, :, :n], ob.bufs))
                    store(KTS_s[cg * 4:(cg + 1) * 4, :, koff + t0:koff + t0 + n].rearrange("h p c -> p h c"), kst[:, :, :n], [bKTs])
                else:
                    vo = VOUT[cg % 2]
                    cp(ACT, vo[:n, :], o[:n, :])
                    cp(DVE, VBt[cg % 2][:n, :], o[:n, :])
                    store(v_dst[:, cs], vo[:n, :], [bOUT])
                    store(VS_s[koff + t0:koff + t0 + n, cs], VBt[cg % 2][:n, :], [bVSs])

            nk_total = koff + t0 + n
            nkb = (nk_total + 127) // 128
            OG = YG
            wout_load(1, 0)
            def genQ(h, par):
                Bw = BW[h % 2]
                if h == 0:
                    load(Bw, BWS[h], [bBWS])
                if h + 1 < NHEAD:
                    load(BW[(h + 1) % 2], BWS[h + 1], [bBWS])
                psQ, psZg, psSS = ps(par, 0), ps(par, 1), ps(par, 2)
                for kc in range(KC):
                    mm(psQ[:, :n], Bw[:, 0, kc, :], H1T[:, kc, :n], start=(kc == 0), stop=(kc == KC - 1))
                yield
                for kc in range(KC):
                    mm(psZg[:, :n], Bw[:, 1, kc, :], H1T[:, kc, :n], start=(kc == 0), stop=(kc == KC - 1))
                q1, q2 = QE1[:, :n], QE2[:, :n]
                act(q1, psQ[:, :n], AF.Square)
                yield
                mm(psSS[:, :n], cc("ones"), q1)
                act(q2, psSS[:, :n], AF.Ln, bias=pv("eps_rms"), scale=1.0 / 128)
                yield
                act(q2, q2, AF.Exp, scale=-0.5)
                ts(DVE, q2, q2, pv("q_gain"), ALU.mult, 128 ** -0.5, ALU.mult)
                yield
                tt(DVE, QTP[par][:, :n], psQ[:, :n], q2, ALU.mult)
                sz = SZQ[par][:, :n]
                act(sz, psZg[:, :n], AF.Exp, scale=-1.0)
                yield
                ts(DVE, sz, sz, 1.0, ALU.add); recip(sz, sz)
                tt(DVE, sz, psZg[:, :n], sz, ALU.mult)
                yield

            def genAtt(h, par):
                QT = QTP[par]
                psO = ps(4, 0)
                load(KTH[:, 0:nk_total], KTS_s[h, :, 0:nk_total], [bKTs])
                nfull = nk_total // 128
                if nfull > 0:
                    load(VH[:, 0:nfull, :], VS_s[0:nfull * 128, h * 128:(h + 1) * 128].rearrange("(b p) c -> p b c", p=128), [bVSs])
                rem = nk_total - nfull * 128
                if rem > 0:
                    load(VH[0:rem, nfull, :], VS_s[nfull * 128:nk_total, h * 128:(h + 1) * 128], [bVSs])
                memset(POOL, CB[:, :n], 0.0)
                blocks = list(range(nkb - 1, -1, -1))
                groups = [[blocks[0]]] + [blocks[i:i + 4] for i in range(1, len(blocks), 4)]
                ng = len(groups)
                PZB = [5, 6, 7]

                def pzv(g):
                    return V(PB[PZB[g % 3]][:, :].rearrange("p (a c) -> p a c", a=4), PBb[PZB[g % 3]][:])

                def S1(g):
                    pz = pzv(g)
                    for i, kb in enumerate(groups[g]):
                        ks = min(128, nk_total - kb * 128)
                        mm(pz[:ks, i, :n], KTH[:, kb * 128:kb * 128 + ks], QT[:, :n], start=(i == 0), stop=False, nochk=True)

                def S2(g):
                    pz = pzv(g)
                    G = len(groups[g])
                    kb0 = groups[g][0]
                    ks = min(128, nk_total - kb0 * 128)
                    sp = SPB[g % 2]
                    act(E1[:ks, 0:G, :n], pz[:ks, 0:G, :n], AF.Exp)
                    act(sp[:ks, 0:G, :n], E1[:ks, 0:G, :n], AF.Ln, bias=pv("one")[:ks])
                    if g == 0:
                        tt(POOL, sp[:ks, 0, :n], sp[:ks, 0, :n], MSKB[:ks, :n], ALU.mult)

                def S3(g):
                    pz = pzv(g)
                    G = len(groups[g])
                    kb0 = groups[g][0]
                    ks = min(128, nk_total - kb0 * 128)
                    sp = SPB[g % 2]
                    at = AT[g % 2]
                    for i in range(G):
                        mm(pz[:ks, i, :n], TRIB[:ks, :ks], sp[:ks, i, :n], start=False, stop=False, nochk=True)
                        for i2 in range(i):
                            mm(pz[:ks, i, :n], NONEB[:ks, :ks], sp[:ks, i2, :n], start=False, stop=False, nochk=True)
                    lastg = (g == ng - 1)
                    if not lastg:
                        pcb = ps(2 + g % 2, 0)
                        for i in range(G):
                            mm(pcb[:, :n], ONEB[:ks, :], sp[:ks, i, :n], start=(i == 0), stop=(i == G - 1))
                    cbb = V(CB.ap[:ks, :n].unsqueeze(1).to_broadcast([ks, G, n]), CB.bufs)
                    tt(DVE, ESt[:ks, 0:G, :n], pz[:ks, 0:G, :n], cbb, ALU.subtract)
                    act(at[:ks, 0:G, :n], ESt[:ks, 0:G, :n], AF.Exp)
                    if g == 0:
                        tt(POOL, at[:ks, 0, :n], at[:ks, 0, :n], MSKB[:ks, :n], ALU.mult)
                    if not lastg:
                        tt(DVE, CB[:, :n], CB[:, :n], pcb[:, :n], ALU.add)
                    for i, kb in enumerate(groups[g]):
                        mm(psO[:, :n], VH[:ks, kb, :], at[:ks, i, :n], start=(g == 0 and i == 0), stop=(lastg and i == G - 1), nochk=True)

                S1(0)
                if ng > 1:
                    S1(1)
                S2(0)
                yield
                for g in range(ng):
                    if g + 2 < ng:
                        S1(g + 2)
                    if g + 1 < ng:
                        S2(g + 1)
                    S3(g)
                    yield
                tt(DVE, OG[:, h, :n], psO[:, :n], SZQ[par][:, :n], ALU.mult)
                yield

            nheads = 0 if 'att' in os.environ.get('KSKIP', '') else NHEAD
            if nheads:
                drain(genQ(0, 0))
            for h in range(nheads):
                interleave(genAtt(h, h % 2), genQ(h + 1, (h + 1) % 2) if h + 1 < NHEAD else None)
            out_proj(1, OG)
            store(y_dst, XT[:n, :], [bOUT])

        if do_samples:
            for b in range(2):
                r = slice(b * DSEQ, (b + 1) * DSEQ)
                token_tile(1 + b, 0, DSEQ, xs[r, :], ys[r, :], kso[r, :], vso[r, :], KTSS[b], VSS[b], PAST, True, True,
                           wkvs[b], shs[b], swkv[b])
        for ti in range(n_prompt_tiles):
            r = slice(ti * 128, (ti + 1) * 128)
            token_tile(0, ti * 128, 128, xp[r, :], yp[r, :], kp[r, :], vp[r, :], KTSP, VSP, 0, ti == 0, ti == n_prompt_tiles - 1,
                       wkvp, shp, None)

        P.final_all()
        nse = [max(1, (len(P.needed[i]) + EPOCH - 1) // EPOCH) for i in range(5)]
        esems = [[es.enter_context(nc.semaphore(f"s{ENG_NAMES[i]}{k}")) for k in range(nse[i])] for i in range(5)]
        dsems = [es.enter_context(nc.semaphore(f"d{k}")) for k in range(NDMA)]
        block = es.enter_context(nc.Block())
        P.emit(block, esems, dsems)
    return nc, P


_CACHE = {}


def make_in_maps(inp, n_cores=8):
    f = lambda a: np.ascontiguousarray(np.asarray(a, np.float32))
    cst = make_consts()
    pvec = make_pvec(inp)
    rowv = np.zeros((3, D), np.float32)
    rowv[0] = np.asarray(inp["a_ada_b"][0][2 * D:3 * D], np.float32)
    rowv[1] = np.asarray(inp["b_ada_b"][0][2 * D:3 * D], np.float32)
    rowv[2, :128] = np.asarray(inp["k_gain"], np.float32)
    shared = {
        "pvec": pvec, "cst": cst, "rowv": rowv,
        "ada_a": f(inp["a_ada_w"][0]), "ada_b": f(inp["b_ada_w"][0]),
        "w_in": f(inp["a_w_in"][0]), "w1": f(inp["a_w1"][0]), "a1": f(inp["a_a1"][0]),
        "w2": f(inp["a_w2"][0]), "a2": f(inp["a_a2"][0]), "wout_a": f(inp["a_w_out"][0]),
        "kvw": f(inp["kv_w"]), "bwin": f(inp["b_w_in"][0]), "wout_b": f(inp["b_w_out"][0]),
    }
    maps = []
    for i in range(n_cores):
        c3 = np.stack([np.asarray(inp["c_prompt"][i], np.float32),
                       np.asarray(inp["c_sample"][2 * i], np.float32),
                       np.asarray(inp["c_sample"][2 * i + 1], np.float32)], axis=0)
        c3T = np.ascontiguousarray(c3.reshape(3, KC, 128).transpose(2, 1, 0))
        hp = np.zeros((3, D), np.float32)
        hp[1] = inp["state_shift"][0, 2 * i]
        hp[2] = inp["state_shift"][0, 2 * i + 1]
        hpT = np.ascontiguousarray(hp.reshape(3, KC, 128).transpose(2, 1, 0))
        m = dict(shared)
        m.update({
            "xp": f(inp["x_prompt"][i]),
            "xs": f(np.asarray(inp["x_sample"][2 * i:2 * i + 2]).reshape(2 * DSEQ, D)),
            "ck": f(np.asarray(inp["cache_k"][2 * i:2 * i + 2]).reshape(2 * PAST, E)),
            "cv": f(np.asarray(inp["cache_v"][2 * i:2 * i + 2]).reshape(2 * PAST, E)),
            "swkv": f(inp["state_wkv"][0, 2 * i:2 * i + 2]),
            "hprev": hpT, "c3T": c3T,
        })
        maps.append(m)
    return maps


def kernel(**inputs):
    n = 8
    if "nc" not in _CACHE:
        _CACHE["nc"] = build_program()[0]
    nc = _CACHE["nc"]
    maps = make_in_maps(inputs, n)
    res = run_bass_kernel_spmd(nc, maps, core_ids=list(range(n)))
    R = res.results
    B, BD = 8, 16
    y_prompt = np.stack([R[i]["yp"] for i in range(n)], 0).astype(np.float32)
    y_sample = np.concatenate([R[i]["ys"].reshape(2, DSEQ, D) for i in range(n)], 0).astype(np.float32)
    k_prompt = np.stack([R[i]["kp"].reshape(SEQ, 16, 128) for i in range(n)], 0).astype(np.float32)
    v_prompt = np.stack([R[i]["vp"].reshape(SEQ, 16, 128) for i in range(n)], 0).astype(np.float32)
    wkv_prompt = np.stack([R[i]["wkvp"] for i in range(n)], 0)[None].astype(np.float32)
    shift_prompt = np.stack([R[i]["shp"].T.reshape(D) for i in range(n)], 0)[None].astype(np.float32)
    k_sample = np.concatenate([R[i]["kso"].reshape(2, DSEQ, 16, 128) for i in range(n)], 0).astype(np.float32)
    v_sample = np.concatenate([R[i]["vso"].reshape(2, DSEQ, 16, 128) for i in range(n)], 0).astype(np.float32)
    wkv_sample = np.concatenate([R[i]["wkvs"] for i in range(n)], 0)[None].astype(np.float32)
    shift_sample = np.concatenate([np.stack([R[i]["shs"][b].T.reshape(D) for b in range(2)], 0) for i in range(n)], 0)[None].astype(np.float32)
    return (y_prompt, y_sample, k_prompt, v_prompt, wkv_prompt, shift_prompt,
            k_sample, v_sample, wkv_sample, shift_sample)
```
